# Optimizing a Trainium2 kernel written in Bass

```python
import math
import jax, jax.numpy as jnp
from jax import lax
import numpy as np

D_MODEL = 2048
BATCH = 4
SEQ = 2048
DEPTH = 2
DEC_BATCH = 128
DEC_SEQ = 8
PAST_LEN = 16384
PAGE_SIZE = 128

N_MIXERS = 2
N_MLSTM_LAYERS = (DEPTH + 1) // N_MIXERS
N_RWKV_LAYERS = DEPTH // N_MIXERS
PLE_DIM = 256
D_FF = -(-8 * D_MODEL // (3 * 256)) * 256

MLSTM_HEADS = 8
MLSTM_DQK = D_MODEL // 2 // MLSTM_HEADS
MLSTM_DV = D_MODEL // MLSTM_HEADS
MLSTM_CHUNK = 64
GATE_SOFTCAP = 15.0

RWKV_HEAD_DIM = 64
RWKV_HEADS = D_MODEL // RWKV_HEAD_DIM
DECAY_LORA = 96
AAA_LORA = 96
GATE_LORA = 256

NORM_EPS = 1e-6
RWKV_GN_EPS = 64e-5
L2_EPS = 1e-12

kernel_name = 'mlstm_rwkv7_hybrid_step'


def rms_norm(x, g):
    xf = x.astype(jnp.float32)
    y = xf * lax.rsqrt(jnp.mean(xf * xf, axis=-1, keepdims=True) + NORM_EPS)
    return (y * g.astype(jnp.float32)).astype(x.dtype)


def softcap(x, cap):
    return cap * jnp.tanh(x / cap)


def swiglu(x, w_gate, w_up, w_down):
    return (jax.nn.silu(x @ w_gate) * (x @ w_up)) @ w_down


def mlstm_chunk_step(carry, inp):
    C, n, m = carry
    q, k, v, li, lf = inp
    L = q.shape[2]
    b = jnp.cumsum(lf, axis=-1)
    causal = jnp.tril(jnp.ones((L, L), dtype=bool))
    dlog = jnp.where(causal, b[..., :, None] - b[..., None, :] + li[..., None, :], -jnp.inf)
    a = b + m[..., None]
    m_t = jnp.maximum(a, jnp.max(dlog, axis=-1))
    w_intra = jnp.exp(dlog - m_t[..., None])
    w_inter = jnp.exp(a - m_t)
    s = jnp.einsum('bhtd,bhsd->bhts', q, k) * w_intra
    num = w_inter[..., None] * jnp.einsum('bhtd,bhde->bhte', q, C) + jnp.einsum('bhts,bhse->bhte', s, v)
    den = w_inter * jnp.einsum('bhtd,bhd->bht', q, n) + jnp.sum(s, axis=-1)
    h = num / jnp.maximum(jnp.abs(den), jnp.exp(-m_t))[..., None]
    m_new = m_t[..., -1]
    decay_state = jnp.exp(b[..., -1] + m - m_new)
    w_k = jnp.exp(b[..., -1:] - b + li - m_new[..., None])
    C_new = decay_state[..., None, None] * C + jnp.einsum('bhs,bhsd,bhse->bhde', w_k, k, v)
    n_new = decay_state[..., None] * n + jnp.einsum('bhs,bhsd->bhd', w_k, k)
    return (C_new, n_new, m_new), h


def mlstm_mixer(xn, w_q, w_k, w_v, w_ig, b_ig, w_fg, b_fg, w_og, norm_w, w_out, C0, n0, m0):
    B, T, _ = xn.shape
    H, DK, DV = MLSTM_HEADS, MLSTM_DQK, MLSTM_DV
    f32 = jnp.float32
    q = (xn @ w_q).astype(f32).reshape(B, T, H, DK).transpose(0, 2, 1, 3)
    k = (xn @ w_k).astype(f32).reshape(B, T, H, DK).transpose(0, 2, 1, 3) * (DK ** -0.5)
    v = (xn @ w_v).astype(f32).reshape(B, T, H, DV).transpose(0, 2, 1, 3)
    li = softcap((xn @ w_ig + b_ig).astype(f32), GATE_SOFTCAP).transpose(0, 2, 1)
    lf = jax.nn.log_sigmoid(softcap((xn @ w_fg + b_fg).astype(f32), GATE_SOFTCAP)).transpose(0, 2, 1)
    L = math.gcd(T, MLSTM_CHUNK)
    NC = T // L

    def blocks(t):
        return jnp.moveaxis(t.reshape(B, H, NC, L, *t.shape[3:]), 2, 0)

    carry0 = (C0.astype(f32), n0.astype(f32), m0.astype(f32))
    (C, n, m), h = lax.scan(mlstm_chunk_step, carry0, (blocks(q), blocks(k), blocks(v), blocks(li), blocks(lf)))
    h = jnp.moveaxis(h, 0, 2).reshape(B, H, T, DV).transpose(0, 2, 1, 3)
    h = h * lax.rsqrt(jnp.mean(h * h, axis=-1, keepdims=True) + NORM_EPS)
    h = h.reshape(B, T, H * DV) * norm_w.astype(f32)
    o = jax.nn.sigmoid((xn @ w_og).astype(f32))
    out = (o * h).astype(xn.dtype) @ w_out
    return out, C, n, m


def rwkv7_step(S, inp):
    r, w, k, v, kk, a = inp
    sa = jnp.einsum('bhvk,bhk->bhv', S, -kk)
    S = S * w[:, :, None, :] + sa[..., None] * (kk * a)[:, :, None, :] + v[..., None] * k[:, :, None, :]
    return S, jnp.einsum('bhvk,bhk->bhv', S, r)


def rwkv7_mixer(xn, mu, w_r, w_k, w_v, w_o, w0, w1, w2, a0, a1, a2, g1, g2, k_k, k_a, r_k, ln_w, ln_b, S0, shift0):
    B, T, D = xn.shape
    H, N = RWKV_HEADS, RWKV_HEAD_DIM
    f32 = jnp.float32
    x_prev = jnp.concatenate([shift0[:, None, :].astype(xn.dtype), xn[:, :-1]], axis=1)
    xx = x_prev - xn
    xr, xw, xk, xv, xa, xg = [xn + xx * mu[c] for c in range(6)]
    r = (xr @ w_r).astype(f32)
    k = (xk @ w_k).astype(f32)
    v = (xv @ w_v).astype(f32)
    w_log = -jax.nn.softplus(-(w0 + jnp.tanh(xw @ w1) @ w2).astype(f32)) - 0.5
    decay = jnp.exp(-jnp.exp(w_log))
    a = jax.nn.sigmoid((a0 + (xa @ a1) @ a2).astype(f32))
    g = (jax.nn.sigmoid(xg @ g1) @ g2).astype(f32)
    kk = (k * k_k).reshape(B, T, H, N)
    kk = kk / jnp.maximum(jnp.linalg.norm(kk, axis=-1, keepdims=True), L2_EPS)
    k = k * (1.0 + (a - 1.0) * k_a)

    def heads(t):
        return jnp.moveaxis(t.reshape(B, T, H, N), 1, 0)

    S, y = lax.scan(rwkv7_step, S0.astype(f32),
                    (heads(r), heads(decay), heads(k), heads(v), jnp.moveaxis(kk, 1, 0), heads(a)))
    y = jnp.moveaxis(y, 0, 1)
    mean = jnp.mean(y, axis=-1, keepdims=True)
    var = jnp.mean(jnp.square(y - mean), axis=-1, keepdims=True)
    y = ((y - mean) * lax.rsqrt(var + RWKV_GN_EPS)).reshape(B, T, D) * ln_w + ln_b
    bonus = jnp.sum(r.reshape(B, T, H, N) * k.reshape(B, T, H, N) * r_k, axis=-1, keepdims=True) * v.reshape(B, T, H, N)
    y = y + bonus.reshape(B, T, D)
    out = (y * g).astype(xn.dtype) @ w_o
    return out, S, xn[:, -1]


def trunk(x, p, mC, mn, mm, rS, rshift, w):
    h = x
    new_C, new_n, new_m, new_S, new_shift = [], [], [], [], []
    for i in range(DEPTH):
        j = i // N_MIXERS
        hn = rms_norm(h, w['norm_mix'][i])
        if i % N_MIXERS == 0:
            out, C, n, m = mlstm_mixer(hn, w['mlstm_w_q'][j], w['mlstm_w_k'][j], w['mlstm_w_v'][j],
                                       w['mlstm_w_igate'][j], w['mlstm_b_igate'][j], w['mlstm_w_fgate'][j],
                                       w['mlstm_b_fgate'][j], w['mlstm_w_ogate'][j], w['mlstm_norm_w'][j],
                                       w['mlstm_w_out'][j], mC[j], mn[j], mm[j])
            new_C.append(C)
            new_n.append(n)
            new_m.append(m)
        else:
            out, S, sh = rwkv7_mixer(hn, w['rwkv_mu'][j], w['rwkv_w_r'][j], w['rwkv_w_k'][j], w['rwkv_w_v'][j],
                                     w['rwkv_w_o'][j], w['rwkv_w0'][j], w['rwkv_w1'][j], w['rwkv_w2'][j],
                                     w['rwkv_a0'][j], w['rwkv_a1'][j], w['rwkv_a2'][j], w['rwkv_g1'][j],
                                     w['rwkv_g2'][j], w['rwkv_k_k'][j], w['rwkv_k_a'][j], w['rwkv_r_k'][j],
                                     w['rwkv_ln_w'][j], w['rwkv_ln_b'][j], rS[j], rshift[j])
            new_S.append(S)
            new_shift.append(sh)
        h = h + out
        h = h + swiglu(rms_norm(h, w['norm_ffn'][i]), w['ffn_w_gate'][i], w['ffn_w_up'][i], w['ffn_w_down'][i])
        gate = jax.nn.sigmoid(rms_norm(h, w['norm_ple'][i]) @ w['ple_w_gate'][i])
        h = h + gate * (p[i].astype(h.dtype) @ w['ple_w_proj'][i])
    y = rms_norm(h, w['norm_final'])
    return y, jnp.stack(new_C), jnp.stack(new_n), jnp.stack(new_m), jnp.stack(new_S), jnp.stack(new_shift)


def setup_inputs(seed: int = 0) -> dict:
    key = jax.random.key(seed)
    ks = list(jax.random.split(key, 64))
    f32 = jnp.float32

    def nrm(shape, scale):
        return jax.random.normal(ks.pop(), shape, f32) * scale

    def gain(shape):
        return 1.0 + nrm(shape, 0.02)

    D = D_MODEL
    sd = D ** -0.5
    NM, NR = N_MLSTM_LAYERS, N_RWKV_LAYERS
    H, DK, DV = MLSTM_HEADS, MLSTM_DQK, MLSTM_DV
    RH, RN = RWKV_HEADS, RWKV_HEAD_DIM
    return {
        'x_prompt': nrm((BATCH, SEQ, D), 1.0),
        'x_sample': nrm((DEC_BATCH, DEC_SEQ, D), 1.0),
        'state_mlstm_C': nrm((NM, DEC_BATCH, H, DK, DV), 0.1),
        'state_mlstm_n': nrm((NM, DEC_BATCH, H, DK), 0.1),
        'state_mlstm_m': nrm((NM, DEC_BATCH, H), 1.0),
        'state_rwkv_S': nrm((NR, DEC_BATCH, RH, RN, RN), 0.1),
        'state_rwkv_shift': nrm((NR, DEC_BATCH, D), 1.0),
        'p_prompt': nrm((DEPTH, BATCH, SEQ, PLE_DIM), 1.0),
        'p_sample': nrm((DEPTH, DEC_BATCH, DEC_SEQ, PLE_DIM), 1.0),
        'norm_mix': gain((DEPTH, D)),
        'norm_ffn': gain((DEPTH, D)),
        'norm_ple': gain((DEPTH, D)),
        'norm_final': gain((D,)),
        'ffn_w_gate': nrm((DEPTH, D, D_FF), sd),
        'ffn_w_up': nrm((DEPTH, D, D_FF), sd),
        'ffn_w_down': nrm((DEPTH, D_FF, D), D_FF ** -0.5),
        'ple_w_proj': nrm((DEPTH, PLE_DIM, D), PLE_DIM ** -0.5),
        'ple_w_gate': nrm((DEPTH, D, D), sd),
        'mlstm_w_q': nrm((NM, D, H * DK), sd),
        'mlstm_w_k': nrm((NM, D, H * DK), sd),
        'mlstm_w_v': nrm((NM, D, H * DV), sd),
        'mlstm_w_igate': nrm((NM, D, H), sd),
        'mlstm_b_igate': -1.0 + nrm((NM, H), 0.1),
        'mlstm_w_fgate': nrm((NM, D, H), sd),
        'mlstm_b_fgate': 3.0 + nrm((NM, H), 0.1),
        'mlstm_w_ogate': nrm((NM, D, D), sd),
        'mlstm_norm_w': gain((NM, D)),
        'mlstm_w_out': nrm((NM, D, D), sd),
        'rwkv_mu': jax.random.uniform(ks.pop(), (NR, 6, D), f32),
        'rwkv_w_r': nrm((NR, D, D), sd),
        'rwkv_w_k': nrm((NR, D, D), sd),
        'rwkv_w_v': nrm((NR, D, D), sd),
        'rwkv_w_o': nrm((NR, D, D), sd),
        'rwkv_w0': -6.0 + 5.0 * jax.random.uniform(ks.pop(), (NR, D), f32),
        'rwkv_w1': nrm((NR, D, DECAY_LORA), sd),
        'rwkv_w2': nrm((NR, DECAY_LORA, D), 0.5 * DECAY_LORA ** -0.5),
        'rwkv_a0': nrm((NR, D), 0.1),
        'rwkv_a1': nrm((NR, D, AAA_LORA), sd),
        'rwkv_a2': nrm((NR, AAA_LORA, D), 0.5 * AAA_LORA ** -0.5),
        'rwkv_g1': nrm((NR, D, GATE_LORA), sd),
        'rwkv_g2': nrm((NR, GATE_LORA, D), GATE_LORA ** -0.5),
        'rwkv_k_k': 0.85 + nrm((NR, D), 0.05),
        'rwkv_k_a': 1.0 + nrm((NR, D), 0.05),
        'rwkv_r_k': nrm((NR, RH, RN), 0.1),
        'rwkv_ln_w': gain((NR, D)),
        'rwkv_ln_b': nrm((NR, D), 0.02),
    }


def reference(x_prompt, x_sample, state_mlstm_C, state_mlstm_n, state_mlstm_m, state_rwkv_S, state_rwkv_shift,
              p_prompt, p_sample, norm_mix, norm_ffn, norm_ple, norm_final, ffn_w_gate, ffn_w_up, ffn_w_down,
              ple_w_proj, ple_w_gate, mlstm_w_q, mlstm_w_k, mlstm_w_v, mlstm_w_igate, mlstm_b_igate,
              mlstm_w_fgate, mlstm_b_fgate, mlstm_w_ogate, mlstm_norm_w, mlstm_w_out, rwkv_mu, rwkv_w_r,
              rwkv_w_k, rwkv_w_v, rwkv_w_o, rwkv_w0, rwkv_w1, rwkv_w2, rwkv_a0, rwkv_a1, rwkv_a2, rwkv_g1,
              rwkv_g2, rwkv_k_k, rwkv_k_a, rwkv_r_k, rwkv_ln_w, rwkv_ln_b):
    w = {
        'norm_mix': norm_mix, 'norm_ffn': norm_ffn, 'norm_ple': norm_ple, 'norm_final': norm_final,
        'ffn_w_gate': ffn_w_gate, 'ffn_w_up': ffn_w_up, 'ffn_w_down': ffn_w_down,
        'ple_w_proj': ple_w_proj, 'ple_w_gate': ple_w_gate,
        'mlstm_w_q': mlstm_w_q, 'mlstm_w_k': mlstm_w_k, 'mlstm_w_v': mlstm_w_v,
        'mlstm_w_igate': mlstm_w_igate, 'mlstm_b_igate': mlstm_b_igate,
        'mlstm_w_fgate': mlstm_w_fgate, 'mlstm_b_fgate': mlstm_b_fgate,
        'mlstm_w_ogate': mlstm_w_ogate, 'mlstm_norm_w': mlstm_norm_w, 'mlstm_w_out': mlstm_w_out,
        'rwkv_mu': rwkv_mu, 'rwkv_w_r': rwkv_w_r, 'rwkv_w_k': rwkv_w_k, 'rwkv_w_v': rwkv_w_v,
        'rwkv_w_o': rwkv_w_o, 'rwkv_w0': rwkv_w0, 'rwkv_w1': rwkv_w1, 'rwkv_w2': rwkv_w2,
        'rwkv_a0': rwkv_a0, 'rwkv_a1': rwkv_a1, 'rwkv_a2': rwkv_a2, 'rwkv_g1': rwkv_g1, 'rwkv_g2': rwkv_g2,
        'rwkv_k_k': rwkv_k_k, 'rwkv_k_a': rwkv_k_a, 'rwkv_r_k': rwkv_r_k,
        'rwkv_ln_w': rwkv_ln_w, 'rwkv_ln_b': rwkv_ln_b,
    }
    f32 = jnp.float32
    B = x_prompt.shape[0]
    zC = jnp.zeros((N_MLSTM_LAYERS, B, MLSTM_HEADS, MLSTM_DQK, MLSTM_DV), f32)
    zn = jnp.zeros((N_MLSTM_LAYERS, B, MLSTM_HEADS, MLSTM_DQK), f32)
    zm = jnp.zeros((N_MLSTM_LAYERS, B, MLSTM_HEADS), f32)
    zS = jnp.zeros((N_RWKV_LAYERS, B, RWKV_HEADS, RWKV_HEAD_DIM, RWKV_HEAD_DIM), f32)
    zsh = jnp.zeros((N_RWKV_LAYERS, B, D_MODEL), x_prompt.dtype)
    y_prompt, C_p, n_p, m_p, S_p, shift_p = trunk(x_prompt, p_prompt, zC, zn, zm, zS, zsh, w)
    y_sample, C_s, n_s, m_s, S_s, shift_s = trunk(x_sample, p_sample, state_mlstm_C, state_mlstm_n,
                                                  state_mlstm_m, state_rwkv_S, state_rwkv_shift, w)
    return (y_prompt, y_sample, C_p, n_p, m_p, S_p, shift_p, C_s, n_s, m_s, S_s, shift_s)
```

```python
import math
from contextlib import ExitStack

import numpy as np
import concourse.bass as bass
import concourse.mybir as mybir
from concourse.bass_utils import run_bass_kernel_spmd

F32 = mybir.dt.float32
BF16 = mybir.dt.bfloat16
AF = mybir.ActivationFunctionType
ALU = mybir.AluOpType
AX = mybir.AxisListType

D = 2048
KC = 16
DFF = 5632
NFF = DFF // 128
PLE = 256
MH, MDK, MDV = 8, 128, 256
RH, RN = 32, 64
NG = 16
SEQ = 2048
TB = 512
NSEQ = 16
ST_ = 8
NV = 20
(V_NMIX0, V_NFFN0, V_NPLE0, V_NMIX1, V_NFFN1, V_NPLE1, V_MNW, V_MU0) = range(8)
V_W0, V_A0, V_KK, V_KA, V_LNW, V_LNB, V_RK = 13, 14, 15, 16, 17, 18, 19
V_1MU0 = 20
V_1KA = 26
NVX = 27
CH = F32
WC_TOT = 400000
WC_N = 3
EXPM05 = math.exp(-0.5)


class Prog:
    ENG = ("pe", "act", "dve", "pool", "sp")

    def __init__(self, nc, es, n_dma_sems=(12, 8, 8)):
        self.nc = nc
        self.eng = {"pe": nc.tensor, "act": nc.scalar, "dve": nc.vector,
                    "pool": nc.gpsimd, "sp": nc.sync}
        self.sem = {}
        self.cnt = {}
        for e in self.ENG:
            self.sem[e] = es.enter_context(nc.semaphore("s_" + e))
            self.cnt[e] = 0
        self.dq = {}
        for q, n in zip(("sp", "pool", "act"), n_dma_sems):
            sems = []
            for i in range(n):
                nm = "d_%s%d" % (q, i)
                self.sem[nm] = es.enter_context(nc.semaphore(nm))
                sems.append(nm)
            self.dq[q] = {"sems": sems, "i": 0}
        self.waited = {}
        self.res = {}
        self.n_inst = 0
        self.n_wait = 0
        self.out_events = []
        self.last_dma = {}
        self.relax_same = False

    def _need(self, eng, ev):
        if ev is None:
            return
        s, v = ev
        if s == eng and (eng == "pe" or self.relax_same):
            return
        k = (eng, s)
        if self.waited.get(k, 0) >= v:
            return
        self.waited[k] = v
        self.eng[eng].wait_ge(self.sem[s], v)
        self.n_wait += 1

    def _deps(self, eng, reads, writes):
        for r in reads:
            st = self.res.get(r)
            if st is not None:
                self._need(eng, st[0])
                if isinstance(r, tuple) and r[0] == "ps":
                    for s, v in list(st[1].items()):
                        if s != eng:
                            self._need(eng, (s, v))
        for w in writes:
            st = self.res.get(w)
            if st is not None:
                self._need(eng, st[0])
                for s, v in list(st[1].items()):
                    self._need(eng, (s, v))

    def _record(self, ev, reads, writes):
        s, v = ev
        for r in reads:
            st = self.res.setdefault(r, [None, {}])
            if st[1].get(s, 0) < v:
                st[1][s] = v
        for w in writes:
            self.res[w] = [ev, {}]

    def op(self, eng, fn, r=(), w=()):
        self._deps(eng, r, w)
        ins = fn(self.eng[eng])
        self.cnt[eng] += 1
        ev = (eng, self.cnt[eng])
        ins.then_inc(self.sem[eng], 1)
        self._record(ev, r, w)
        self.n_inst += 1
        return ev

    def dma(self, q, out, in_, r=(), w=(), is_output=False, **kw):
        d = self.dq[q]
        i = d["i"]
        d["i"] += 1
        n = len(d["sems"])
        sname = d["sems"][i % n]
        tgt = 16 * (i // n + 1)
        if i >= n:
            self._need(q, (sname, tgt - 16))
        self._deps(q, r, w)
        ins = self.eng[q].dma_start(out=out, in_=in_, **kw)
        ins.then_inc(self.sem[sname], 16)
        ev = (sname, tgt)
        self._record(ev, r, w)
        self.n_inst += 1
        self.last_dma[sname] = tgt
        if is_output:
            self.out_events.append(ev)
        return ev

    def barrier(self):
        evs = [(e, self.cnt[e]) for e in ("pe", "act", "dve", "pool") if self.cnt[e]]
        evs += list(self.last_dma.items())
        for e in self.ENG:
            for ev in evs:
                if ev[0] != e:
                    self._need(e, ev)
        self.res = {}

    def finish(self):
        for ev in self.out_events:
            self._need("sp", ev)
        for e in ("pe", "act", "dve", "pool"):
            if self.cnt[e]:
                self._need("sp", (e, self.cnt[e]))
        for s, v in self.last_dma.items():
            self._need("sp", (s, v))


class PsumPool:
    def __init__(self, nc, es):
        self.banks = [es.enter_context(nc.psum_tensor("psb%d" % i, [128, 512], F32)) for i in range(8)]
        self.held = set()
        self.i = 0

    def get(self, hold=False):
        for _ in range(16):
            b = self.i % 8
            self.i += 1
            if b not in self.held:
                if hold:
                    self.held.add(b)
                return b
        raise RuntimeError("no free PSUM bank")

    def release(self, b):
        self.held.discard(b)

    def f32(self, b):
        return self.banks[b]

    def bf16(self, b):
        return self.banks[b].bitcast(BF16)


def AP(t, offset, dims):
    return bass.AP(t, offset, [list(d) for d in dims])


def make_consts():
    c = {}
    c["ident"] = np.eye(128, dtype=np.float32)
    c["ones"] = np.ones((128, 128), dtype=np.float32)
    p = np.arange(128)
    c["bones"] = (p[:, None] // 64 == p[None, :] // 64).astype(np.float32)
    for L, nm in ((64, "p"), (8, "s")):
        same = (p[:, None] // L == p[None, :] // L)
        le = p[:, None] <= p[None, :]
        lt = p[:, None] < p[None, :]
        c["negmask_" + nm] = np.where(same & le, 0.0, 30000.0).astype(np.float32)
        msu = (same & lt).astype(np.float32)
        miu = (same & le).astype(np.float32)
        c["mu2_" + nm] = np.concatenate([msu, miu], axis=1)
        c["msl_" + nm] = np.ascontiguousarray(msu.T)
    sel = np.zeros((128, 2, 128), np.float32)
    sel[0, 0, :] = 1.0
    sel[64, 1, :] = 1.0
    c["sel_p"] = sel
    oh = np.zeros((128, 16), np.float32)
    oh[np.arange(16) * 8, np.arange(16)] = 1.0
    c["onehot_s"] = oh
    rm = (p[:, None] // 8 == np.arange(16)[None, :]).astype(np.float32)
    c["rowmask_s"] = rm
    sm = (np.arange(16)[:, None] == (p[None, :] // 8)).astype(np.float32)
    c["seqmask_s"] = np.ascontiguousarray(np.broadcast_to(sm[None], (128, 16, 128))).astype(np.float32)
    t = np.arange(TB)
    c["rst64"] = np.ascontiguousarray(np.broadcast_to((t % 64 != 0).astype(np.float32)[None], (128, TB)))
    c["rst8"] = np.ascontiguousarray(np.broadcast_to((p % 8 != 0).astype(np.float32)[None], (128, 128)))
    c["addrst8"] = np.ascontiguousarray(np.broadcast_to(np.where(p % 8 == 0, -1e30, 0.0).astype(np.float32)[None], (128, 128)))
    c["onesrow"] = np.ones((128, TB), np.float32)
    return c


CONST_SHAPES = {k: v.shape for k, v in make_consts().items()}

IN_SPECS = {
    "xp": [SEQ, D], "xs": [128, D], "pp": [2, SEQ, PLE], "psm": [2, 128, PLE],
    "mC": [NSEQ, MH, MDK, MDV], "mn": [128, 128], "mmT": [MH, NSEQ],
    "rS": [NSEQ, RH, RN, RN], "rsh": [NSEQ, D],
    "vecs": [NV * 16, 128], "nfin": [1, D], "nmix1": [1, D], "big": [MH, 1], "bfg": [MH, 1],
    "ffn_wg": [2, D, DFF], "ffn_wu": [2, D, DFF], "ffn_wd": [2, DFF, D],
    "ple_wp": [2, PLE, D], "ple_wg": [2, D, D],
    "m_wq": [D, MH * MDK], "m_wk": [D, MH * MDK], "m_wv": [D, D], "m_wi": [D, MH], "m_wf": [D, MH],
    "m_wo": [D, D], "m_wout": [D, D],
    "r_wr": [D, D], "r_wk": [D, D], "r_wv": [D, D], "r_wo": [D, D],
    "r_w1": [D, 96], "r_w2": [96, D], "r_a1": [D, 96], "r_a2": [96, D], "r_g1": [D, 256], "r_g2": [256, D],
}
OUT_SPECS = {
    "yp": [SEQ, D], "ys": [128, D],
    "Cp": [MH, MDK, MDV], "np_": [MH, MDK], "mp": [MH, 1], "Sp": [RH, RN, RN], "shp": [1, D],
    "Cs": [NSEQ, MH, MDK, MDV], "ns": [128, 128], "msT": [MH, NSEQ], "Ss": [NSEQ, RH, RN, RN], "shs": [NSEQ, D],
}


def build(cfg=None):
    cfg = cfg or {}
    blocks = cfg.get("blocks")
    if blocks is None:
        blocks = [("p", i) for i in range(SEQ // TB)] + [("s", 0)]
    stop = cfg.get("stop", "end")
    taps = cfg.get("taps", ())
    nc = bass.Bass("TRN2", target_bir_lowering=False)
    dr = {}
    for k, shp in IN_SPECS.items():
        dr[k] = nc.dram_tensor(k, list(shp), F32, kind="ExternalInput")
    for k, shp in CONST_SHAPES.items():
        dr["c_" + k] = nc.dram_tensor("c_" + k, list(shp), F32, kind="ExternalInput")
    for k, shp in OUT_SPECS.items():
        dr[k] = nc.dram_tensor(k, list(shp), F32, kind="ExternalOutput")
    for i_ in range(WC_N):
        dr["wcache%d" % i_] = nc.dram_tensor("wcache%d" % i_, [128, WC_TOT], BF16, kind="Internal")
    tap_d = {}

    with ExitStack() as es:
        P = Prog(nc, es)
        P.relax_same = bool(cfg.get("relax_same", False))
        PS = PsumPool(nc, es)

        uid = [0]

        def sb(name, shape, dt=F32, stack=es):
            uid[0] += 1
            return stack.enter_context(nc.sbuf_tensor("%s_%d" % (name, uid[0]), list(shape), dt))

        def tap(name, src_ap, shape, r):
            if name not in taps:
                return
            if name not in tap_d:
                tap_d[name] = nc.dram_tensor("tap_" + name, list(shape), F32, kind="ExternalOutput")
            P.dma("pool", tap_d[name].ap(), src_ap, r=r, is_output=True)

        def TT(eng, out, a, b, op, r, w):
            return P.op(eng, lambda e: e.tensor_tensor(out=out, in0=a, in1=b, op=op), r, w)

        def TS(eng, out, a, s1, s2, op0, op1, r, w):
            if s2 is None:
                return P.op(eng, lambda e: e.tensor_scalar(out=out, in0=a, scalar1=s1, scalar2=None, op0=op0), r, w)
            return P.op(eng, lambda e: e.tensor_scalar(out=out, in0=a, scalar1=s1, scalar2=s2, op0=op0, op1=op1), r, w)

        def STT(out, a, s, b, op0, op1, r, w):
            return P.op("dve", lambda e: e.scalar_tensor_tensor(out=out, in0=a, scalar=s, in1=b, op0=op0, op1=op1), r, w)

        def ACT(out, in_, func, r, w, scale=None, bias=None, accum=None):
            kw = {}
            if scale is not None:
                kw["scale"] = scale
            if bias is not None:
                kw["bias"] = bias
            if accum is not None:
                kw["accum_out"] = accum
            return P.op("act", lambda e: e.activation(out=out, in_=in_, func=func, **kw), r, w)

        def CP(eng, out, in_, r, w):
            if eng == "act":
                return P.op("act", lambda e: e.copy(out=out, in_=in_), r, w)
            return P.op(eng, lambda e: e.tensor_copy(out=out, in_=in_), r, w)

        def MM(out, lhsT, rhs, start, stop_, r, w, skip=False):
            return P.op("pe", lambda e: e.matmul(out, lhsT=lhsT, rhs=rhs, start=start, stop=stop_,
                                                 skip_group_check=skip), r, w)

        def TR(out, in_, ident, r, w):
            return P.op("pe", lambda e: e.transpose(out=out, in_=in_, identity=ident), r, w)

        def MSET(eng, ap, val, w):
            return P.op(eng, lambda e: e.memset(ap, val), (), w)

        NTMAX = TB // 128
        H = [sb("H%d" % t, [128, D]) for t in range(NTMAX)]
        XT = sb("XT", [128, KC, TB + 1], BF16)
        xb = sb("xb", [128, D], BF16)
        nst = sb("nst", [128, 4])
        NWS, NWB = 2, 2
        wst = [sb("wst%d" % i, [128, 4096]) for i in range(NWS)]
        wbf = [sb("wbf%d" % i, [128, 4096], BF16) for i in range(NWB)]
        wctr = {"s": 0, "b": 0, "c": 0, "p": 0, "off": 0, "ci": 0, "mode": "store"}
        wcidx = {}
        idf = sb("idf", [128, 128])
        idb = sb("idb", [128, 128], BF16)
        onesf = sb("onesf", [128, 128])
        bonesf = sb("bonesf", [128, 128])
        FMV = sb("FMV", [128, NVX * 16])
        xlast = sb("xlast", [128, KC], BF16)
        Cst = sb("Cst", [128, MH, MDV + 1])
        STD = sb("STD", [128, NG, 2, RN])
        carry = sb("carry", [8, 4])

        def load_const(name, dst, q="sp"):
            P.dma(q, dst, dr["c_" + name].ap(), w=[name])

        P.dma("sp", idf[:], dr["c_ident"].ap(), w=["idf"])
        P.dma("sp", onesf[:], dr["c_ones"].ap(), w=["onesf"])
        P.dma("sp", bonesf[:], dr["c_bones"].ap(), w=["bonesf"])
        CP("dve", idb[:], idf[:], ["idf"], ["idb"])
        with ExitStack() as st:
            vl = sb("vl", [128, 3, 128], F32, st)
            nrows = NV * 16
            for i in range(3):
                n = min(128, nrows - i * 128)
                P.dma("sp", vl[0:n, i, :], dr["vecs"].ap()[i * 128:i * 128 + n, :], w=[("vl", i)])
            b = PS.get()
            for i in range(3):
                n = min(128, nrows - i * 128)
                TR(PS.f32(b)[:, i * 128:i * 128 + n], vl[0:n, i, :], idf[0:n, 0:n], [("vl", i), "idf"], [("ps", b)])
            CP("dve", FMV[:, 0:nrows], PS.f32(b)[:, 0:nrows], [("ps", b)], ["FMV"])
            TS("dve", FMV[:, V_1MU0 * 16:(V_1MU0 + 6) * 16], FMV[:, V_MU0 * 16:(V_MU0 + 6) * 16], -1.0, 1.0,
               ALU.mult, ALU.add, ["FMV"], ["FMV"])
            TS("dve", FMV[:, V_1KA * 16:(V_1KA + 1) * 16], FMV[:, V_KA * 16:(V_KA + 1) * 16], -1.0, 1.0,
               ALU.mult, ALU.add, ["FMV"], ["FMV"])
            P.barrier()
        MSET("dve", Cst[:], 0.0, ["Cst"])
        MSET("dve", STD[:], 0.0, ["STD"])
        MSET("dve", xlast[:], 0.0, ["xlast"])
        MSET("dve", carry[:], 0.0, ["carry"])

        def fmv(v, kc0=0, n=1):
            return FMV[:, v * 16 + kc0:v * 16 + kc0 + n]

        CAST_ENGS = cfg.get("cast_engs", ("act",))

        def wpanel(src3, np_, kcn, ncols, scales=None):
            sz = kcn * ncols
            nver = 1 if scales is None else len(scales)
            assert sz * nver <= 4096
            bi = wctr["b"] % NWB
            wctr["b"] += 1
            idx = wctr["p"]
            wctr["p"] += 1
            sig = (np_, kcn, ncols, nver)
            if wctr["mode"] == "load":
                ci, off, sig0 = wcidx[idx]
                assert sig0 == sig, (idx, sig0, sig)
                P.dma("sp", wbf[bi][0:np_, 0:nver * sz], dr["wcache%d" % ci].ap()[0:np_, off:off + nver * sz], r=[("wc", idx)], w=[("wbf", bi)])
                return [wbf[bi][0:np_, vi * sz:(vi + 1) * sz].rearrange("p (k n) -> p k n", k=kcn) for vi in range(nver)], ("wbf", bi)
            si = wctr["s"] % NWS
            wctr["s"] += 1
            stg = wst[si][0:np_, 0:sz].rearrange("p (k n) -> p k n", k=kcn)
            P.dma("sp", stg, src3, w=[("wst", si)])
            outs = []
            for vi in range(nver):
                dst = wbf[bi][0:np_, vi * sz:(vi + 1) * sz].rearrange("p (k n) -> p k n", k=kcn)
                eng = CAST_ENGS[wctr["c"] % len(CAST_ENGS)]
                wctr["c"] += 1
                if scales is None or scales[vi] is None:
                    CP(eng, dst, stg, [("wst", si)], [("wbf", bi)] if vi == 0 else [("wbf", bi)])
                else:
                    eng = "pool"
                    sc = AP(FMV, scales[vi], [[NVX * 16, np_], [1, kcn], [0, ncols]])
                    TT(eng, dst, stg, sc, ALU.mult, [("wst", si), "FMV"], [("wbf", bi)])
                outs.append(dst)
            if wctr["mode"] == "store":
                if wctr["off"] + nver * sz > WC_TOT:
                    wctr["ci"] += 1
                    wctr["off"] = 0
                    assert wctr["ci"] < WC_N
                ci, off = wctr["ci"], wctr["off"]
                wctr["off"] += nver * sz
                wcidx[idx] = (ci, off, sig)
                P.dma("act" if (scales is None) else "pool", dr["wcache%d" % ci].ap()[0:np_, off:off + nver * sz], wbf[bi][0:np_, 0:nver * sz],
                      r=[("wbf", bi)], w=[("wc", idx)])
            return outs, ("wbf", bi)

        def wsrc(name, layer, r0, nrows, c0, ncols):
            t = dr[name].ap()
            if layer is not None:
                t = t[layer]
            if nrows <= 128:
                return t[r0:r0 + nrows, c0:c0 + ncols].rearrange("(k p) n -> p k n", k=1), nrows, 1
            return t[r0:r0 + nrows, c0:c0 + ncols].rearrange("(k p) n -> p k n", p=128), 128, nrows // 128

        def load_block(kind, bi, nt):
            for t in range(nt):
                if kind == "p":
                    src = dr["xp"].ap()[bi * TB + t * 128:bi * TB + (t + 1) * 128, :]
                else:
                    src = dr["xs"].ap()
                P.dma("sp", H[t][:], src, w=[("H", t)])

        def rms_stats(t):
            ACT(xb[:], H[t][:], AF.Square, [("H", t)], ["xb", "nst"], accum=nst[:, 0:1])
            ACT(nst[:, 1:2], nst[:, 0:1], AF.Sqrt, ["nst"], ["nst"], scale=1.0 / D, bias=epsb[:, 0:1])
            P.op("dve", lambda e: e.reciprocal(out=nst[:, 2:3], in_=nst[:, 1:2]), ["nst"], ["nst"])

        def norm_to_xt(vidx, nt):
            for t in range(nt):
                rms_stats(t)
                ACT(xb[:], H[t][:], AF.Copy, [("H", t), "nst"], ["xb"], scale=nst[:, 2:3])
                for half in range(2):
                    b = PS.get()
                    for j in range(8):
                        kc = half * 8 + j
                        TR(PS.bf16(b)[:, j * 128:(j + 1) * 128], xb[:, kc * 128:(kc + 1) * 128], idb[:], ["xb", "idb"], [("ps", b)])
                    src = PS.bf16(b)[:, 0:1024].rearrange("p (a b) -> p a b", a=8)
                    gb = AP(FMV, vidx * 16 + half * 8, [[NVX * 16, 128], [1, 8], [0, 128]])
                    TT("dve", XT[:, half * 8:half * 8 + 8, 1 + t * 128:1 + (t + 1) * 128], src, gb, ALU.mult,
                       [("ps", b), "FMV"], ["XT"])

        def xt_cols(kc, c0, n):
            return XT[:, kc, 1 + c0:1 + c0 + n]

        def mm_fm(name, layer, c0, ncols, ntok, consumer, rhs_fn=None, scales=None, keys_r=("XT",)):
            pc = 256 if scales is None else 128
            for p0 in range(0, ncols, pc):
                n = min(pc, ncols - p0)
                src, np_, kcn = wsrc(name, layer, 0, D, c0 + p0, n)
                wv, wkey = wpanel(src, np_, kcn, n, scales)
                for m0 in range(0, n, 128):
                    mw = min(128, n - m0)
                    b = PS.get()
                    steps = []
                    for vi, wt in enumerate(wv):
                        for kc in range(KC):
                            steps.append((wt[:, kc, m0:m0 + mw], vi, kc))
                    for i, (lt, vi, kc) in enumerate(steps):
                        rhs = rhs_fn(vi, kc) if rhs_fn else xt_cols(kc, 0, ntok)
                        MM(PS.f32(b)[0:mw, 0:ntok], lt, rhs, i == 0, i == len(steps) - 1, [wkey] + list(keys_r), [("ps", b)])
                    consumer((c0 + p0 + m0) // 128, mw, b)

        def mm_tm(name, layer, c0, ncols, nt, consumer, lhs_fn=None, keys_r=("XT",), kin=D, r0=0):
            for p0 in range(0, ncols, 256):
                n = min(256, ncols - p0)
                src, np_, kcn = wsrc(name, layer, r0, kin, c0 + p0, n)
                wv, wkey = wpanel(src, np_, kcn, n)
                wt = wv[0]
                for t in range(nt):
                    b = PS.get()
                    for kc in range(kcn):
                        lt = lhs_fn(kc, t) if lhs_fn else xt_cols(kc, t * 128, 128)
                        MM(PS.f32(b)[:, 0:n], lt, wt[:, kc, :], kc == 0, kc == kcn - 1, [wkey] + list(keys_r), [("ps", b)])
                    consumer(t, c0 + p0, n, b)

        def ffn(layer, nt):
            ntok = nt * 128
            norm_to_xt(V_NFFN0 if layer == 0 else V_NFFN1, nt)
            with ExitStack() as st:
                act = sb("ffn_act", [128, 16, TB], BF16, st)
                sg = [sb("ffn_sg%d" % i, [128, TB], F32, st) for i in range(2)]
                groups = [(0, 16), (16, 16), (32, 12)]
                for g0, gn in groups:
                    for m in range(g0, g0 + gn, 2):
                        srcg, np_, kcn = wsrc("ffn_wg", layer, 0, D, m * 128, 256)
                        wg, kg = wpanel(srcg, np_, kcn, 256)
                        srcu, np_, kcn = wsrc("ffn_wu", layer, 0, D, m * 128, 256)
                        wu, ku = wpanel(srcu, np_, kcn, 256)
                        for mm_ in range(2):
                            bg_ = PS.get()
                            for kc in range(KC):
                                MM(PS.f32(bg_)[:, 0:ntok], wg[0][:, kc, mm_ * 128:(mm_ + 1) * 128], xt_cols(kc, 0, ntok),
                                   kc == 0, kc == KC - 1, [kg, "XT"], [("ps", bg_)])
                            bu_ = PS.get()
                            for kc in range(KC):
                                MM(PS.f32(bu_)[:, 0:ntok], wu[0][:, kc, mm_ * 128:(mm_ + 1) * 128], xt_cols(kc, 0, ntok),
                                   kc == 0, kc == KC - 1, [ku, "XT"], [("ps", bu_)])
                            j = m + mm_ - g0
                            s_ = sg[(m + mm_) % 2]
                            sk = ("sg", (m + mm_) % 2)
                            ACT(s_[:, 0:ntok], PS.f32(bg_)[:, 0:ntok], AF.Silu, [("ps", bg_)], [sk])
                            TT("dve", act[:, j, 0:ntok], PS.f32(bu_)[:, 0:ntok], s_[:, 0:ntok], ALU.mult,
                               [("ps", bu_), sk], [("act", j)])
                    def down_cons(t, c0, n, b):
                        TT("dve", H[t][:, c0:c0 + n], PS.f32(b)[:, 0:n], H[t][:, c0:c0 + n], ALU.add,
                           [("ps", b), ("H", t)], [("H", t)])
                    mm_tm("ffn_wd", layer, 0, D, nt, down_cons,
                          lhs_fn=lambda kc, t: act[:, kc, t * 128:(t + 1) * 128],
                          keys_r=[("act", j) for j in range(gn)], kin=gn * 128, r0=g0 * 128)
                P.barrier()

        def ple(layer, kind, bi, nt):
            norm_to_xt(V_NPLE0 if layer == 0 else V_NPLE1, nt)
            with ExitStack() as st:
                pl = sb("ple_p", [128, PLE], F32, st)
                plb = sb("ple_pb", [128, PLE], BF16, st)
                pT = sb("ple_pT", [128, 2, TB], BF16, st)
                sgm = [sb("ple_sg%d" % i, [128, 256], F32, st) for i in range(2)]
                for t in range(nt):
                    if kind == "p":
                        src = dr["pp"].ap()[layer, bi * TB + t * 128:bi * TB + (t + 1) * 128, :]
                    else:
                        src = dr["psm"].ap()[layer]
                    P.dma("sp", pl[:], src, w=["pl"])
                    CP("dve", plb[:], pl[:], ["pl"], ["plb"])
                    b = PS.get()
                    for j in range(2):
                        TR(PS.bf16(b)[:, j * 128:(j + 1) * 128], plb[:, j * 128:(j + 1) * 128], idb[:], ["plb", "idb"], [("ps", b)])
                    CP("dve", pT[:, :, t * 128:(t + 1) * 128], PS.bf16(b)[:, 0:256].rearrange("p (a b) -> p a b", a=2),
                       [("ps", b)], ["pT"])
                ctr = [0]
                for p0 in range(0, D, 256):
                    srcg, np_, kcn = wsrc("ple_wg", layer, 0, D, p0, 256)
                    wg, kg = wpanel(srcg, np_, kcn, 256)
                    srcp, np2, kcn2 = wsrc("ple_wp", layer, 0, PLE, p0, 256)
                    wp, kp = wpanel(srcp, np2, kcn2, 256)
                    for t in range(nt):
                        b1 = PS.get()
                        for kc in range(KC):
                            MM(PS.f32(b1)[:, 0:256], xt_cols(kc, t * 128, 128), wg[0][:, kc, :], kc == 0, kc == KC - 1,
                               [kg, "XT"], [("ps", b1)])
                        b2 = PS.get()
                        for j in range(2):
                            MM(PS.f32(b2)[:, 0:256], pT[:, j, t * 128:(t + 1) * 128], wp[0][:, j, :], j == 0, j == 1,
                               [kp, "pT"], [("ps", b2)])
                        s_ = sgm[ctr[0] % 2]
                        sk = ("plesg", ctr[0] % 2)
                        ctr[0] += 1
                        ACT(s_[:], PS.f32(b1)[:, 0:256], AF.Sigmoid, [("ps", b1)], [sk])
                        TT("dve", s_[:], PS.f32(b2)[:, 0:256], s_[:], ALU.mult, [("ps", b2), sk], [sk])
                        TT("pool", H[t][:, p0:p0 + 256], H[t][:, p0:p0 + 256], s_[:], ALU.add, [sk, ("H", t)], [("H", t)])
                P.barrier()

        def final_out(kind, bi, nt):
            with ExitStack() as st:
                gt = sb("fin_g", [128, D], F32, st)
                yo = [sb("fin_y%d" % i, [128, D], F32, st) for i in range(2)]
                P.dma("sp", gt[:], AP(dr["nfin"], 0, [[0, 128], [1, D]]), w=["fin_g"])
                for t in range(nt):
                    rms_stats(t)
                    y = yo[t % 2]
                    STT(y[:], H[t][:], nst[:, 2:3], gt[:], ALU.mult, ALU.mult, [("H", t), "nst", "fin_g"], [("yo", t % 2)])
                    if kind == "p":
                        dst = dr["yp"].ap()[bi * TB + t * 128:bi * TB + (t + 1) * 128, :]
                    else:
                        dst = dr["ys"].ap()
                    P.dma("pool", dst, y[:], r=[("yo", t % 2)], is_output=True)
                P.barrier()

        epsb = sb("epsb", [128, 4])
        MSET("dve", epsb[:, 0:1], 1e-6, ["epsb"])
        MSET("dve", epsb[:, 1:2], 1.0, ["epsb"])
        MSET("dve", epsb[:, 2:3], 64e-5, ["epsb"])
        MSET("dve", epsb[:, 3:4], 1e-24, ["epsb"])

        gb15 = sb("gb15", [8, 2])
        P.dma("sp", gb15[:, 0:1], dr["big"].ap(), w=["gb15"])
        P.dma("sp", gb15[:, 1:2], dr["bfg"].ap(), w=["gb15"])
        TS("dve", gb15[:], gb15[:], 1.0 / 15.0, None, ALU.mult, None, ["gb15"], ["gb15"])

        def mlstm(kind, bi, nt):
            ntok = nt * 128
            W = ntok
            L = 64 if kind == "p" else 8
            nch = ntok // L
            sfx = kind
            norm_to_xt(V_NMIX0, nt)
            with ExitStack() as st:
                rw = {nm: sb("ml_r_" + nm, [8, W], F32, st) for nm in
                      ("li", "lf", "bg", "z", "cm", "cmP", "cmE", "t0", "qA", "qB", "qC", "qD")}
                QT = sb("ml_QT", [128, nt, 6, 8], F32, st)
                DEC = sb("ml_DEC", [128, nt, 16, 8], F32, st) if kind == "s" else sb("ml_DEC", [128, nt, 2, 8], F32, st)
                WT = sb("ml_WT", [128, nt, 8, 128], BF16, st)
                Dcm = sb("ml_Dcm", [128, 8, 128], F32, st)
                tmpw = sb("ml_tmpw", [128, 8, 128], F32, st)
                negm = sb("ml_negm", [128, 128], F32, st)
                onesrow = sb("ml_onesrow", [8, W], F32, st)
                qT = [sb("ml_qT%d" % i, [128, W], BF16, st) for i in range(2)]
                kT = [sb("ml_kT%d" % i, [128, W], BF16, st) for i in range(2)]
                V1 = sb("ml_V1", [128, nt, MDV + 1], BF16, st)
                ogs = sb("ml_ogs", [128, nt, MDV], F32, st)
                kw = sb("ml_kw", [128, 128], BF16, st)
                Sb = sb("ml_Sb", [128, 128], BF16, st)
                intra = sb("ml_intra", [128, MDV + 1], F32, st)
                hn = sb("ml_hn", [128, MDV + 1], F32, st)
                sc = sb("ml_sc", [128, 8], F32, st)
                HO = sb("ml_HO", [128, MDV], BF16, st)
                Cb = sb("ml_Cb", [128, MH, MDV + 1], BF16, st)
                YT = sb("ml_YT", [128, KC, W], BF16, st)
                if kind == "p":
                    kwL = [kw] + [sb("ml_kwL", [128, 128], BF16, st) for _ in range(nt - 1)]
                    SbL = [Sb] + [sb("ml_SbL", [128, 128], BF16, st) for _ in range(nt - 1)]
                    inL = [intra] + [sb("ml_inL", [128, MDV + 1], F32, st) for _ in range(nt - 1)]
                    scL = [sc] + [sb("ml_scL", [128, 8], F32, st) for _ in range(nt - 1)]
                    HOL = [HO] + [sb("ml_HOL", [128, MDV], BF16, st) for _ in range(nt - 1)]
                    junk = hn
                P.dma("sp", negm[:], dr["c_negmask_" + sfx].ap(), w=["negm"])
                MSET("pool", V1[:, :, MDV:MDV + 1], 1.0, ["V1"])
                if kind == "p":
                    P.dma("sp", onesrow[:], dr["c_onesrow"].ap()[0:8, 0:W], w=["onesrow"])
                    selp = sb("ml_selp", [128, 2, 128], F32, st)
                    P.dma("sp", selp[:], dr["c_sel_p"].ap(), w=["selp"])
                    CP("act", Cb[:], Cst[:], ["Cst"], ["Cb"])
                else:
                    rst8 = sb("ml_rst8", [8, 128], F32, st)
                    addrst8 = sb("ml_addrst8", [8, 128], F32, st)
                    m0T = sb("ml_m0T", [8, 16], F32, st)
                    onehot = sb("ml_onehot", [128, 16], F32, st)
                    rowmask = sb("ml_rowmask", [128, 16], F32, st)
                    seqmask = sb("ml_seqmask", [128, 16, 128], BF16, st)
                    rhs2 = sb("ml_rhs2", [128, 16, 8], F32, st)
                    QZ = sb("ml_QZ", [128, 16, 128], BF16, st)
                    KZ = sb("ml_KZ", [128, 16, 128], BF16, st)
                    C0f = sb("ml_C0f", [128, 16, MDV + 1], F32, st)
                    C0b = sb("ml_C0b", [128, 16, MDV + 1], BF16, st)
                    n0l = sb("ml_n0l", [128, 128], F32, st)
                    n0T = sb("ml_n0T", [128, 128], F32, st)
                    NS = sb("ml_NS", [128, 128], F32, st)
                    msr = sb("ml_msr", [8, 16], F32, st)
                    P.dma("sp", rst8[:], dr["c_rst8"].ap()[0:8, :], w=["rst8"])
                    P.dma("sp", addrst8[:], dr["c_addrst8"].ap()[0:8, :], w=["addrst8"])
                    P.dma("sp", m0T[:], dr["mmT"].ap(), w=["m0T"])
                    P.dma("sp", onehot[:], dr["c_onehot_s"].ap(), w=["onehot"])
                    P.dma("sp", rowmask[:], dr["c_rowmask_s"].ap(), w=["rowmask"])
                    P.dma("sp", wst[0][:, 0:2048], dr["c_seqmask_s"].ap().rearrange("p a b -> p (a b)"), w=[("wst", 0)])
                    CP("pool", seqmask[:].rearrange("p a b -> p (a b)"), wst[0][:, 0:2048], [("wst", 0)], ["seqmask"])
                    P.dma("sp", n0l[:], dr["mn"].ap(), w=["n0l"])
                    b = PS.get()
                    TR(PS.f32(b)[:, 0:128], n0l[:], idf[:], ["n0l", "idf"], [("ps", b)])
                    CP("dve", n0T[:], PS.f32(b)[:, 0:128], [("ps", b)], ["n0T"])

                wi, ki = wpanel(wsrc("m_wi", None, 0, D, 0, MH)[0], 128, KC, MH)
                bgi = PS.get()
                for kc in range(KC):
                    MM(PS.f32(bgi)[0:8, 0:ntok], wi[0][:, kc, :], xt_cols(kc, 0, ntok), kc == 0, kc == KC - 1, [ki, "XT"], [("ps", bgi)])
                ACT(rw["li"][:, 0:ntok], PS.f32(bgi)[0:8, 0:ntok], AF.Tanh, [("ps", bgi), "gb15"], ["li"], scale=1.0 / 15.0, bias=gb15[:, 0:1])
                TS("dve", rw["li"][:, 0:ntok], rw["li"][:, 0:ntok], 15.0, None, ALU.mult, None, ["li"], ["li"])
                wf, kf = wpanel(wsrc("m_wf", None, 0, D, 0, MH)[0], 128, KC, MH)
                bgf = PS.get()
                for kc in range(KC):
                    MM(PS.f32(bgf)[0:8, 0:ntok], wf[0][:, kc, :], xt_cols(kc, 0, ntok), kc == 0, kc == KC - 1, [kf, "XT"], [("ps", bgf)])
                ACT(rw["t0"][:, 0:ntok], PS.f32(bgf)[0:8, 0:ntok], AF.Tanh, [("ps", bgf), "gb15"], ["t0"], scale=1.0 / 15.0, bias=gb15[:, 1:2])
                ACT(rw["t0"][:, 0:ntok], rw["t0"][:, 0:ntok], AF.Exp, ["t0"], ["t0"], scale=-15.0)
                ACT(rw["lf"][:, 0:ntok], rw["t0"][:, 0:ntok], AF.Ln, ["t0", "epsb"], ["lf"], bias=epsb[0:8, 1:2])
                TS("dve", rw["lf"][:, 0:ntok], rw["lf"][:, 0:ntok], -1.0, None, ALU.mult, None, ["lf"], ["lf"])
                if kind == "p":
                    P.op("dve", lambda e: e.tensor_tensor_scan(out=rw["bg"][:, 0:ntok], data0=onesrow[:, 0:ntok], data1=rw["lf"][:, 0:ntok],
                                                               initial=carry[:, 0:1], op0=ALU.mult, op1=ALU.add),
                         ["onesrow", "lf", "carry"], ["bg"])
                else:
                    P.op("dve", lambda e: e.tensor_tensor_scan(out=rw["bg"][:, 0:ntok], data0=rst8[:, 0:ntok], data1=rw["lf"][:, 0:ntok],
                                                               initial=0.0, op0=ALU.mult, op1=ALU.add),
                         ["rst8", "lf"], ["bg"])
                TT("dve", rw["z"][:, 0:ntok], rw["li"][:, 0:ntok], rw["bg"][:, 0:ntok], ALU.subtract, ["li", "bg"], ["z"])
                if kind == "p":
                    P.op("dve", lambda e: e.tensor_tensor_scan(out=rw["cm"][:, 0:ntok], data0=rw["z"][:, 0:ntok], data1=rw["z"][:, 0:ntok],
                                                               initial=carry[:, 1:2], op0=ALU.max, op1=ALU.max),
                         ["z", "carry"], ["cm"])
                    CP("dve", rw["cmP"][:, L:ntok].rearrange("p (c l) -> p c l", l=L),
                       AP(rw["cm"], L - 1, [[W, 8], [L, nch - 1], [0, L]]), ["cm"], ["cmP"])
                    CP("dve", rw["cmP"][:, 0:L], AP(carry, 1, [[4, 8], [0, L]]), ["carry"], ["cmP"])
                else:
                    CP("dve", rw["t0"][:, 0:ntok], rw["z"][:, 0:ntok], ["z"], ["t0"])
                    zv = AP(rw["t0"], 0, [[W, 8], [8, 16], [1, 1]])
                    TT("dve", zv, zv, AP(m0T, 0, [[16, 8], [1, 16], [1, 1]]), ALU.max, ["t0", "m0T"], ["t0"])
                    P.op("dve", lambda e: e.tensor_tensor_scan(out=rw["cm"][:, 0:ntok], data0=addrst8[:, 0:ntok], data1=rw["t0"][:, 0:ntok],
                                                               initial=0.0, op0=ALU.add, op1=ALU.max),
                         ["t0", "addrst8"], ["cm"])
                    CP("dve", rw["cmP"][:, 0:ntok].rearrange("p (c l) -> p c l", l=L), AP(m0T, 0, [[16, 8], [1, 16], [0, L]]), ["m0T"], ["cmP"])
                CP("dve", rw["cmE"][:, 0:ntok].rearrange("p (c l) -> p c l", l=L),
                   AP(rw["cm"], L - 1, [[W, 8], [L, nch], [0, L]]), ["cm"], ["cmE"])
                if kind == "p":
                    if bi == SEQ // TB - 1:
                        TT("dve", rw["t0"][:, 0:1], rw["bg"][:, ntok - 1:ntok], rw["cm"][:, ntok - 1:ntok], ALU.add, ["bg", "cm"], ["t0"])
                        P.dma("pool", dr["mp"].ap(), rw["t0"][:, 0:1], r=["t0"], is_output=True)
                    CP("dve", carry[:, 0:1], rw["bg"][:, ntok - 1:ntok], ["bg", "cmP"], ["carry"])
                    CP("dve", carry[:, 1:2], rw["cm"][:, ntok - 1:ntok], ["cm", "cmP"], ["carry"])
                else:
                    TT("dve", msr[:], AP(rw["bg"], 7, [[W, 8], [8, 16]]), AP(rw["cm"], 7, [[W, 8], [8, 16]]), ALU.add, ["bg", "cm"], ["msr"])
                    P.dma("pool", dr["msT"].ap(), msr[:], r=["msr"], is_output=True)
                for nm in ("li", "lf", "bg", "z", "cm", "cmP", "cmE"):
                    tap("ml_%s_%s%d" % (nm, kind, bi), rw[nm][:, 0:ntok], [8, ntok], [nm])
                N_ = slice(0, ntok)
                TT("dve", rw["qA"][:, N_], rw["cmP"][:, N_], rw["cm"][:, N_], ALU.subtract, ["cmP", "cm"], ["qA"])
                ACT(rw["qA"][:, N_], rw["qA"][:, N_], AF.Exp, ["qA"], ["qA"])
                TT("dve", rw["qB"][:, N_], rw["z"][:, N_], rw["cmE"][:, N_], ALU.subtract, ["z", "cmE"], ["qB"])
                ACT(rw["qB"][:, N_], rw["qB"][:, N_], AF.Exp, ["qB"], ["qB"])
                TT("dve", rw["qC"][:, N_], rw["cmP"][:, N_], rw["cmE"][:, N_], ALU.subtract, ["cmP", "cmE"], ["qC"])
                ACT(rw["qC"][:, N_], rw["qC"][:, N_], AF.Exp, ["qC"], ["qC"])
                TT("dve", rw["qD"][:, N_], rw["bg"][:, N_], rw["cm"][:, N_], ALU.add, ["bg", "cm"], ["qD"])
                ACT(rw["qD"][:, N_], rw["qD"][:, N_], AF.Exp, ["qD"], ["qD"], scale=-1.0)
                qnames = ["qA", "qB", "qC", "qD", "z", "cm"]
                for t in range(nt):
                    b = PS.get()
                    for qi, qn in enumerate(qnames):
                        TR(PS.f32(b)[:, qi * 8:(qi + 1) * 8], rw[qn][:, t * 128:(t + 1) * 128], idf[0:8, 0:8], [qn, "idf"], [("ps", b)])
                    CP("dve", QT[:, t, :, :].rearrange("p a b -> p (a b)"), PS.f32(b)[:, 0:48], [("ps", b)], ["QT"])
                for t in range(nt):
                    if kind == "p":
                        b = PS.get()
                        for c in range(2):
                            MM(PS.f32(b)[:, c * 8:(c + 1) * 8], selp[:, c, :], QT[:, t, 2, :], True, True, ["selp", "QT"], [("ps", b)], skip=True)
                        CP("dve", DEC[:, t, :, :].rearrange("p a b -> p (a b)"), PS.f32(b)[:, 0:16], [("ps", b)], ["DEC"])
                    else:
                        TT("dve", rhs2[:], AP(QT, 2 * 8, [[nt * 48, 128], [0, 16], [1, 8]]), AP(onehot, 0, [[16, 128], [1, 16], [0, 8]]),
                           ALU.mult, ["QT", "onehot"], ["rhs2"])
                        b = PS.get()
                        MM(PS.f32(b)[:, 0:128], onesf[:], rhs2[:].rearrange("p a b -> p (a b)"), True, True, ["onesf", "rhs2"], [("ps", b)])
                        CP("dve", DEC[:, t, :, :].rearrange("p a b -> p (a b)"), PS.f32(b)[:, 0:128], [("ps", b)], ["DEC"])
                    TT("dve", Dcm[:], AP(idf, 0, [[128, 128], [0, 8], [1, 128]]),
                       AP(QT, t * 48 + 5 * 8, [[nt * 48, 128], [1, 8], [0, 128]]), ALU.mult, ["idf", "QT"], ["Dcm"])
                    for hh in range(2):
                        b = PS.get()
                        MM(PS.f32(b)[:, 0:512], onesf[:], Dcm[:, 4 * hh:4 * hh + 4, :].rearrange("p a b -> p (a b)"), True, True,
                           ["onesf", "Dcm"], [("ps", b)])
                        TT("dve", tmpw[:, 4 * hh:4 * hh + 4, :], PS.f32(b)[:, 0:512].rearrange("p (a b) -> p a b", a=4),
                           AP(QT, t * 48 + 4 * 8 + 4 * hh, [[nt * 48, 128], [1, 4], [0, 128]]), ALU.subtract, [("ps", b), "QT"], ["tmpw"])
                    TT("dve", tmpw[:], tmpw[:], AP(negm, 0, [[128, 128], [0, 8], [1, 128]]), ALU.max, ["tmpw", "negm"], ["tmpw"])
                    ACT(WT[:, t, :, :], tmpw[:], AF.Exp, ["tmpw"], [("WT", t)], scale=-1.0)

                for hp in range(MH // 2):
                    def qcons(m, mw, b):
                        CP("act", qT[m % 2][:, 0:ntok], PS.f32(b)[:, 0:ntok], [("ps", b)], [("qT", m % 2)])
                    mm_fm("m_wq", None, hp * 256, 256, ntok, qcons)

                    def kcons(m, mw, b):
                        ACT(kT[m % 2][:, 0:ntok], PS.f32(b)[:, 0:ntok], AF.Copy, [("ps", b)], [("kT", m % 2)], scale=MDK ** -0.5)
                    mm_fm("m_wk", None, hp * 256, 256, ntok, kcons)
                    for hh in range(2):
                        h = hp * 2 + hh
                        qh, kh = qT[hh], kT[hh]
                        qk, kk_ = ("qT", hh), ("kT", hh)

                        def vcons(t, c0, n, b):
                            CP("act", V1[:, t, 0:MDV], PS.f32(b)[:, 0:MDV], [("ps", b)], ["V1"])
                        mm_tm("m_wv", None, h * MDV, MDV, nt, vcons)

                        def ocons(t, c0, n, b):
                            ACT(ogs[:, t, :], PS.f32(b)[:, 0:MDV], AF.Sigmoid, [("ps", b)], ["ogs"])
                        mm_tm("m_wo", None, h * MDV, MDV, nt, ocons)
                        if kind == "s":
                            P.dma("sp", C0f[:, :, 0:MDV], dr["mC"].ap()[:, h].rearrange("s k v -> k s v"), w=["C0f"])
                            CP("pool", C0f[:, :, MDV:MDV + 1], AP(n0T, h, [[128, 128], [8, 16], [1, 1]]), ["n0T"], ["C0f"])
                            CP("act", C0b[:], C0f[:], ["C0f"], ["C0b"])
                            TT("dve", QZ[:], AP(qh, 0, [[W, 128], [0, 16], [1, 128]]), seqmask[:], ALU.mult, [qk, "seqmask"], ["QZ"])
                        if kind == "p":
                            TS_ = [slice(t * 128, (t + 1) * 128) for t in range(nt)]
                            bks = []
                            for t in range(nt):
                                b = PS.get()
                                bks.append(b)
                                TR(PS.bf16(b)[:, 0:128], kh[:, TS_[t]], idb[:], [kk_, "idb"], [("ps", b)])
                            for t in range(nt):
                                TS("dve", kwL[t][:], PS.bf16(bks[t])[:, 0:128], QT[:, t, 1, h:h + 1], None, ALU.mult, None,
                                   [("ps", bks[t]), "QT"], [("kw", t)])
                            bks = []
                            for t in range(nt):
                                b = PS.get()
                                bks.append(b)
                                MM(PS.f32(b)[:, 0:128], kh[:, TS_[t]], qh[:, TS_[t]], True, True, [qk, kk_], [("ps", b)])
                            for t in range(nt):
                                TT("dve", SbL[t][:], PS.f32(bks[t])[:, 0:128], WT[:, t, h, :], ALU.mult, [("ps", bks[t]), ("WT", t)], [("Sb", t)])
                            bks = []
                            for t in range(nt):
                                b = PS.get()
                                bks.append(b)
                                MM(PS.f32(b)[:, 0:MDV + 1], SbL[t][:], V1[:, t, :], True, True, [("Sb", t), "V1"], [("ps", b)])
                            for t in range(nt):
                                CP("act", inL[t][:], PS.f32(bks[t])[:, 0:MDV + 1], [("ps", bks[t])], [("intra", t)])
                            for t in range(nt):
                                t0_ = t * 128
                                for c in range(2):
                                    r0 = 64 * c
                                    bU = PS.get()
                                    MM(PS.f32(bU)[:, 0:MDV + 1], kwL[t][r0:r0 + 64, :], V1[r0:r0 + 64, t, :], True, True, [("kw", t), "V1"], [("ps", bU)])
                                    bN = PS.get()
                                    MM(PS.f32(bN)[r0:r0 + 64, 0:MDV + 1], qh[:, t0_ + r0:t0_ + r0 + 64], Cb[:, h, :], True, True,
                                       [qk, ("Cb", h)], [("ps", bN)])
                                    STT(Cst[:, h, :], Cst[:, h, :], DEC[:, t, c, h:h + 1], PS.f32(bU)[:, 0:MDV + 1], ALU.mult, ALU.add,
                                        [("ps", bU), "DEC", ("Cst", h)], [("Cst", h)])
                                    CP("act", Cb[:, h, :], Cst[:, h, :], [("Cst", h)], [("Cb", h)])
                                    STT(inL[t][r0:r0 + 64, :], PS.f32(bN)[r0:r0 + 64, 0:MDV + 1], QT[r0:r0 + 64, t, 0, h:h + 1], inL[t][r0:r0 + 64, :],
                                        ALU.mult, ALU.add, [("ps", bN), "QT", ("intra", t)], [("intra", t)])
                            def fin_stages():
                                yield lambda t: TS("dve", scL[t][:, 7:8], inL[t][:, MDV:MDV + 1], -1.0, None, ALU.mult, None, [("intra", t)], [("sc", t)])
                                yield lambda t: TT("dve", scL[t][:, 7:8], scL[t][:, 7:8], inL[t][:, MDV:MDV + 1], ALU.max, [("intra", t), ("sc", t)], [("sc", t)])
                                yield lambda t: TT("dve", scL[t][:, 0:1], scL[t][:, 7:8], QT[:, t, 3, h:h + 1], ALU.max, [("sc", t), "QT"], [("sc", t)])
                                yield lambda t: P.op("dve", lambda e: e.reciprocal(out=scL[t][:, 1:2], in_=scL[t][:, 0:1]), [("sc", t)], [("sc", t)])
                                yield lambda t: ACT(junk[:, 0:MDV], inL[t][:, 0:MDV], AF.Square, [("intra", t)], ["junk", ("sc", t)], accum=scL[t][:, 2:3])
                                yield lambda t: TT("dve", scL[t][:, 3:4], scL[t][:, 1:2], scL[t][:, 1:2], ALU.mult, [("sc", t)], [("sc", t)])
                                yield lambda t: TT("dve", scL[t][:, 3:4], scL[t][:, 3:4], scL[t][:, 2:3], ALU.mult, [("sc", t)], [("sc", t)])
                                yield lambda t: ACT(scL[t][:, 4:5], scL[t][:, 3:4], AF.Sqrt, [("sc", t), "epsb"], [("sc", t)], scale=1.0 / MDV, bias=epsb[:, 0:1])
                                yield lambda t: P.op("dve", lambda e: e.reciprocal(out=scL[t][:, 5:6], in_=scL[t][:, 4:5]), [("sc", t)], [("sc", t)])
                                yield lambda t: TT("dve", scL[t][:, 6:7], scL[t][:, 5:6], scL[t][:, 1:2], ALU.mult, [("sc", t)], [("sc", t)])
                                yield lambda t: STT(HOL[t][:], inL[t][:, 0:MDV], scL[t][:, 6:7], ogs[:, t, :], ALU.mult, ALU.mult,
                                                    [("intra", t), ("sc", t), "ogs"], [("HO", t)])
                            for fn in fin_stages():
                                for t in range(nt):
                                    fn(t)
                            bks = []
                            for t in range(nt):
                                b = PS.get()
                                bks.append(b)
                                for j in range(2):
                                    TR(PS.bf16(b)[:, j * 128:(j + 1) * 128], HOL[t][:, j * 128:(j + 1) * 128], idb[:], [("HO", t), "idb"], [("ps", b)])
                            for t in range(nt):
                                TT("dve", YT[:, 2 * h:2 * h + 2, TS_[t]], PS.bf16(bks[t])[:, 0:256].rearrange("p (a b) -> p a b", a=2),
                                   AP(FMV, V_MNW * 16 + 2 * h, [[NVX * 16, 128], [1, 2], [0, 128]]), ALU.mult, [("ps", bks[t]), "FMV"], ["YT"])
                            continue
                        for t in range(nt):
                            t0_ = t * 128
                            b = PS.get()
                            TR(PS.bf16(b)[:, 0:128], kh[:, t0_:t0_ + 128], idb[:], [kk_, "idb"], [("ps", b)])
                            TS("dve", kw[:], PS.bf16(b)[:, 0:128], QT[:, t, 1, h:h + 1], None, ALU.mult, None, [("ps", b), "QT"], ["kw"])
                            b = PS.get()
                            MM(PS.f32(b)[:, 0:128], kh[:, t0_:t0_ + 128], qh[:, t0_:t0_ + 128], True, True, [qk, kk_], [("ps", b)])
                            TT("dve", Sb[:], PS.f32(b)[:, 0:128], WT[:, t, h, :], ALU.mult, [("ps", b), ("WT", t)], ["Sb"])
                            bI = PS.get()
                            MM(PS.f32(bI)[:, 0:MDV + 1], Sb[:], V1[:, t, :], True, True, ["Sb", "V1"], [("ps", bI)])
                            CP("act", intra[:], PS.f32(bI)[:, 0:MDV + 1], [("ps", bI)], ["intra"])
                            if kind == "p":
                                for c in range(2):
                                    r0 = 64 * c
                                    bN = PS.get()
                                    MM(PS.f32(bN)[r0:r0 + 64, 0:MDV + 1], qh[:, t0_ + r0:t0_ + r0 + 64], Cb[:, h, :], True, True,
                                       [qk, ("Cb", h)], [("ps", bN)])
                                    STT(hn[r0:r0 + 64, :], PS.f32(bN)[r0:r0 + 64, 0:MDV + 1], QT[r0:r0 + 64, t, 0, h:h + 1], intra[r0:r0 + 64, :],
                                        ALU.mult, ALU.add, [("ps", bN), "QT", "intra"], [("hn", c)])
                                    bU = PS.get()
                                    MM(PS.f32(bU)[:, 0:MDV + 1], kw[r0:r0 + 64, :], V1[r0:r0 + 64, t, :], True, True, ["kw", "V1"], [("ps", bU)])
                                    STT(Cst[:, h, :], Cst[:, h, :], DEC[:, t, c, h:h + 1], PS.f32(bU)[:, 0:MDV + 1], ALU.mult, ALU.add,
                                        [("ps", bU), "DEC", ("Cst", h)], [("Cst", h)])
                                    CP("act", Cb[:, h, :], Cst[:, h, :], [("Cst", h)], [("Cb", h)])
                                hnk = [("hn", 0), ("hn", 1)]
                            else:
                                bN = PS.get()
                                for s in range(NSEQ):
                                    MM(PS.f32(bN)[:, 0:MDV + 1], QZ[:, s, :], C0b[:, s, :], s == 0, s == NSEQ - 1, ["QZ", "C0b"], [("ps", bN)])
                                STT(hn[:], PS.f32(bN)[:, 0:MDV + 1], QT[:, t, 0, h:h + 1], intra[:], ALU.mult, ALU.add,
                                    [("ps", bN), "QT", "intra"], [("hn", 0), ("hn", 1)])
                                hnk = [("hn", 0), ("hn", 1)]
                                TT("dve", KZ[:], AP(kw, 0, [[128, 128], [0, 16], [1, 128]]), AP(rowmask, 0, [[16, 128], [1, 16], [0, 128]]),
                                   ALU.mult, ["kw", "rowmask"], ["KZ"])
                                for s in range(NSEQ):
                                    bU = PS.get()
                                    MM(PS.f32(bU)[:, 0:MDV + 1], KZ[:, s, :], V1[:, t, :], True, True, ["KZ", "V1"], [("ps", bU)])
                                    STT(C0f[:, s, :], C0f[:, s, :], DEC[:, t, s, h:h + 1], PS.f32(bU)[:, 0:MDV + 1], ALU.mult, ALU.add,
                                        [("ps", bU), "DEC", "C0f", "C0b"], ["C0f"])
                                P.dma("pool", dr["Cs"].ap()[:, h].rearrange("s k v -> k s v"), C0f[:, :, 0:MDV], r=["C0f"], is_output=True)
                                CP("pool", AP(NS, h, [[128, 128], [8, 16], [1, 1]]), C0f[:, :, MDV:MDV + 1], ["C0f"], ["NS"])
                            TS("dve", sc[:, 7:8], hn[:, MDV:MDV + 1], -1.0, None, ALU.mult, None, hnk, ["sc"])
                            TT("dve", sc[:, 7:8], sc[:, 7:8], hn[:, MDV:MDV + 1], ALU.max, hnk + ["sc"], ["sc"])
                            TT("dve", sc[:, 0:1], sc[:, 7:8], QT[:, t, 3, h:h + 1], ALU.max, ["sc", "QT"], ["sc"])
                            P.op("dve", lambda e: e.reciprocal(out=sc[:, 1:2], in_=sc[:, 0:1]), ["sc"], ["sc"])
                            ACT(intra[:, 0:MDV], hn[:, 0:MDV], AF.Square, hnk, ["intra", "sc"], accum=sc[:, 2:3])
                            TT("dve", sc[:, 3:4], sc[:, 1:2], sc[:, 1:2], ALU.mult, ["sc"], ["sc"])
                            TT("dve", sc[:, 3:4], sc[:, 3:4], sc[:, 2:3], ALU.mult, ["sc"], ["sc"])
                            ACT(sc[:, 4:5], sc[:, 3:4], AF.Sqrt, ["sc", "epsb"], ["sc"], scale=1.0 / MDV, bias=epsb[:, 0:1])
                            P.op("dve", lambda e: e.reciprocal(out=sc[:, 5:6], in_=sc[:, 4:5]), ["sc"], ["sc"])
                            TT("dve", sc[:, 6:7], sc[:, 5:6], sc[:, 1:2], ALU.mult, ["sc"], ["sc"])
                            STT(HO[:], hn[:, 0:MDV], sc[:, 6:7], ogs[:, t, :], ALU.mult, ALU.mult, hnk + ["sc", "ogs"], ["HO"])
                            b = PS.get()
                            for j in range(2):
                                TR(PS.bf16(b)[:, j * 128:(j + 1) * 128], HO[:, j * 128:(j + 1) * 128], idb[:], ["HO", "idb"], [("ps", b)])
                            TT("dve", YT[:, 2 * h:2 * h + 2, t0_:t0_ + 128], PS.bf16(b)[:, 0:256].rearrange("p (a b) -> p a b", a=2),
                               AP(FMV, V_MNW * 16 + 2 * h, [[NVX * 16, 128], [1, 2], [0, 128]]), ALU.mult, [("ps", b), "FMV"], ["YT"])
                def outcons(t, c0, n, b):
                    TT("dve", H[t][:, c0:c0 + n], PS.f32(b)[:, 0:n], H[t][:, c0:c0 + n], ALU.add, [("ps", b), ("H", t)], [("H", t)])
                mm_tm("m_wout", None, 0, D, nt, outcons, lhs_fn=lambda kc, t: YT[:, kc, t * 128:(t + 1) * 128], keys_r=["YT"])
                if kind == "s":
                    b = PS.get()
                    TR(PS.f32(b)[:, 0:128], NS[:], idf[:], ["NS", "idf"], [("ps", b)])
                    CP("dve", n0l[:], PS.f32(b)[:, 0:128], [("ps", b)], ["n0l"])
                    P.dma("pool", dr["ns"].ap(), n0l[:], r=["n0l"], is_output=True)
                elif bi == SEQ // TB - 1:
                    P.dma("pool", dr["Cp"].ap().rearrange("h k v -> k h v"), Cst[:, :, 0:MDV], r=[("Cst", h) for h in range(MH)], is_output=True)
                    b = PS.get()
                    CP("dve", hn[:, 0:MH], Cst[:, :, MDV], [("Cst", h) for h in range(MH)], [("hn", 0), ("hn", 1)])
                    TR(PS.f32(b)[0:MH, 0:128], hn[:, 0:MH], idf[:], [("hn", 0), ("hn", 1), "idf"], [("ps", b)])
                    CP("dve", intra[0:MH, 0:128], PS.f32(b)[0:MH, 0:128], [("ps", b)], ["intra"])
                    P.dma("pool", dr["np_"].ap(), intra[0:MH, 0:128], r=["intra"], is_output=True)
                P.barrier()

        def rwkv(kind, bi, nt):
            ntok = nt * 128
            W = ntok
            W2 = min(ntok, 256)
            nhalf = ntok // W2
            L = 64 if kind == "p" else 8
            sfx = kind
            nsq = 5 if kind == "p" else 2
            norm_to_xt(V_NMIX1, nt)
            last_block = (kind == "s") or (bi == SEQ // TB - 1)
            if last_block:
                with ExitStack() as st0:
                    gt = sb("rk_gt", [128, D], F32, st0)
                    yy = sb("rk_yy", [128, D], F32, st0)
                    P.dma("sp", gt[:], AP(dr["nmix1"], 0, [[0, 128], [1, D]]), w=["rk_gt"])
                    rms_stats(nt - 1)
                    STT(yy[:], H[nt - 1][:], nst[:, 2:3], gt[:], ALU.mult, ALU.mult, [("H", nt - 1), "nst", "rk_gt"], ["rk_yy"])
                    if kind == "p":
                        P.dma("pool", dr["shp"].ap(), yy[127:128, :], r=["rk_yy"], is_output=True)
                    else:
                        P.dma("pool", dr["shs"].ap(), AP(yy, 7 * D, [[8 * D, 16], [1, D]]), r=["rk_yy"], is_output=True)
                    P.barrier()
            with ExitStack() as st:
                YT = sb("rk_YT", [128, KC, W], BF16, st)
                w2b = sb("rk_w2b", [96, D], BF16, st)
                a2b = sb("rk_a2b", [96, D], BF16, st)
                g2b = sb("rk_g2b", [128, 2, D], BF16, st)
                h1w = sb("rk_h1w", [96, W], BF16, st)
                h1a = sb("rk_h1a", [96, W], BF16, st)
                h1g = sb("rk_h1g", [128, 2, W], BF16, st)
                fm = {nm: sb("rk_f_" + nm, [128, W2], F32, st) for nm in
                      ("a", "gg", "lwn", "t1", "t2", "bonus", "e1", "e2", "e3")}
                fm["cum"] = fm["a"]
                fm["e4"] = fm["e3"]
                fmW2 = [{nm: sb("rk_fw_" + nm, [128, W], F32, st) for nm in ("r", "k", "v")} for _ in range(2)]
                rst = sb("rk_rst", [128, W2], F32, st)
                mu2 = sb("rk_mu2", [128, 256], F32, st)
                msl = sb("rk_msl", [128, 128], F32, st)
                CHD = BF16 if cfg.get("ch_bf16", True) else F32
                NTL = W2 // 128
                NU = 2 * NTL
                TM4 = [sb("rk_TM4", [128, 4, 128], CHD, st) for _ in range(NTL)]
                RtZ = [sb("rk_RtZ", [128, 2, 128], CHD, st) for _ in range(NTL)]
                VZ = [sb("rk_VZ", [128, 2, 128], CHD, st) for _ in range(NTL)]
                UZ = [sb("rk_UZ", [128, 2, 128], CHD, st) for _ in range(NTL)]
                M1T = [sb("rk_M1T", [128, 128], CHD, st) for _ in range(NTL)]
                Zs = [sb("rk_Zs", [128, 2, 64], F32, st) for _ in range(NTL)]
                NG4 = [sb("rk_NG4", [128, 2, 256], CHD, st) for _ in range(NU)]
                PQ = [[sb("rk_PQ", [128, 2, 128], CHD, st) for _ in range(2)] for _ in range(NU)]
                Tb = [[sb("rk_T", [128, 128], CHD, st) for _ in range(2)] for _ in range(NU)]
                AVs = [sb("rk_AVs", [128, 64], CHD, st) for _ in range(NU)]
                STDb = sb("rk_STDb", [128, NG, 128], CHD, st)
                fmc = {nm: sb("rk_fc_" + nm, [128, W2], CHD, st) for nm in ("Bt", "Kt", "Bh", "Kh", "vb")}
                ARc = sb("rk_ARc", [128, 2, W2], CHD, st)
                Ytm2 = sb("rk_Ytm2", [128, NTL, 128], F32, st)
                ysq2 = sb("rk_ysq2", [128, NTL, 128], F32, st)
                gst = sb("rk_gst", [128, 32], F32, st)
                yfm = sb("rk_yfm", [128, 128], F32, st)
                P.dma("sp", mu2[:], dr["c_mu2_" + sfx].ap(), w=["mu2"])
                P.dma("sp", msl[:], dr["c_msl_" + sfx].ap(), w=["msl"])
                if kind == "p":
                    P.dma("sp", rst[:], dr["c_rst64"].ap()[:, 0:W2], w=["rst"])
                else:
                    P.dma("sp", rst[:], dr["c_rst8"].ap()[:, 0:W2], w=["rst"])
                for ztl, zk in ((UZ, "UZ"), (VZ, "VZ"), (RtZ, "RtZ")):
                    for tl_, zt in enumerate(ztl):
                        MSET("pool", zt[:], 0.0, [(zk, tl_)])
                if kind == "p":
                    CP("act", STDb[:], STD[:].rearrange("p g j v -> p g (j v)"), [("STD", g_) for g_ in range(NG)], [("STDb", g_) for g_ in range(NG)])
                zkeys = {}
                if kind == "s":
                    XP = sb("rk_XP", [128, KC, 128], BF16, st)
                    seqm = sb("rk_seqm", [128, 16, 128], F32, st)
                    rowm = sb("rk_rowm", [128, 16], F32, st)
                    M1TZ = sb("rk_M1TZ", [128, 16, 128], CHD, st)
                    S0Db = sb("rk_S0Db", [128, 16, 128], CHD, st)
                    RtZs = M1TZ
                    BhZ = sb("rk_BhZ", [128, 16, 128], CHD, st)
                    KhZ = sb("rk_KhZ", [128, 16, 128], CHD, st)
                    S0l = H[1][0:64, :].rearrange("p (s k) -> p s k", s=16)
                    S0D = H[2][:, :].rearrange("p (s j v) -> p s j v", s=16, j=2)
                    Scomp = H[3][:, 0:1024].rearrange("p (s v) -> p s v", s=16)
                    P.dma("sp", seqm[:], dr["c_seqmask_s"].ap(), w=["seqm"])
                    P.dma("sp", rowm[:], dr["c_rowmask_s"].ap(), w=["rowm"])
                    MSET("pool", S0D[:], 0.0, ["S0D"])
                    si = wctr["s"] % NWS
                    wctr["s"] += 1
                    shl = wst[si][0:16, 0:D]
                    P.dma("sp", shl, dr["rsh"].ap(), w=[("wst", si)])
                    b = PS.get()
                    for kc in range(KC):
                        TR(PS.f32(b)[:, kc * 16:(kc + 1) * 16], shl[:, kc * 128:(kc + 1) * 128], idf[0:16, 0:16], [("wst", si), "idf"], [("ps", b)])
                    CP("dve", AP(XP, 0, [[KC * 128, 128], [128, KC], [8, 16]]), PS.f32(b)[:, 0:256].rearrange("p (a b) -> p a b", a=KC),
                       [("ps", b)], ["XP"])
                    for kc in range(KC):
                        CP("pool", AP(XP, kc * 128 + 1, [[KC * 128, 128], [8, 16], [1, 7]]),
                           AP(XT, kc * (TB + 1) + 1, [[KC * (TB + 1), 128], [8, 16], [1, 7]]), ["XT"], ["XP"])
                    cur_fn = lambda kc, c0, n: XT[:, kc, 1 + c0:1 + c0 + n]
                    prev_fn = lambda kc, c0, n: XP[:, kc, c0:c0 + n]
                    xkeys = ["XT", "XP"]
                else:
                    CP("pool", XT[:, :, 0:1], xlast[:].rearrange("p (a b) -> p a b", b=1), ["xlast"], ["XT"])
                    CP("pool", xlast[:].rearrange("p (a b) -> p a b", b=1), XT[:, :, ntok:ntok + 1], ["XT"], ["xlast"])
                    cur_fn = lambda kc, c0, n: XT[:, kc, 1 + c0:1 + c0 + n]
                    prev_fn = lambda kc, c0, n: XT[:, kc, c0:c0 + n]
                    xkeys = ["XT"]

                def rhs2(c0, n):
                    return lambda vi, kc: (cur_fn if vi == 0 else prev_fn)(kc, c0, n)

                def scl(c):
                    return [(V_1MU0 + c) * 16, (V_MU0 + c) * 16]

                def wload_to(dst, name, r0, nrows):
                    src, np_, kcn = wsrc(name, None, r0, nrows, 0, D)
                    wv, key = wpanel(src, np_, kcn, D if kcn == 1 else D)
                    return wv, key
                si = wctr["s"] % NWS
                wctr["s"] += 1
                P.dma("sp", wst[si][0:96, 0:D], dr["r_w2"].ap(), w=[("wst", si)])
                CP("pool", w2b[:], wst[si][0:96, 0:D], [("wst", si)], ["w2b"])
                si = wctr["s"] % NWS
                wctr["s"] += 1
                P.dma("sp", wst[si][0:96, 0:D], dr["r_a2"].ap(), w=[("wst", si)])
                CP("pool", a2b[:], wst[si][0:96, 0:D], [("wst", si)], ["a2b"])
                si = wctr["s"] % NWS
                wctr["s"] += 1
                P.dma("sp", wst[si][:, 0:2 * D].rearrange("p (k n) -> p k n", k=2), dr["r_g2"].ap().rearrange("(k p) n -> p k n", p=128), w=[("wst", si)])
                CP("pool", g2b[:], wst[si][:, 0:2 * D].rearrange("p (k n) -> p k n", k=2), [("wst", si)], ["g2b"])

                def c_h1w(m, mw, b):
                    ACT(h1w[:, 0:ntok], PS.f32(b)[0:96, 0:ntok], AF.Tanh, [("ps", b)], ["h1w"])
                mm_fm("r_w1", None, 0, 96, ntok, c_h1w, rhs_fn=rhs2(0, ntok), scales=scl(1), keys_r=xkeys)

                def c_h1a(m, mw, b):
                    CP("act", h1a[:, 0:ntok], PS.f32(b)[0:96, 0:ntok], [("ps", b)], ["h1a"])
                mm_fm("r_a1", None, 0, 96, ntok, c_h1a, rhs_fn=rhs2(0, ntok), scales=scl(4), keys_r=xkeys)

                def c_h1g(m, mw, b):
                    ACT(h1g[:, m, 0:ntok], PS.f32(b)[:, 0:ntok], AF.Sigmoid, [("ps", b)], ["h1g"])
                mm_fm("r_g1", None, 0, 256, ntok, c_h1g, rhs_fn=rhs2(0, ntok), scales=scl(5), keys_r=xkeys)

                for g in range(NG):
                    gc = slice(g * 128, (g + 1) * 128)
                    def proj_slice(gg, nm):
                        wn, c = {"r": ("r_wr", 0), "k": ("r_wk", 2), "v": ("r_wv", 3)}[nm]
                        par_ = gg % 2
                        src, np_, kcn = wsrc(wn, None, 0, D, gg * 128, 128)
                        wv, wkey = wpanel(src, np_, kcn, 128, scl(c))
                        b_ = PS.get(hold=True)
                        i = 0
                        for vi in range(2):
                            for kc in range(KC):
                                MM(PS.f32(b_)[:, 0:W], wv[vi][:, kc, :], (cur_fn if vi == 0 else prev_fn)(kc, 0, W),
                                   i == 0, i == 2 * KC - 1, [wkey] + xkeys, [("ps", b_)])
                                i += 1

                        def evac():
                            PS.release(b_)
                            CP("act" if nm != "k" else "dve", fmW2[par_][nm][:], PS.f32(b_)[:, 0:W], [("ps", b_)], ["%s%d" % (nm, par_)])
                        return evac
                    if g == 0:
                        for nm in ("r", "k", "v"):
                            proj_slice(0, nm)()
                    par = g % 2
                    kR, kK, kV = "r%d" % par, "k%d" % par, "v%d" % par
                    fmW = fmW2[par]
                    look = (g + 1 < NG)
                    for hb in range(nhalf):
                        c0 = hb * W2
                        fm["r"] = fmW["r"][:, c0:c0 + W2]
                        fm["k"] = fmW["k"][:, c0:c0 + W2]
                        fm["v"] = fmW["v"][:, c0:c0 + W2]
                        b = PS.get()
                        MM(PS.f32(b)[:, 0:W2], w2b[:, gc], h1w[:, c0:c0 + W2], True, True, ["w2b", "h1w"], [("ps", b)])
                        ACT(fm["lwn"][:], PS.f32(b)[:, 0:W2], AF.Sigmoid, [("ps", b), "FMV"], ["lwn"], bias=fmv(V_W0, g))
                        b = PS.get()
                        MM(PS.f32(b)[:, 0:W2], a2b[:, gc], h1a[:, c0:c0 + W2], True, True, ["a2b", "h1a"], [("ps", b)])
                        ACT(fm["a"][:], PS.f32(b)[:, 0:W2], AF.Sigmoid, [("ps", b), "FMV"], ["a"], bias=fmv(V_A0, g))
                        b = PS.get()
                        for j in range(2):
                            MM(PS.f32(b)[:, 0:W2], g2b[:, j, gc], h1g[:, j, c0:c0 + W2], j == 0, j == 1, ["g2b", "h1g"], [("ps", b)])
                        CP("act", fm["gg"][:], PS.f32(b)[:, 0:W2], [("ps", b)], ["gg"])
                        TS("dve", fm["t1"][:], fm["k"], fmv(V_KK, g), None, ALU.mult, None, [kK, "FMV"], ["t1"])
                        ACT(fm["t2"][:], fm["k"], AF.Square, [kK, "FMV"], ["t2"], scale=fmv(V_KK, g))
                        b = PS.get()
                        MM(PS.f32(b)[:, 0:W2], bonesf[:], fm["t2"][:], True, True, ["bonesf", "t2"], [("ps", b)])
                        evacs = []
                        if look and hb == nhalf - 1:
                            evacs.append(proj_slice(g + 1, "r"))
                        ACT(fm["t2"][:], PS.f32(b)[:, 0:W2], AF.Ln, [("ps", b), "epsb"], ["t2"], bias=epsb[:, 3:4])
                        ACT(fm["t2"][:], fm["t2"][:], AF.Exp, ["t2"], ["t2"], scale=-0.5)
                        TT("pool", fm["t1"][:], fm["t1"][:], fm["t2"][:], ALU.mult, ["t1", "t2"], ["t1"])
                        TS("dve", fm["t2"][:], fm["a"][:], fmv(V_KA, g), fmv(V_1KA, g), ALU.mult, ALU.add, ["a", "FMV"], ["t2"])
                        TT("pool", fm["k"], fm["k"], fm["t2"][:], ALU.mult, [kK, "t2"], [kK])
                        TT("pool", fm["t2"][:], fm["t1"][:], fm["a"][:], ALU.mult, ["t1", "a"], ["t2"])
                        STT(fm["e3"][:], fm["r"], fmv(V_RK, g), fm["k"], ALU.mult, ALU.mult, [kR, kK, "FMV"], ["e3"])
                        b = PS.get()
                        MM(PS.f32(b)[:, 0:W2], bonesf[:], fm["e3"][:], True, True, ["bonesf", "e3"], [("ps", b)])
                        if look and hb == nhalf - 1:
                            evacs.append(proj_slice(g + 1, "k"))
                        TT("dve", fm["bonus"][:], PS.f32(b)[:, 0:W2], fm["v"], ALU.mult, [("ps", b), kV], ["bonus"])
                        P.op("dve", lambda e: e.tensor_tensor_scan(out=fm["cum"][:], data0=rst[:], data1=fm["lwn"][:], initial=0.0,
                                                                   op0=ALU.mult, op1=ALU.add), ["rst", "lwn"], ["a"])
                        ACT(fm["e1"][:], fm["cum"][:], AF.Exp, ["a"], ["e1"], scale=-EXPM05)
                        ACT(fm["e2"][:], fm["cum"][:], AF.Exp, ["a"], ["e2"], scale=EXPM05)
                        TT("pool", fm["e3"][:], fm["cum"][:], fm["lwn"][:], ALU.subtract, ["a", "lwn"], ["e3"])
                        ACT(fm["e3"][:], fm["e3"][:], AF.Exp, ["e3"], ["e3"], scale=-EXPM05)
                        STT(ARc[:, 0, :], fm["t1"][:], -1.0, fm["e3"][:], ALU.mult, ALU.mult, ["t1", "e3", "AR"], ["AR"])
                        nch2 = W2 // L
                        TT("pool", fm["lwn"][:].rearrange("p (c l) -> p c l", l=L), AP(fm["cum"], L - 1, [[W2, 128], [L, nch2], [0, L]]),
                           fm["cum"][:].rearrange("p (c l) -> p c l", l=L), ALU.subtract, ["a", "lwn"], ["lwn"])
                        ACT(fm["e4"][:], fm["lwn"][:], AF.Exp, ["lwn"], ["e3"], scale=-EXPM05)
                        TT("pool", ARc[:, 1, :], fm["r"], fm["e1"][:], ALU.mult, [kR, "e1"], ["AR"])
                        TT("pool", fmc["Bt"][:], fm["t2"][:], fm["e2"][:], ALU.mult, ["t2", "e2"], ["Bt"])
                        TT("dve", fmc["Kt"][:], fm["k"], fm["e2"][:], ALU.mult, [kK, "e2"], ["Kt"])
                        TT("pool", fmc["Bh"][:], fm["t2"][:], fm["e4"][:], ALU.mult, ["t2", "e3"], ["Bh"])
                        TT("dve", fmc["Kh"][:], fm["k"], fm["e4"][:], ALU.mult, [kK, "e3"], ["Kh"])
                        CP("act", fmc["vb"][:], fm["v"], [kV], ["vb"])
                        if look and hb == nhalf - 1:
                            evacs.append(proj_slice(g + 1, "v"))
                        for ev_ in evacs:
                            ev_()
                        if kind == "s":
                            for j in range(2):
                                P.dma("sp", S0l[:, :, j * 64:(j + 1) * 64],
                                      dr["rS"].ap()[:, 2 * g + j].rearrange("s v k -> v s k"), w=["S0l"])
                            for s8 in range(2):
                                b = PS.get()
                                for ss_ in range(8):
                                    s = s8 * 8 + ss_
                                    TR(PS.f32(b)[:, ss_ * 64:(ss_ + 1) * 64], S0l[:, s, :], idf[0:64, 0:64], ["S0l", "idf"], [("ps", b)])
                                for j in range(2):
                                    pb = 64 * j
                                    CP("dve", S0D[pb:pb + 64, s8 * 8:s8 * 8 + 8, j, :],
                                       PS.f32(b)[pb:pb + 64, 0:512].rearrange("p (a b) -> p a b", a=8), [("ps", b)], ["S0D"])
                        ntl = W2 // 128
                        units = [(tl, j) for tl in range(ntl) for j in range(2)]
                        bYs = []
                        for tl in range(ntl):
                            tc0 = tl * 128
                            tsl = slice(tc0, tc0 + 128)
                            b = PS.get()
                            pv = PS.bf16(b) if CHD == BF16 else PS.f32(b)
                            idc = idb if CHD == BF16 else idf
                            for qi, src_ in enumerate((fmc["Bh"][:, tsl], fmc["Kh"][:, tsl], ARc[:, 0, tsl], fmc["vb"][:, tsl])):
                                TR(pv[:, qi * 128:(qi + 1) * 128], src_, idc[:], ["Bh", "Kh", "AR", "vb", "idb", "idf"], [("ps", b)])
                            CP("act", TM4[tl][:].rearrange("p a b -> p (a b)"), pv[:, 0:512], [("ps", b)], [("TM4", tl)])
                            if kind == "p":
                                CP("pool", AP(RtZ[tl], 0, [[256, 128], [128 + 64, 2], [1, 64]]),
                                   AP(ARc, W2 + tc0, [[2 * W2, 128], [64, 2], [1, 64]]), ["AR"], [("RtZ", tl)])
                                for c in range(2):
                                    CP("dve", VZ[tl][64 * c:64 * c + 64, c, :], TM4[tl][64 * c:64 * c + 64, 3, :], [("TM4", tl)], [("VZ", tl)])
                            bYs.append(PS.get(hold=True))
                        Pc, Qc, Tc = {}, {}, {}
                        for ui, (tl, j) in enumerate(units):
                            tc0 = tl * 128
                            tsl = slice(tc0, tc0 + 128)
                            pb = 64 * j
                            arv = ARc[pb:pb + 64, :, tsl]
                            b12 = PS.get()
                            MM(PS.f32(b12)[:, 0:256], fmc["Bt"][pb:pb + 64, tsl], arv, True, True, ["Bt", "AR"], [("ps", b12)], skip=True)
                            MM(PS.f32(b12)[:, 256:512], fmc["Kt"][pb:pb + 64, tsl], arv, False, True, ["Kt", "AR"], [("ps", b12)], skip=True)
                            b3 = PS.get()
                            MM(PS.f32(b3)[:, 0:128], ARc[pb:pb + 64, 0, tsl], fmc["Bt"][pb:pb + 64, tsl], True, True, ["AR", "Bt"], [("ps", b3)])
                            TT("dve", NG4[ui][:], PS.f32(b12)[:, 0:512].rearrange("p (a b) -> p a b", a=2),
                               AP(mu2, 0, [[256, 128], [0, 2], [1, 256]]), ALU.mult, [("ps", b12), "mu2"], [("NG4", ui)])
                            TT("dve", PQ[ui][0][:, 0, :], PS.f32(b3)[:, 0:128], msl[:], ALU.mult, [("ps", b3), "msl"], [("PQ", ui, 0)])
                            TT("pool", Tb[ui][0][:], NG4[ui][:, 0, 0:128], idc[:], ALU.add, [("NG4", ui), "idb", "idf"], [("T", ui, 0)])
                            Pc[ui] = (PQ[ui][0][:, 0, :], ("PQ", ui, 0))
                            Qc[ui] = (NG4[ui][:, 0, 0:128], ("NG4", ui))
                            Tc[ui] = 0
                        for k_ in range(nsq + 1):
                            nxt = (k_ + 1) % 2
                            do_sq = k_ < nsq
                            last_sq = (k_ == nsq - 1)
                            do_t = k_ >= 1
                            bpq = {}
                            for ui in range(len(units)):
                                bp = PS.get()
                                bpq[ui] = bp
                                if do_sq:
                                    MM(PS.f32(bp)[:, 0:128], Qc[ui][0], Pc[ui][0], True, True, [Qc[ui][1], Pc[ui][1]], [("ps", bp)], skip=True)
                                    if not last_sq:
                                        MM(PS.f32(bp)[:, 128:256], Pc[ui][0], Qc[ui][0], True, True, [Qc[ui][1], Pc[ui][1]], [("ps", bp)], skip=True)
                                if do_t:
                                    MM(PS.f32(bp)[:, 256:384], Pc[ui][0], Tb[ui][Tc[ui]][:], True, True, [Pc[ui][1], ("T", ui, Tc[ui])], [("ps", bp)], skip=True)
                            for ui in range(len(units)):
                                bp = bpq[ui]
                                eng = "act" if ui % 2 == 0 else "dve"
                                if do_t:
                                    tn = 1 - Tc[ui]
                                    TT("dve", Tb[ui][tn][:], PS.f32(bp)[:, 256:384], Tb[ui][Tc[ui]][:], ALU.add, [("ps", bp), ("T", ui, Tc[ui])], [("T", ui, tn)])
                                    Tc[ui] = tn
                                if do_sq:
                                    if not last_sq:
                                        CP(eng, PQ[ui][nxt][:].rearrange("p a b -> p (a b)"), PS.f32(bp)[:, 0:256], [("ps", bp)], [("PQ", ui, nxt)])
                                    else:
                                        CP(eng, PQ[ui][nxt][:, 0, :], PS.f32(bp)[:, 0:128], [("ps", bp)], [("PQ", ui, nxt)])
                                    Pc[ui] = (PQ[ui][nxt][:, 0, :], ("PQ", ui, nxt))
                                    Qc[ui] = (PQ[ui][nxt][:, 1, :], ("PQ", ui, nxt))
                        bms, bas = {}, {}
                        for ui, (tl, j) in enumerate(units):
                            pb = 64 * j
                            TTj, tk = Tb[ui][Tc[ui]], ("T", ui, Tc[ui])
                            bm = PS.get()
                            bms[ui] = bm
                            MM(PS.f32(bm)[:, 0:128], TM4[tl][:, 2, :], TTj[:], True, True, [("TM4", tl), tk], [("ps", bm)], skip=True)
                            MM(PS.f32(bm)[:, 128:192], NG4[ui][:, 1, 0:128], TM4[tl][:, 3, pb:pb + 64], True, True, [("NG4", ui), ("TM4", tl)], [("ps", bm)], skip=True)
                        for ui, (tl, j) in enumerate(units):
                            pb = 64 * j
                            bm = bms[ui]
                            CP("act", M1T[tl][pb:pb + 64, :], PS.f32(bm)[pb:pb + 64, 0:128], [("ps", bm)], [("M1T", tl)])
                            CP("dve", AVs[ui][:], PS.f32(bm)[:, 128:192], [("ps", bm)], [("AVs", ui)])
                        for ui, (tl, j) in enumerate(units):
                            pb = 64 * j
                            TTj, tk = Tb[ui][Tc[ui]], ("T", ui, Tc[ui])
                            bz = PS.get()
                            bas[ui] = bz
                            MM(PS.f32(bz)[:, 0:64], TTj[:], AVs[ui][:], True, True, [tk, ("AVs", ui)], [("ps", bz)])
                            MM(PS.f32(bYs[tl])[:, pb:pb + 64], NG4[ui][:, 1, 128:256], TM4[tl][:, 3, pb:pb + 64], j == 0, False,
                               [("NG4", ui), ("TM4", tl)], [("ps", bYs[tl])], skip=True)
                        for ui, (tl, j) in enumerate(units):
                            CP("act" if ui % 2 else "dve", Zs[tl][:, j, :], PS.f32(bas[ui])[:, 0:64], [("ps", bas[ui])], [("Zs", tl)])
                        for tl in range(ntl):
                            tc0 = tl * 128
                            tsl = slice(tc0, tc0 + 128)
                            bY = bYs[tl]
                            u0 = 2 * tl
                            if kind == "p":
                                for c in range(2):
                                    r0 = 64 * c
                                    bu = PS.get()
                                    MM(PS.f32(bu)[:, 0:128], M1T[tl][:], STDb[:, g, :], True, True, [("M1T", tl), ("STDb", g)], [("ps", bu)])
                                    TT("dve", UZ[tl][r0:r0 + 64, c, :], PS.f32(bu)[r0:r0 + 64, 0:128], Zs[tl][r0:r0 + 64, :, :].rearrange("p a b -> p (a b)"),
                                       ALU.add, [("ps", bu), ("Zs", tl)], [("UZ", tl)])
                                    MM(PS.f32(bY)[:, 0:128], RtZ[tl][:, c, :], STDb[:, g, :], False, False,
                                       [("RtZ", tl), ("STDb", g)], [("ps", bY)], skip=True)
                                    bs = PS.get()
                                    MM(PS.f32(bs)[:, 0:128], TM4[tl][:, 0, :], UZ[tl][:, c, :], True, False, [("TM4", tl), ("UZ", tl)], [("ps", bs)])
                                    MM(PS.f32(bs)[:, 0:128], TM4[tl][:, 1, :], VZ[tl][:, c, :], False, True, [("TM4", tl), ("VZ", tl)], [("ps", bs)])
                                    for j in range(2):
                                        pb = 64 * j
                                        wl_ap = fm["e1"][pb:pb + 64, tc0 + r0 + 63:tc0 + r0 + 64]
                                        STT(STDb[pb:pb + 64, g, pb:pb + 64], STD[pb:pb + 64, g, j, :], wl_ap, PS.f32(bs)[pb:pb + 64, pb:pb + 64],
                                            ALU.mult, ALU.add, [("ps", bs), "e1", ("STD", g)], [("STDb", g)])
                                    for j in range(2):
                                        pb = 64 * j
                                        wl_ap = fm["e1"][pb:pb + 64, tc0 + r0 + 63:tc0 + r0 + 64]
                                        STT(STD[pb:pb + 64, g, j, :], STD[pb:pb + 64, g, j, :], wl_ap, PS.f32(bs)[pb:pb + 64, pb:pb + 64],
                                            ALU.mult, ALU.add, [("ps", bs), "e1", ("STD", g)], [("STD", g)])
                                for j in range(2):
                                    pb = 64 * j
                                    for c in range(2):
                                        MM(PS.f32(bY)[:, pb:pb + 64], NG4[u0 + j][:, 0, 128:256], UZ[tl][:, c, pb:pb + 64], False, (j == 1 and c == 1),
                                           [("NG4", u0 + j), ("UZ", tl)], [("ps", bY)], skip=True)
                            else:
                                TT("dve", M1TZ[:], AP(M1T[tl], 0, [[128, 128], [0, 16], [1, 128]]), seqm[:], ALU.mult, [("M1T", tl), "seqm"], ["M1TZ"])
                                TT("dve", BhZ[:], AP(TM4[tl], 0, [[512, 128], [0, 16], [1, 128]]), AP(rowm, 0, [[16, 128], [1, 16], [0, 128]]),
                                   ALU.mult, [("TM4", tl), "rowm"], ["BhZ"])
                                TT("pool", KhZ[:], AP(TM4[tl], 128, [[512, 128], [0, 16], [1, 128]]), AP(rowm, 0, [[16, 128], [1, 16], [0, 128]]),
                                   ALU.mult, [("TM4", tl), "rowm"], ["KhZ"])
                                CP("act", S0Db[:], S0D[:].rearrange("p s j v -> p s (j v)"), ["S0D"], ["S0Db"])
                                bu = PS.get()
                                for s in range(NSEQ):
                                    MM(PS.f32(bu)[:, 0:128], M1TZ[:, s, :], S0Db[:, s, :], s == 0, s == NSEQ - 1, ["M1TZ", "S0Db"], [("ps", bu)])
                                TT("dve", UZ[tl][:, 0, :], PS.f32(bu)[:, 0:128], Zs[tl][:].rearrange("p a b -> p (a b)"), ALU.add, [("ps", bu), ("Zs", tl)], [("UZ", tl)])
                                TT("pool", RtZs[:], AP(ARc, W2 + tc0, [[2 * W2, 128], [0, 16], [1, 128]]), seqm[:], ALU.mult, ["AR", "seqm"], ["M1TZ"])
                                for s in range(NSEQ):
                                    MM(PS.f32(bY)[:, 0:128], RtZs[:, s, :], S0Db[:, s, :], False, False, ["M1TZ", "S0Db"], [("ps", bY)], skip=True)
                                for j in range(2):
                                    pb = 64 * j
                                    MM(PS.f32(bY)[:, pb:pb + 64], NG4[u0 + j][:, 0, 128:256], UZ[tl][:, 0, pb:pb + 64], False, j == 1,
                                       [("NG4", u0 + j), ("UZ", tl)], [("ps", bY)], skip=True)
                                for s in range(NSEQ):
                                    bs = PS.get()
                                    MM(PS.f32(bs)[:, 0:128], BhZ[:, s, :], UZ[tl][:, 0, :], True, False, ["BhZ", ("UZ", tl)], [("ps", bs)])
                                    MM(PS.f32(bs)[:, 0:128], KhZ[:, s, :], TM4[tl][:, 3, :], False, True, ["KhZ", ("TM4", tl)], [("ps", bs)])
                                    for j in range(2):
                                        pb = 64 * j
                                        wl_ap = fm["e1"][pb:pb + 64, s * 8 + 7:s * 8 + 8]
                                        STT(S0D[pb:pb + 64, s, j, :], S0D[pb:pb + 64, s, j, :], wl_ap, PS.f32(bs)[pb:pb + 64, pb:pb + 64],
                                            ALU.mult, ALU.add, [("ps", bs), "e1", "S0D", "S0Db"], ["S0D"])
                                TT("pool", Scomp[:], S0D[:, :, 0, :], S0D[:, :, 1, :], ALU.add, ["S0D"], ["Scomp"])
                                for s4 in range(4):
                                    b = PS.get()
                                    for ss_ in range(4):
                                        s = s4 * 4 + ss_
                                        TR(PS.f32(b)[0:64, ss_ * 128:(ss_ + 1) * 128], Scomp[:, s, :], idf[:], ["Scomp", "idf"], [("ps", b)])
                                    CP("act", S0l[:, s4 * 4:s4 * 4 + 4, :].rearrange("p a b -> p (a b)"), PS.f32(b)[0:64, 0:512], [("ps", b)], ["S0l"])
                                for j in range(2):
                                    P.dma("pool", dr["Ss"].ap()[:, 2 * g + j].rearrange("s v k -> v s k"),
                                          S0l[:, :, j * 64:(j + 1) * 64], r=["S0l"], is_output=True)
                            PS.release(bY)
                            CP("act", Ytm2[:, tl, :], PS.f32(bY)[:, 0:128], [("ps", bY)], ["Ytm"])
                        nq = 2 * ntl
                        GW = 32
                        yv = Ytm2[:, 0:ntl, :].rearrange("p t (a b) -> p (t a) b", a=2)
                        P.op("dve", lambda e: e.tensor_reduce(out=gst[:, 0:nq], in_=yv, axis=AX.X, op=ALU.add), ["Ytm"], ["gst"])
                        ACT(ysq2[:, 0:ntl, :], Ytm2[:, 0:ntl, :], AF.Square, ["Ytm"], ["ysq2"])
                        P.op("dve", lambda e: e.tensor_reduce(out=gst[:, nq:2 * nq], in_=ysq2[:, 0:ntl, :].rearrange("p t (a b) -> p (t a) b", a=2),
                                                              axis=AX.X, op=ALU.add), ["ysq2"], ["gst"])
                        TS("dve", gst[:, 2 * nq:3 * nq], gst[:, 0:nq], 1.0 / RN, None, ALU.mult, None, ["gst"], ["gst"])
                        TT("dve", gst[:, 3 * nq:4 * nq], gst[:, 2 * nq:3 * nq], gst[:, 2 * nq:3 * nq], ALU.mult, ["gst"], ["gst"])
                        STT(gst[:, 4 * nq:5 * nq], gst[:, nq:2 * nq], 1.0 / RN, gst[:, 3 * nq:4 * nq], ALU.mult, ALU.subtract, ["gst"], ["gst"])
                        TS("dve", gst[:, 4 * nq:5 * nq], gst[:, 4 * nq:5 * nq], 0.0, None, ALU.max, None, ["gst"], ["gst"])
                        ACT(gst[:, 4 * nq:5 * nq], gst[:, 4 * nq:5 * nq], AF.Sqrt, ["gst", "epsb"], ["gst"], bias=epsb[:, 2:3])
                        P.op("dve", lambda e: e.reciprocal(out=gst[:, 5 * nq:6 * nq], in_=gst[:, 4 * nq:5 * nq]), ["gst"], ["gst"])
                        TT("dve", yv, yv, AP(gst, 2 * nq, [[GW, 128], [1, nq], [0, 64]]), ALU.subtract, ["Ytm", "gst"], ["Ytm"])
                        TT("dve", yv, yv, AP(gst, 5 * nq, [[GW, 128], [1, nq], [0, 64]]), ALU.mult, ["Ytm", "gst"], ["Ytm"])
                        for tl in range(ntl):
                            tc0 = tl * 128
                            tsl = slice(tc0, tc0 + 128)
                            b = PS.get()
                            TR(PS.f32(b)[:, 0:128], Ytm2[:, tl, :], idf[:], ["Ytm", "idf"], [("ps", b)])
                            TS("dve", yfm[:], PS.f32(b)[:, 0:128], fmv(V_LNW, g), fmv(V_LNB, g), ALU.mult, ALU.add, [("ps", b), "FMV"], ["yfm"])
                            TT("pool", yfm[:], yfm[:], fm["bonus"][:, tsl], ALU.add, ["yfm", "bonus"], ["yfm"])
                            TT("pool", YT[:, g, c0 + tc0:c0 + tc0 + 128], yfm[:], fm["gg"][:, tsl], ALU.mult, ["yfm", "gg"], ["YT"])
                def outcons(t, c0_, n, b):
                    TT("dve", H[t][:, c0_:c0_ + n], PS.f32(b)[:, 0:n], H[t][:, c0_:c0_ + n], ALU.add, [("ps", b), ("H", t)], [("H", t)])
                mm_tm("r_wo", None, 0, D, nt, outcons, lhs_fn=lambda kc, t: YT[:, kc, t * 128:(t + 1) * 128], keys_r=["YT"])
                if kind == "p" and bi == SEQ // TB - 1:
                    Sc = fm["t1"]
                    So = fm["t2"]
                    for g in range(NG):
                        TT("pool", Sc[:, 0:64], STD[:, g, 0, :], STD[:, g, 1, :], ALU.add, [("STD", g)], ["t1"])
                        b = PS.get()
                        TR(PS.f32(b)[0:64, 0:128], Sc[:, 0:64], idf[:], ["t1", "idf"], [("ps", b)])
                        CP("act", So[0:64, 0:128], PS.f32(b)[0:64, 0:128], [("ps", b)], ["t2"])
                        P.dma("pool", dr["Sp"].ap()[2 * g:2 * g + 2].rearrange("j v k -> v j k"),
                              So[0:64, 0:128].rearrange("v (j k) -> v j k", j=2), r=["t2"], is_output=True)
                P.barrier()

        for bidx_, (kind, bi) in enumerate(blocks):
            nt = TB // 128 if kind == "p" else 1
            wctr["p"] = 0
            wctr["mode"] = "store" if bidx_ == 0 else "load"
            if cfg.get("no_wcache"):
                wctr["mode"] = "none"
            load_block(kind, bi, nt)
            for layer in range(2):
                if not cfg.get("skip_mixers"):
                    if layer == 0:
                        mlstm(kind, bi, nt)
                    else:
                        rwkv(kind, bi, nt)
                for t in range(nt):
                    tap("h_mix%d_%s%d_%d" % (layer, kind, bi, t), H[t][:], [128, D], [("H", t)])
                if stop == "mix%d" % layer:
                    break
                ffn(layer, nt)
                for t in range(nt):
                    tap("h_ffn%d_%s%d_%d" % (layer, kind, bi, t), H[t][:], [128, D], [("H", t)])
                if stop == "ffn%d" % layer:
                    break
                ple(layer, kind, bi, nt)
                for t in range(nt):
                    tap("h_ple%d_%s%d_%d" % (layer, kind, bi, t), H[t][:], [128, D], [("H", t)])
                if stop == "ple%d" % layer:
                    break
            else:
                final_out(kind, bi, nt)
            P.barrier()
        P.finish()
        cfg["_stats"] = (P.n_inst, P.n_wait, dict(P.cnt))
    return nc, tap_d


def make_in_maps(inp, ncores=8):
    consts = make_consts()
    f = lambda a: np.ascontiguousarray(a, dtype=np.float32)
    vec_names = [("norm_mix", 0), ("norm_ffn", 0), ("norm_ple", 0), ("norm_mix", 1), ("norm_ffn", 1), ("norm_ple", 1)]
    vecs = [inp[n][i] for n, i in vec_names]
    vecs.append(inp["mlstm_norm_w"][0])
    for c in range(6):
        vecs.append(inp["rwkv_mu"][0, c])
    for n in ("rwkv_w0", "rwkv_a0", "rwkv_k_k", "rwkv_k_a", "rwkv_ln_w", "rwkv_ln_b"):
        vecs.append(inp[n][0])
    vecs.append(inp["rwkv_r_k"][0].reshape(-1))
    vecs = f(np.stack(vecs, 0).reshape(NV * 16, 128))
    shared = {
        "vecs": vecs, "nfin": f(inp["norm_final"].reshape(1, D)), "nmix1": f(inp["norm_mix"][1].reshape(1, D)),
        "big": f(inp["mlstm_b_igate"][0].reshape(MH, 1)), "bfg": f(inp["mlstm_b_fgate"][0].reshape(MH, 1)),
        "ffn_wg": f(inp["ffn_w_gate"]), "ffn_wu": f(inp["ffn_w_up"]), "ffn_wd": f(inp["ffn_w_down"]),
        "ple_wp": f(inp["ple_w_proj"]), "ple_wg": f(inp["ple_w_gate"]),
        "m_wq": f(inp["mlstm_w_q"][0]), "m_wk": f(inp["mlstm_w_k"][0]), "m_wv": f(inp["mlstm_w_v"][0]),
        "m_wi": f(inp["mlstm_w_igate"][0]), "m_wf": f(inp["mlstm_w_fgate"][0]),
        "m_wo": f(inp["mlstm_w_ogate"][0]), "m_wout": f(inp["mlstm_w_out"][0]),
        "r_wr": f(inp["rwkv_w_r"][0]), "r_wk": f(inp["rwkv_w_k"][0]), "r_wv": f(inp["rwkv_w_v"][0]), "r_wo": f(inp["rwkv_w_o"][0]),
        "r_w1": f(inp["rwkv_w1"][0]), "r_w2": f(inp["rwkv_w2"][0]), "r_a1": f(inp["rwkv_a1"][0]), "r_a2": f(inp["rwkv_a2"][0]),
        "r_g1": f(inp["rwkv_g1"][0]), "r_g2": f(inp["rwkv_g2"][0]),
    }
    for k, v in consts.items():
        shared["c_" + k] = f(v)
    maps = []
    for c in range(ncores):
        b = c % 4
        s0 = NSEQ * c
        m = dict(shared)
        m["xp"] = f(inp["x_prompt"][b])
        m["xs"] = f(inp["x_sample"][s0:s0 + NSEQ].reshape(128, D))
        m["pp"] = f(inp["p_prompt"][:, b])
        m["psm"] = f(inp["p_sample"][:, s0:s0 + NSEQ].reshape(2, 128, PLE))
        m["mC"] = f(inp["state_mlstm_C"][0, s0:s0 + NSEQ])
        m["mn"] = f(inp["state_mlstm_n"][0, s0:s0 + NSEQ].reshape(128, 128))
        m["mmT"] = f(inp["state_mlstm_m"][0, s0:s0 + NSEQ].T)
        m["rS"] = f(inp["state_rwkv_S"][0, s0:s0 + NSEQ])
        m["rsh"] = f(inp["state_rwkv_shift"][0, s0:s0 + NSEQ])
        maps.append(m)
    return maps


_NC_CACHE = {}


def kernel(**inputs):
    inp = {k: np.asarray(v) for k, v in inputs.items()}
    if "nc" not in _NC_CACHE:
        _NC_CACHE["nc"] = build({})[0]
    nc = _NC_CACHE["nc"]
    maps = make_in_maps(inp, 8)
    res = run_bass_kernel_spmd(nc, maps, core_ids=list(range(8)))
    R = res.results
    f = lambda a: np.ascontiguousarray(a, dtype=np.float32)
    y_prompt = f(np.stack([R[c]["yp"] for c in range(4)], 0))
    y_sample = f(np.concatenate([R[c]["ys"].reshape(NSEQ, ST_, D) for c in range(8)], 0))
    C_p = f(np.stack([R[c]["Cp"] for c in range(4)], 0)[None])
    n_p = f(np.stack([R[c]["np_"] for c in range(4)], 0)[None])
    m_p = f(np.stack([R[c]["mp"][:, 0] for c in range(4)], 0)[None])
    S_p = f(np.stack([R[c]["Sp"] for c in range(4)], 0)[None])
    sh_p = f(np.stack([R[c]["shp"][0] for c in range(4)], 0)[None])
    C_s = f(np.concatenate([R[c]["Cs"] for c in range(8)], 0)[None])
    n_s = f(np.concatenate([R[c]["ns"].reshape(NSEQ, MH, MDK) for c in range(8)], 0)[None])
    m_s = f(np.concatenate([R[c]["msT"].T for c in range(8)], 0)[None])
    S_s = f(np.concatenate([R[c]["Ss"] for c in range(8)], 0)[None])
    sh_s = f(np.concatenate([R[c]["shs"] for c in range(8)], 0)[None])
    return (y_prompt, y_sample, C_p, n_p, m_p, S_p, sh_p, C_s, n_s, m_s, S_s, sh_s)
```

```python
import math
from contextlib import ExitStack

import numpy as np
import concourse.bass as bass
import concourse.mybir as mybir
from concourse.bass_utils import run_bass_kernel_spmd

F32 = mybir.dt.float32
BF16 = mybir.dt.bfloat16
AF = mybir.ActivationFunctionType
ALU = mybir.AluOpType
AX = mybir.AxisListType

D = 2048
KC = 16
DFF = 5632
NFF = DFF // 128
PLE = 256
MH, MDK, MDV = 8, 128, 256
RH, RN = 32, 64
NG = 16
SEQ = 2048
TB = 512
NSEQ = 16
ST_ = 8
NV = 20
(V_NMIX0, V_NFFN0, V_NPLE0, V_NMIX1, V_NFFN1, V_NPLE1, V_MNW, V_MU0) = range(8)
V_W0, V_A0, V_KK, V_KA, V_LNW, V_LNB, V_RK = 13, 14, 15, 16, 17, 18, 19
V_1MU0 = 20
V_1KA = 26
NVX = 27
CH = F32
WC_TOT = 400000
WC_N = 3
EXPM05 = math.exp(-0.5)


class Prog:
    ENG = ("pe", "act", "dve", "pool", "sp")

    def __init__(self, nc, es, n_dma_sems=(12, 8, 8)):
        self.nc = nc
        self.eng = {"pe": nc.tensor, "act": nc.scalar, "dve": nc.vector,
                    "pool": nc.gpsimd, "sp": nc.sync}
        self.sem = {}
        self.cnt = {}
        for e in self.ENG:
            self.sem[e] = es.enter_context(nc.semaphore("s_" + e))
            self.cnt[e] = 0
        self.dq = {}
        for q, n in zip(("sp", "pool", "act"), n_dma_sems):
            sems = []
            for i in range(n):
                nm = "d_%s%d" % (q, i)
                self.sem[nm] = es.enter_context(nc.semaphore(nm))
                sems.append(nm)
            self.dq[q] = {"sems": sems, "i": 0}
        self.waited = {}
        self.res = {}
        self.n_inst = 0
        self.n_wait = 0
        self.out_events = []
        self.last_dma = {}
        self.relax_same = False

    def _need(self, eng, ev):
        if ev is None:
            return
        s, v = ev
        if s == eng and (eng == "pe" or self.relax_same):
            return
        k = (eng, s)
        if self.waited.get(k, 0) >= v:
            return
        self.waited[k] = v
        self.eng[eng].wait_ge(self.sem[s], v)
        self.n_wait += 1

    def _deps(self, eng, reads, writes):
        for r in reads:
            st = self.res.get(r)
            if st is not None:
                self._need(eng, st[0])
                if isinstance(r, tuple) and r[0] == "ps":
                    for s, v in list(st[1].items()):
                        if s != eng:
                            self._need(eng, (s, v))
        for w in writes:
            st = self.res.get(w)
            if st is not None:
                self._need(eng, st[0])
                for s, v in list(st[1].items()):
                    self._need(eng, (s, v))

    def _record(self, ev, reads, writes):
        s, v = ev
        for r in reads:
            st = self.res.setdefault(r, [None, {}])
            if st[1].get(s, 0) < v:
                st[1][s] = v
        for w in writes:
            self.res[w] = [ev, {}]

    def op(self, eng, fn, r=(), w=()):
        self._deps(eng, r, w)
        ins = fn(self.eng[eng])
        self.cnt[eng] += 1
        ev = (eng, self.cnt[eng])
        ins.then_inc(self.sem[eng], 1)
        self._record(ev, r, w)
        self.n_inst += 1
        return ev

    def dma(self, q, out, in_, r=(), w=(), is_output=False, **kw):
        d = self.dq[q]
        i = d["i"]
        d["i"] += 1
        n = len(d["sems"])
        sname = d["sems"][i % n]
        tgt = 16 * (i // n + 1)
        if i >= n:
            self._need(q, (sname, tgt - 16))
        self._deps(q, r, w)
        ins = self.eng[q].dma_start(out=out, in_=in_, **kw)
        ins.then_inc(self.sem[sname], 16)
        ev = (sname, tgt)
        self._record(ev, r, w)
        self.n_inst += 1
        self.last_dma[sname] = tgt
        if is_output:
            self.out_events.append(ev)
        return ev

    def barrier(self):
        evs = [(e, self.cnt[e]) for e in ("pe", "act", "dve", "pool") if self.cnt[e]]
        evs += list(self.last_dma.items())
        for e in self.ENG:
            for ev in evs:
                if ev[0] != e:
                    self._need(e, ev)
        self.res = {}

    def finish(self):
        for ev in self.out_events:
            self._need("sp", ev)
        for e in ("pe", "act", "dve", "pool"):
            if self.cnt[e]:
                self._need("sp", (e, self.cnt[e]))
        for s, v in self.last_dma.items():
            self._need("sp", (s, v))


class PsumPool:
    def __init__(self, nc, es):
        self.banks = [es.enter_context(nc.psum_tensor("psb%d" % i, [128, 512], F32)) for i in range(8)]
        self.held = set()
        self.i = 0

    def get(self, hold=False):
        for _ in range(16):
            b = self.i % 8
            self.i += 1
            if b not in self.held:
                if hold:
                    self.held.add(b)
                return b
        raise RuntimeError("no free PSUM bank")

    def release(self, b):
        self.held.discard(b)

    def f32(self, b):
        return self.banks[b]

    def bf16(self, b):
        return self.banks[b].bitcast(BF16)


def AP(t, offset, dims):
    return bass.AP(t, offset, [list(d) for d in dims])


def make_consts():
    c = {}
    c["ident"] = np.eye(128, dtype=np.float32)
    c["ones"] = np.ones((128, 128), dtype=np.float32)
    p = np.arange(128)
    c["bones"] = (p[:, None] // 64 == p[None, :] // 64).astype(np.float32)
    for L, nm in ((64, "p"), (8, "s")):
        same = (p[:, None] // L == p[None, :] // L)
        le = p[:, None] <= p[None, :]
        lt = p[:, None] < p[None, :]
        c["negmask_" + nm] = np.where(same & le, 0.0, 30000.0).astype(np.float32)
        msu = (same & lt).astype(np.float32)
        miu = (same & le).astype(np.float32)
        c["mu2_" + nm] = np.concatenate([msu, miu], axis=1)
        c["msl_" + nm] = np.ascontiguousarray(msu.T)
    sel = np.zeros((128, 2, 128), np.float32)
    sel[0, 0, :] = 1.0
    sel[64, 1, :] = 1.0
    c["sel_p"] = sel
    oh = np.zeros((128, 16), np.float32)
    oh[np.arange(16) * 8, np.arange(16)] = 1.0
    c["onehot_s"] = oh
    rm = (p[:, None] // 8 == np.arange(16)[None, :]).astype(np.float32)
    c["rowmask_s"] = rm
    sm = (np.arange(16)[:, None] == (p[None, :] // 8)).astype(np.float32)
    c["seqmask_s"] = np.ascontiguousarray(np.broadcast_to(sm[None], (128, 16, 128))).astype(np.float32)
    t = np.arange(TB)
    c["rst64"] = np.ascontiguousarray(np.broadcast_to((t % 64 != 0).astype(np.float32)[None], (128, TB)))
    c["rst8"] = np.ascontiguousarray(np.broadcast_to((p % 8 != 0).astype(np.float32)[None], (128, 128)))
    c["addrst8"] = np.ascontiguousarray(np.broadcast_to(np.where(p % 8 == 0, -1e30, 0.0).astype(np.float32)[None], (128, 128)))
    c["onesrow"] = np.ones((128, TB), np.float32)
    return c


CONST_SHAPES = {k: v.shape for k, v in make_consts().items()}

IN_SPECS = {
    "xp": [SEQ, D], "xs": [128, D], "pp": [2, SEQ, PLE], "psm": [2, 128, PLE],
    "mC": [NSEQ, MH, MDK, MDV], "mn": [128, 128], "mmT": [MH, NSEQ],
    "rS": [NSEQ, RH, RN, RN], "rsh": [NSEQ, D],
    "vecs": [NV * 16, 128], "nfin": [1, D], "nmix1": [1, D], "big": [MH, 1], "bfg": [MH, 1],
    "ffn_wg": [2, D, DFF], "ffn_wu": [2, D, DFF], "ffn_wd": [2, DFF, D],
    "ple_wp": [2, PLE, D], "ple_wg": [2, D, D],
    "m_wq": [D, MH * MDK], "m_wk": [D, MH * MDK], "m_wv": [D, D], "m_wi": [D, MH], "m_wf": [D, MH],
    "m_wo": [D, D], "m_wout": [D, D],
    "r_wr": [D, D], "r_wk": [D, D], "r_wv": [D, D], "r_wo": [D, D],
    "r_w1": [D, 96], "r_w2": [96, D], "r_a1": [D, 96], "r_a2": [96, D], "r_g1": [D, 256], "r_g2": [256, D],
}
OUT_SPECS = {
    "yp": [SEQ, D], "ys": [128, D],
    "Cp": [MH, MDK, MDV], "np_": [MH, MDK], "mp": [MH, 1], "Sp": [RH, RN, RN], "shp": [1, D],
    "Cs": [NSEQ, MH, MDK, MDV], "ns": [128, 128], "msT": [MH, NSEQ], "Ss": [NSEQ, RH, RN, RN], "shs": [NSEQ, D],
}


def build(cfg=None):
    cfg = cfg or {}
    blocks = cfg.get("blocks")
    if blocks is None:
        blocks = [("p", i) for i in range(SEQ // TB)] + [("s", 0)]
    stop = cfg.get("stop", "end")
    taps = cfg.get("taps", ())
    nc = bass.Bass("TRN2", target_bir_lowering=False)
    dr = {}
    for k, shp in IN_SPECS.items():
        dr[k] = nc.dram_tensor(k, list(shp), F32, kind="ExternalInput")
    for k, shp in CONST_SHAPES.items():
        dr["c_" + k] = nc.dram_tensor("c_" + k, list(shp), F32, kind="ExternalInput")
    for k, shp in OUT_SPECS.items():
        dr[k] = nc.dram_tensor(k, list(shp), F32, kind="ExternalOutput")
    for i_ in range(WC_N):
        dr["wcache%d" % i_] = nc.dram_tensor("wcache%d" % i_, [128, WC_TOT], BF16, kind="Internal")
    tap_d = {}

    with ExitStack() as es:
        P = Prog(nc, es)
        P.relax_same = bool(cfg.get("relax_same", False))
        PS = PsumPool(nc, es)

        uid = [0]

        def sb(name, shape, dt=F32, stack=es):
            uid[0] += 1
            return stack.enter_context(nc.sbuf_tensor("%s_%d" % (name, uid[0]), list(shape), dt))

        def tap(name, src_ap, shape, r):
            if name not in taps:
                return
            if name not in tap_d:
                tap_d[name] = nc.dram_tensor("tap_" + name, list(shape), F32, kind="ExternalOutput")
            P.dma("pool", tap_d[name].ap(), src_ap, r=r, is_output=True)

        def TT(eng, out, a, b, op, r, w):
            return P.op(eng, lambda e: e.tensor_tensor(out=out, in0=a, in1=b, op=op), r, w)

        def TS(eng, out, a, s1, s2, op0, op1, r, w):
            if s2 is None:
                return P.op(eng, lambda e: e.tensor_scalar(out=out, in0=a, scalar1=s1, scalar2=None, op0=op0), r, w)
            return P.op(eng, lambda e: e.tensor_scalar(out=out, in0=a, scalar1=s1, scalar2=s2, op0=op0, op1=op1), r, w)

        def STT(out, a, s, b, op0, op1, r, w):
            return P.op("dve", lambda e: e.scalar_tensor_tensor(out=out, in0=a, scalar=s, in1=b, op0=op0, op1=op1), r, w)

        def ACT(out, in_, func, r, w, scale=None, bias=None, accum=None):
            kw = {}
            if scale is not None:
                kw["scale"] = scale
            if bias is not None:
                kw["bias"] = bias
            if accum is not None:
                kw["accum_out"] = accum
            return P.op("act", lambda e: e.activation(out=out, in_=in_, func=func, **kw), r, w)

        def CP(eng, out, in_, r, w):
            if eng == "act":
                return P.op("act", lambda e: e.copy(out=out, in_=in_), r, w)
            return P.op(eng, lambda e: e.tensor_copy(out=out, in_=in_), r, w)

        def MM(out, lhsT, rhs, start, stop_, r, w, skip=False):
            return P.op("pe", lambda e: e.matmul(out, lhsT=lhsT, rhs=rhs, start=start, stop=stop_,
                                                 skip_group_check=skip), r, w)

        def TR(out, in_, ident, r, w):
            return P.op("pe", lambda e: e.transpose(out=out, in_=in_, identity=ident), r, w)

        def MSET(eng, ap, val, w):
            return P.op(eng, lambda e: e.memset(ap, val), (), w)

        NTMAX = TB // 128
        H = [sb("H%d" % t, [128, D]) for t in range(NTMAX)]
        XT = sb("XT", [128, KC, TB + 1], BF16)
        xb = sb("xb", [128, D], BF16)
        nst = sb("nst", [128, 4])
        NWS, NWB = 2, 2
        wst = [sb("wst%d" % i, [128, 4096]) for i in range(NWS)]
        wbf = [sb("wbf%d" % i, [128, 4096], BF16) for i in range(NWB)]
        wctr = {"s": 0, "b": 0, "c": 0, "p": 0, "off": 0, "ci": 0, "mode": "store"}
        wcidx = {}
        idf = sb("idf", [128, 128])
        idb = sb("idb", [128, 128], BF16)
        onesf = sb("onesf", [128, 128])
        bonesf = sb("bonesf", [128, 128])
        FMV = sb("FMV", [128, NVX * 16])
        xlast = sb("xlast", [128, KC], BF16)
        Cst = sb("Cst", [128, MH, MDV + 1])
        STD = sb("STD", [128, NG, 2, RN])
        carry = sb("carry", [8, 4])

        def load_const(name, dst, q="sp"):
            P.dma(q, dst, dr["c_" + name].ap(), w=[name])

        P.dma("sp", idf[:], dr["c_ident"].ap(), w=["idf"])
        P.dma("sp", onesf[:], dr["c_ones"].ap(), w=["onesf"])
        P.dma("sp", bonesf[:], dr["c_bones"].ap(), w=["bonesf"])
        CP("dve", idb[:], idf[:], ["idf"], ["idb"])
        with ExitStack() as st:
            vl = sb("vl", [128, 3, 128], F32, st)
            nrows = NV * 16
            for i in range(3):
                n = min(128, nrows - i * 128)
                P.dma("sp", vl[0:n, i, :], dr["vecs"].ap()[i * 128:i * 128 + n, :], w=[("vl", i)])
            b = PS.get()
            for i in range(3):
                n = min(128, nrows - i * 128)
                TR(PS.f32(b)[:, i * 128:i * 128 + n], vl[0:n, i, :], idf[0:n, 0:n], [("vl", i), "idf"], [("ps", b)])
            CP("dve", FMV[:, 0:nrows], PS.f32(b)[:, 0:nrows], [("ps", b)], ["FMV"])
            TS("dve", FMV[:, V_1MU0 * 16:(V_1MU0 + 6) * 16], FMV[:, V_MU0 * 16:(V_MU0 + 6) * 16], -1.0, 1.0,
               ALU.mult, ALU.add, ["FMV"], ["FMV"])
            TS("dve", FMV[:, V_1KA * 16:(V_1KA + 1) * 16], FMV[:, V_KA * 16:(V_KA + 1) * 16], -1.0, 1.0,
               ALU.mult, ALU.add, ["FMV"], ["FMV"])
            P.barrier()
        MSET("dve", Cst[:], 0.0, ["Cst"])
        MSET("dve", STD[:], 0.0, ["STD"])
        MSET("dve", xlast[:], 0.0, ["xlast"])
        MSET("dve", carry[:], 0.0, ["carry"])

        def fmv(v, kc0=0, n=1):
            return FMV[:, v * 16 + kc0:v * 16 + kc0 + n]

        CAST_ENGS = cfg.get("cast_engs", ("act",))

        def wpanel(src3, np_, kcn, ncols, scales=None):
            sz = kcn * ncols
            nver = 1 if scales is None else len(scales)
            assert sz * nver <= 4096
            bi = wctr["b"] % NWB
            wctr["b"] += 1
            idx = wctr["p"]
            wctr["p"] += 1
            sig = (np_, kcn, ncols, nver)
            if wctr["mode"] == "load":
                ci, off, sig0 = wcidx[idx]
                assert sig0 == sig, (idx, sig0, sig)
                P.dma("sp", wbf[bi][0:np_, 0:nver * sz], dr["wcache%d" % ci].ap()[0:np_, off:off + nver * sz], r=[("wc", idx)], w=[("wbf", bi)])
                return [wbf[bi][0:np_, vi * sz:(vi + 1) * sz].rearrange("p (k n) -> p k n", k=kcn) for vi in range(nver)], ("wbf", bi)
            si = wctr["s"] % NWS
            wctr["s"] += 1
            stg = wst[si][0:np_, 0:sz].rearrange("p (k n) -> p k n", k=kcn)
            P.dma("sp", stg, src3, w=[("wst", si)])
            outs = []
            for vi in range(nver):
                dst = wbf[bi][0:np_, vi * sz:(vi + 1) * sz].rearrange("p (k n) -> p k n", k=kcn)
                eng = CAST_ENGS[wctr["c"] % len(CAST_ENGS)]
                wctr["c"] += 1
                if scales is None or scales[vi] is None:
                    CP(eng, dst, stg, [("wst", si)], [("wbf", bi)] if vi == 0 else [("wbf", bi)])
                else:
                    eng = "pool"
                    sc = AP(FMV, scales[vi], [[NVX * 16, np_], [1, kcn], [0, ncols]])
                    TT(eng, dst, stg, sc, ALU.mult, [("wst", si), "FMV"], [("wbf", bi)])
                outs.append(dst)
            if wctr["mode"] == "store":
                if wctr["off"] + nver * sz > WC_TOT:
                    wctr["ci"] += 1
                    wctr["off"] = 0
                    assert wctr["ci"] < WC_N
                ci, off = wctr["ci"], wctr["off"]
                wctr["off"] += nver * sz
                wcidx[idx] = (ci, off, sig)
                P.dma("act" if (scales is None) else "pool", dr["wcache%d" % ci].ap()[0:np_, off:off + nver * sz], wbf[bi][0:np_, 0:nver * sz],
                      r=[("wbf", bi)], w=[("wc", idx)])
            return outs, ("wbf", bi)

        def wsrc(name, layer, r0, nrows, c0, ncols):
            t = dr[name].ap()
            if layer is not None:
                t = t[layer]
            if nrows <= 128:
                return t[r0:r0 + nrows, c0:c0 + ncols].rearrange("(k p) n -> p k n", k=1), nrows, 1
            return t[r0:r0 + nrows, c0:c0 + ncols].rearrange("(k p) n -> p k n", p=128), 128, nrows // 128

        def load_block(kind, bi, nt):
            for t in range(nt):
                if kind == "p":
                    src = dr["xp"].ap()[bi * TB + t * 128:bi * TB + (t + 1) * 128, :]
                else:
                    src = dr["xs"].ap()
                P.dma("sp", H[t][:], src, w=[("H", t)])

        def rms_stats(t):
            ACT(xb[:], H[t][:], AF.Square, [("H", t)], ["xb", "nst"], accum=nst[:, 0:1])
            ACT(nst[:, 1:2], nst[:, 0:1], AF.Sqrt, ["nst"], ["nst"], scale=1.0 / D, bias=epsb[:, 0:1])
            P.op("dve", lambda e: e.reciprocal(out=nst[:, 2:3], in_=nst[:, 1:2]), ["nst"], ["nst"])

        def norm_to_xt(vidx, nt):
            for t in range(nt):
                rms_stats(t)
                ACT(xb[:], H[t][:], AF.Copy, [("H", t), "nst"], ["xb"], scale=nst[:, 2:3])
                for half in range(2):
                    b = PS.get()
                    for j in range(8):
                        kc = half * 8 + j
                        TR(PS.bf16(b)[:, j * 128:(j + 1) * 128], xb[:, kc * 128:(kc + 1) * 128], idb[:], ["xb", "idb"], [("ps", b)])
                    src = PS.bf16(b)[:, 0:1024].rearrange("p (a b) -> p a b", a=8)
                    gb = AP(FMV, vidx * 16 + half * 8, [[NVX * 16, 128], [1, 8], [0, 128]])
                    TT("dve", XT[:, half * 8:half * 8 + 8, 1 + t * 128:1 + (t + 1) * 128], src, gb, ALU.mult,
                       [("ps", b), "FMV"], ["XT"])

        def xt_cols(kc, c0, n):
            return XT[:, kc, 1 + c0:1 + c0 + n]

        def mm_fm(name, layer, c0, ncols, ntok, consumer, rhs_fn=None, scales=None, keys_r=("XT",)):
            pc = 256 if scales is None else 128
            for p0 in range(0, ncols, pc):
                n = min(pc, ncols - p0)
                src, np_, kcn = wsrc(name, layer, 0, D, c0 + p0, n)
                wv, wkey = wpanel(src, np_, kcn, n, scales)
                for m0 in range(0, n, 128):
                    mw = min(128, n - m0)
                    b = PS.get()
                    steps = []
                    for vi, wt in enumerate(wv):
                        for kc in range(KC):
                            steps.append((wt[:, kc, m0:m0 + mw], vi, kc))
                    for i, (lt, vi, kc) in enumerate(steps):
                        rhs = rhs_fn(vi, kc) if rhs_fn else xt_cols(kc, 0, ntok)
                        MM(PS.f32(b)[0:mw, 0:ntok], lt, rhs, i == 0, i == len(steps) - 1, [wkey] + list(keys_r), [("ps", b)])
                    consumer((c0 + p0 + m0) // 128, mw, b)

        def mm_tm(name, layer, c0, ncols, nt, consumer, lhs_fn=None, keys_r=("XT",), kin=D, r0=0):
            for p0 in range(0, ncols, 256):
                n = min(256, ncols - p0)
                src, np_, kcn = wsrc(name, layer, r0, kin, c0 + p0, n)
                wv, wkey = wpanel(src, np_, kcn, n)
                wt = wv[0]
                for t in range(nt):
                    b = PS.get()
                    for kc in range(kcn):
                        lt = lhs_fn(kc, t) if lhs_fn else xt_cols(kc, t * 128, 128)
                        MM(PS.f32(b)[:, 0:n], lt, wt[:, kc, :], kc == 0, kc == kcn - 1, [wkey] + list(keys_r), [("ps", b)])
                    consumer(t, c0 + p0, n, b)

        def ffn(layer, nt):
            ntok = nt * 128
            norm_to_xt(V_NFFN0 if layer == 0 else V_NFFN1, nt)
            with ExitStack() as st:
                act = sb("ffn_act", [128, 16, TB], BF16, st)
                sg = [sb("ffn_sg%d" % i, [128, TB], F32, st) for i in range(2)]
                groups = [(0, 16), (16, 16), (32, 12)]
                for g0, gn in groups:
                    for m in range(g0, g0 + gn, 2):
                        srcg, np_, kcn = wsrc("ffn_wg", layer, 0, D, m * 128, 256)
                        wg, kg = wpanel(srcg, np_, kcn, 256)
                        srcu, np_, kcn = wsrc("ffn_wu", layer, 0, D, m * 128, 256)
                        wu, ku = wpanel(srcu, np_, kcn, 256)
                        for mm_ in range(2):
                            bg_ = PS.get()
                            for kc in range(KC):
                                MM(PS.f32(bg_)[:, 0:ntok], wg[0][:, kc, mm_ * 128:(mm_ + 1) * 128], xt_cols(kc, 0, ntok),
                                   kc == 0, kc == KC - 1, [kg, "XT"], [("ps", bg_)])
                            bu_ = PS.get()
                            for kc in range(KC):
                                MM(PS.f32(bu_)[:, 0:ntok], wu[0][:, kc, mm_ * 128:(mm_ + 1) * 128], xt_cols(kc, 0, ntok),
                                   kc == 0, kc == KC - 1, [ku, "XT"], [("ps", bu_)])
                            j = m + mm_ - g0
                            s_ = sg[(m + mm_) % 2]
                            sk = ("sg", (m + mm_) % 2)
                            ACT(s_[:, 0:ntok], PS.f32(bg_)[:, 0:ntok], AF.Silu, [("ps", bg_)], [sk])
                            TT("dve", act[:, j, 0:ntok], PS.f32(bu_)[:, 0:ntok], s_[:, 0:ntok], ALU.mult,
                               [("ps", bu_), sk], [("act", j)])
                    def down_cons(t, c0, n, b):
                        TT("dve", H[t][:, c0:c0 + n], PS.f32(b)[:, 0:n], H[t][:, c0:c0 + n], ALU.add,
                           [("ps", b), ("H", t)], [("H", t)])
                    mm_tm("ffn_wd", layer, 0, D, nt, down_cons,
                          lhs_fn=lambda kc, t: act[:, kc, t * 128:(t + 1) * 128],
                          keys_r=[("act", j) for j in range(gn)], kin=gn * 128, r0=g0 * 128)
                P.barrier()

        def ple(layer, kind, bi, nt):
            norm_to_xt(V_NPLE0 if layer == 0 else V_NPLE1, nt)
            with ExitStack() as st:
                pl = sb("ple_p", [128, PLE], F32, st)
                plb = sb("ple_pb", [128, PLE], BF16, st)
                pT = sb("ple_pT", [128, 2, TB], BF16, st)
                sgm = [sb("ple_sg%d" % i, [128, 256], F32, st) for i in range(2)]
                for t in range(nt):
                    if kind == "p":
                        src = dr["pp"].ap()[layer, bi * TB + t * 128:bi * TB + (t + 1) * 128, :]
                    else:
                        src = dr["psm"].ap()[layer]
                    P.dma("sp", pl[:], src, w=["pl"])
                    CP("dve", plb[:], pl[:], ["pl"], ["plb"])
                    b = PS.get()
                    for j in range(2):
                        TR(PS.bf16(b)[:, j * 128:(j + 1) * 128], plb[:, j * 128:(j + 1) * 128], idb[:], ["plb", "idb"], [("ps", b)])
                    CP("dve", pT[:, :, t * 128:(t + 1) * 128], PS.bf16(b)[:, 0:256].rearrange("p (a b) -> p a b", a=2),
                       [("ps", b)], ["pT"])
                ctr = [0]
                for p0 in range(0, D, 256):
                    srcg, np_, kcn = wsrc("ple_wg", layer, 0, D, p0, 256)
                    wg, kg = wpanel(srcg, np_, kcn, 256)
                    srcp, np2, kcn2 = wsrc("ple_wp", layer, 0, PLE, p0, 256)
                    wp, kp = wpanel(srcp, np2, kcn2, 256)
                    for t in range(nt):
                        b1 = PS.get()
                        for kc in range(KC):
                            MM(PS.f32(b1)[:, 0:256], xt_cols(kc, t * 128, 128), wg[0][:, kc, :], kc == 0, kc == KC - 1,
                               [kg, "XT"], [("ps", b1)])
                        b2 = PS.get()
                        for j in range(2):
                            MM(PS.f32(b2)[:, 0:256], pT[:, j, t * 128:(t + 1) * 128], wp[0][:, j, :], j == 0, j == 1,
                               [kp, "pT"], [("ps", b2)])
                        s_ = sgm[ctr[0] % 2]
                        sk = ("plesg", ctr[0] % 2)
                        ctr[0] += 1
                        ACT(s_[:], PS.f32(b1)[:, 0:256], AF.Sigmoid, [("ps", b1)], [sk])
                        TT("dve", s_[:], PS.f32(b2)[:, 0:256], s_[:], ALU.mult, [("ps", b2), sk], [sk])
                        TT("pool", H[t][:, p0:p0 + 256], H[t][:, p0:p0 + 256], s_[:], ALU.add, [sk, ("H", t)], [("H", t)])
                P.barrier()

        def final_out(kind, bi, nt):
            with ExitStack() as st:
                gt = sb("fin_g", [128, D], F32, st)
                yo = [sb("fin_y%d" % i, [128, D], F32, st) for i in range(2)]
                P.dma("sp", gt[:], AP(dr["nfin"], 0, [[0, 128], [1, D]]), w=["fin_g"])
                for t in range(nt):
                    rms_stats(t)
                    y = yo[t % 2]
                    STT(y[:], H[t][:], nst[:, 2:3], gt[:], ALU.mult, ALU.mult, [("H", t), "nst", "fin_g"], [("yo", t % 2)])
                    if kind == "p":
                        dst = dr["yp"].ap()[bi * TB + t * 128:bi * TB + (t + 1) * 128, :]
                    else:
                        dst = dr["ys"].ap()
                    P.dma("pool", dst, y[:], r=[("yo", t % 2)], is_output=True)
                P.barrier()

        epsb = sb("epsb", [128, 4])
        MSET("dve", epsb[:, 0:1], 1e-6, ["epsb"])
        MSET("dve", epsb[:, 1:2], 1.0, ["epsb"])
        MSET("dve", epsb[:, 2:3], 64e-5, ["epsb"])
        MSET("dve", epsb[:, 3:4], 1e-24, ["epsb"])

        gb15 = sb("gb15", [8, 2])
        P.dma("sp", gb15[:, 0:1], dr["big"].ap(), w=["gb15"])
        P.dma("sp", gb15[:, 1:2], dr["bfg"].ap(), w=["gb15"])
        TS("dve", gb15[:], gb15[:], 1.0 / 15.0, None, ALU.mult, None, ["gb15"], ["gb15"])

        def mlstm(kind, bi, nt):
            ntok = nt * 128
            W = ntok
            L = 64 if kind == "p" else 8
            nch = ntok // L
            sfx = kind
            norm_to_xt(V_NMIX0, nt)
            with ExitStack() as st:
                rw = {nm: sb("ml_r_" + nm, [8, W], F32, st) for nm in
                      ("li", "lf", "bg", "z", "cm", "cmP", "cmE", "t0", "qA", "qB", "qC", "qD")}
                QT = sb("ml_QT", [128, nt, 6, 8], F32, st)
                DEC = sb("ml_DEC", [128, nt, 16, 8], F32, st) if kind == "s" else sb("ml_DEC", [128, nt, 2, 8], F32, st)
                WT = sb("ml_WT", [128, nt, 8, 128], BF16, st)
                Dcm = sb("ml_Dcm", [128, 8, 128], F32, st)
                tmpw = sb("ml_tmpw", [128, 8, 128], F32, st)
                negm = sb("ml_negm", [128, 128], F32, st)
                onesrow = sb("ml_onesrow", [8, W], F32, st)
                qT = [sb("ml_qT%d" % i, [128, W], BF16, st) for i in range(2)]
                kT = [sb("ml_kT%d" % i, [128, W], BF16, st) for i in range(2)]
                V1 = sb("ml_V1", [128, nt, MDV + 1], BF16, st)
                ogs = sb("ml_ogs", [128, nt, MDV], F32, st)
                kw = sb("ml_kw", [128, 128], BF16, st)
                Sb = sb("ml_Sb", [128, 128], BF16, st)
                intra = sb("ml_intra", [128, MDV + 1], F32, st)
                hn = sb("ml_hn", [128, MDV + 1], F32, st)
                sc = sb("ml_sc", [128, 8], F32, st)
                HO = sb("ml_HO", [128, MDV], BF16, st)
                Cb = sb("ml_Cb", [128, MH, MDV + 1], BF16, st)
                YT = sb("ml_YT", [128, KC, W], BF16, st)
                if kind == "p":
                    kwL = [kw] + [sb("ml_kwL", [128, 128], BF16, st) for _ in range(nt - 1)]
                    SbL = [Sb] + [sb("ml_SbL", [128, 128], BF16, st) for _ in range(nt - 1)]
                    inL = [intra] + [sb("ml_inL", [128, MDV + 1], F32, st) for _ in range(nt - 1)]
                    scL = [sc] + [sb("ml_scL", [128, 8], F32, st) for _ in range(nt - 1)]
                    HOL = [HO] + [sb("ml_HOL", [128, MDV], BF16, st) for _ in range(nt - 1)]
                    junk = hn
                P.dma("sp", negm[:], dr["c_negmask_" + sfx].ap(), w=["negm"])
                MSET("pool", V1[:, :, MDV:MDV + 1], 1.0, ["V1"])
                if kind == "p":
                    P.dma("sp", onesrow[:], dr["c_onesrow"].ap()[0:8, 0:W], w=["onesrow"])
                    selp = sb("ml_selp", [128, 2, 128], F32, st)
                    P.dma("sp", selp[:], dr["c_sel_p"].ap(), w=["selp"])
                    CP("act", Cb[:], Cst[:], ["Cst"], ["Cb"])
                else:
                    rst8 = sb("ml_rst8", [8, 128], F32, st)
                    addrst8 = sb("ml_addrst8", [8, 128], F32, st)
                    m0T = sb("ml_m0T", [8, 16], F32, st)
                    onehot = sb("ml_onehot", [128, 16], F32, st)
                    rowmask = sb("ml_rowmask", [128, 16], F32, st)
                    seqmask = sb("ml_seqmask", [128, 16, 128], BF16, st)
                    rhs2 = sb("ml_rhs2", [128, 16, 8], F32, st)
                    QZ = sb("ml_QZ", [128, 16, 128], BF16, st)
                    KZ = sb("ml_KZ", [128, 16, 128], BF16, st)
                    C0f = sb("ml_C0f", [128, 16, MDV + 1], F32, st)
                    C0b = sb("ml_C0b", [128, 16, MDV + 1], BF16, st)
                    n0l = sb("ml_n0l", [128, 128], F32, st)
                    n0T = sb("ml_n0T", [128, 128], F32, st)
                    NS = sb("ml_NS", [128, 128], F32, st)
                    msr = sb("ml_msr", [8, 16], F32, st)
                    P.dma("sp", rst8[:], dr["c_rst8"].ap()[0:8, :], w=["rst8"])
                    P.dma("sp", addrst8[:], dr["c_addrst8"].ap()[0:8, :], w=["addrst8"])
                    P.dma("sp", m0T[:], dr["mmT"].ap(), w=["m0T"])
                    P.dma("sp", onehot[:], dr["c_onehot_s"].ap(), w=["onehot"])
                    P.dma("sp", rowmask[:], dr["c_rowmask_s"].ap(), w=["rowmask"])
                    P.dma("sp", wst[0][:, 0:2048], dr["c_seqmask_s"].ap().rearrange("p a b -> p (a b)"), w=[("wst", 0)])
                    CP("pool", seqmask[:].rearrange("p a b -> p (a b)"), wst[0][:, 0:2048], [("wst", 0)], ["seqmask"])
                    P.dma("sp", n0l[:], dr["mn"].ap(), w=["n0l"])
                    b = PS.get()
                    TR(PS.f32(b)[:, 0:128], n0l[:], idf[:], ["n0l", "idf"], [("ps", b)])
                    CP("dve", n0T[:], PS.f32(b)[:, 0:128], [("ps", b)], ["n0T"])

                wi, ki = wpanel(wsrc("m_wi", None, 0, D, 0, MH)[0], 128, KC, MH)
                bgi = PS.get()
                for kc in range(KC):
                    MM(PS.f32(bgi)[0:8, 0:ntok], wi[0][:, kc, :], xt_cols(kc, 0, ntok), kc == 0, kc == KC - 1, [ki, "XT"], [("ps", bgi)])
                ACT(rw["li"][:, 0:ntok], PS.f32(bgi)[0:8, 0:ntok], AF.Tanh, [("ps", bgi), "gb15"], ["li"], scale=1.0 / 15.0, bias=gb15[:, 0:1])
                TS("dve", rw["li"][:, 0:ntok], rw["li"][:, 0:ntok], 15.0, None, ALU.mult, None, ["li"], ["li"])
                wf, kf = wpanel(wsrc("m_wf", None, 0, D, 0, MH)[0], 128, KC, MH)
                bgf = PS.get()
                for kc in range(KC):
                    MM(PS.f32(bgf)[0:8, 0:ntok], wf[0][:, kc, :], xt_cols(kc, 0, ntok), kc == 0, kc == KC - 1, [kf, "XT"], [("ps", bgf)])
                ACT(rw["t0"][:, 0:ntok], PS.f32(bgf)[0:8, 0:ntok], AF.Tanh, [("ps", bgf), "gb15"], ["t0"], scale=1.0 / 15.0, bias=gb15[:, 1:2])
                ACT(rw["t0"][:, 0:ntok], rw["t0"][:, 0:ntok], AF.Exp, ["t0"], ["t0"], scale=-15.0)
                ACT(rw["lf"][:, 0:ntok], rw["t0"][:, 0:ntok], AF.Ln, ["t0", "epsb"], ["lf"], bias=epsb[0:8, 1:2])
                TS("dve", rw["lf"][:, 0:ntok], rw["lf"][:, 0:ntok], -1.0, None, ALU.mult, None, ["lf"], ["lf"])
                if kind == "p":
                    P.op("dve", lambda e: e.tensor_tensor_scan(out=rw["bg"][:, 0:ntok], data0=onesrow[:, 0:ntok], data1=rw["lf"][:, 0:ntok],
                                                               initial=carry[:, 0:1], op0=ALU.mult, op1=ALU.add),
                         ["onesrow", "lf", "carry"], ["bg"])
                else:
                    P.op("dve", lambda e: e.tensor_tensor_scan(out=rw["bg"][:, 0:ntok], data0=rst8[:, 0:ntok], data1=rw["lf"][:, 0:ntok],
                                                               initial=0.0, op0=ALU.mult, op1=ALU.add),
                         ["rst8", "lf"], ["bg"])
                TT("dve", rw["z"][:, 0:ntok], rw["li"][:, 0:ntok], rw["bg"][:, 0:ntok], ALU.subtract, ["li", "bg"], ["z"])
                if kind == "p":
                    P.op("dve", lambda e: e.tensor_tensor_scan(out=rw["cm"][:, 0:ntok], data0=rw["z"][:, 0:ntok], data1=rw["z"][:, 0:ntok],
                                                               initial=carry[:, 1:2], op0=ALU.max, op1=ALU.max),
                         ["z", "carry"], ["cm"])
                    CP("dve", rw["cmP"][:, L:ntok].rearrange("p (c l) -> p c l", l=L),
                       AP(rw["cm"], L - 1, [[W, 8], [L, nch - 1], [0, L]]), ["cm"], ["cmP"])
                    CP("dve", rw["cmP"][:, 0:L], AP(carry, 1, [[4, 8], [0, L]]), ["carry"], ["cmP"])
                else:
                    CP("dve", rw["t0"][:, 0:ntok], rw["z"][:, 0:ntok], ["z"], ["t0"])
                    zv = AP(rw["t0"], 0, [[W, 8], [8, 16], [1, 1]])
                    TT("dve", zv, zv, AP(m0T, 0, [[16, 8], [1, 16], [1, 1]]), ALU.max, ["t0", "m0T"], ["t0"])
                    P.op("dve", lambda e: e.tensor_tensor_scan(out=rw["cm"][:, 0:ntok], data0=addrst8[:, 0:ntok], data1=rw["t0"][:, 0:ntok],
                                                               initial=0.0, op0=ALU.add, op1=ALU.max),
                         ["t0", "addrst8"], ["cm"])
                    CP("dve", rw["cmP"][:, 0:ntok].rearrange("p (c l) -> p c l", l=L), AP(m0T, 0, [[16, 8], [1, 16], [0, L]]), ["m0T"], ["cmP"])
                CP("dve", rw["cmE"][:, 0:ntok].rearrange("p (c l) -> p c l", l=L),
                   AP(rw["cm"], L - 1, [[W, 8], [L, nch], [0, L]]), ["cm"], ["cmE"])
                if kind == "p":
                    if bi == SEQ // TB - 1:
                        TT("dve", rw["t0"][:, 0:1], rw["bg"][:, ntok - 1:ntok], rw["cm"][:, ntok - 1:ntok], ALU.add, ["bg", "cm"], ["t0"])
                        P.dma("pool", dr["mp"].ap(), rw["t0"][:, 0:1], r=["t0"], is_output=True)
                    CP("dve", carry[:, 0:1], rw["bg"][:, ntok - 1:ntok], ["bg", "cmP"], ["carry"])
                    CP("dve", carry[:, 1:2], rw["cm"][:, ntok - 1:ntok], ["cm", "cmP"], ["carry"])
                else:
                    TT("dve", msr[:], AP(rw["bg"], 7, [[W, 8], [8, 16]]), AP(rw["cm"], 7, [[W, 8], [8, 16]]), ALU.add, ["bg", "cm"], ["msr"])
                    P.dma("pool", dr["msT"].ap(), msr[:], r=["msr"], is_output=True)
                for nm in ("li", "lf", "bg", "z", "cm", "cmP", "cmE"):
                    tap("ml_%s_%s%d" % (nm, kind, bi), rw[nm][:, 0:ntok], [8, ntok], [nm])
                N_ = slice(0, ntok)
                TT("dve", rw["qA"][:, N_], rw["cmP"][:, N_], rw["cm"][:, N_], ALU.subtract, ["cmP", "cm"], ["qA"])
                ACT(rw["qA"][:, N_], rw["qA"][:, N_], AF.Exp, ["qA"], ["qA"])
                TT("dve", rw["qB"][:, N_], rw["z"][:, N_], rw["cmE"][:, N_], ALU.subtract, ["z", "cmE"], ["qB"])
                ACT(rw["qB"][:, N_], rw["qB"][:, N_], AF.Exp, ["qB"], ["qB"])
                TT("dve", rw["qC"][:, N_], rw["cmP"][:, N_], rw["cmE"][:, N_], ALU.subtract, ["cmP", "cmE"], ["qC"])
                ACT(rw["qC"][:, N_], rw["qC"][:, N_], AF.Exp, ["qC"], ["qC"])
                TT("dve", rw["qD"][:, N_], rw["bg"][:, N_], rw["cm"][:, N_], ALU.add, ["bg", "cm"], ["qD"])
                ACT(rw["qD"][:, N_], rw["qD"][:, N_], AF.Exp, ["qD"], ["qD"], scale=-1.0)
                qnames = ["qA", "qB", "qC", "qD", "z", "cm"]
                for t in range(nt):
                    b = PS.get()
                    for qi, qn in enumerate(qnames):
                        TR(PS.f32(b)[:, qi * 8:(qi + 1) * 8], rw[qn][:, t * 128:(t + 1) * 128], idf[0:8, 0:8], [qn, "idf"], [("ps", b)])
                    CP("dve", QT[:, t, :, :].rearrange("p a b -> p (a b)"), PS.f32(b)[:, 0:48], [("ps", b)], ["QT"])
                for t in range(nt):
                    if kind == "p":
                        b = PS.get()
                        for c in range(2):
                            MM(PS.f32(b)[:, c * 8:(c + 1) * 8], selp[:, c, :], QT[:, t, 2, :], True, True, ["selp", "QT"], [("ps", b)], skip=True)
                        CP("dve", DEC[:, t, :, :].rearrange("p a b -> p (a b)"), PS.f32(b)[:, 0:16], [("ps", b)], ["DEC"])
                    else:
                        TT("dve", rhs2[:], AP(QT, 2 * 8, [[nt * 48, 128], [0, 16], [1, 8]]), AP(onehot, 0, [[16, 128], [1, 16], [0, 8]]),
                           ALU.mult, ["QT", "onehot"], ["rhs2"])
                        b = PS.get()
                        MM(PS.f32(b)[:, 0:128], onesf[:], rhs2[:].rearrange("p a b -> p (a b)"), True, True, ["onesf", "rhs2"], [("ps", b)])
                        CP("dve", DEC[:, t, :, :].rearrange("p a b -> p (a b)"), PS.f32(b)[:, 0:128], [("ps", b)], ["DEC"])
                    TT("dve", Dcm[:], AP(idf, 0, [[128, 128], [0, 8], [1, 128]]),
                       AP(QT, t * 48 + 5 * 8, [[nt * 48, 128], [1, 8], [0, 128]]), ALU.mult, ["idf", "QT"], ["Dcm"])
                    for hh in range(2):
                        b = PS.get()
                        MM(PS.f32(b)[:, 0:512], onesf[:], Dcm[:, 4 * hh:4 * hh + 4, :].rearrange("p a b -> p (a b)"), True, True,
                           ["onesf", "Dcm"], [("ps", b)])
                        TT("dve", tmpw[:, 4 * hh:4 * hh + 4, :], PS.f32(b)[:, 0:512].rearrange("p (a b) -> p a b", a=4),
                           AP(QT, t * 48 + 4 * 8 + 4 * hh, [[nt * 48, 128], [1, 4], [0, 128]]), ALU.subtract, [("ps", b), "QT"], ["tmpw"])
                    TT("dve", tmpw[:], tmpw[:], AP(negm, 0, [[128, 128], [0, 8], [1, 128]]), ALU.max, ["tmpw", "negm"], ["tmpw"])
                    ACT(WT[:, t, :, :], tmpw[:], AF.Exp, ["tmpw"], [("WT", t)], scale=-1.0)

                for hp in range(MH // 2):
                    def qcons(m, mw, b):
                        CP("act", qT[m % 2][:, 0:ntok], PS.f32(b)[:, 0:ntok], [("ps", b)], [("qT", m % 2)])
                    mm_fm("m_wq", None, hp * 256, 256, ntok, qcons)

                    def kcons(m, mw, b):
                        ACT(kT[m % 2][:, 0:ntok], PS.f32(b)[:, 0:ntok], AF.Copy, [("ps", b)], [("kT", m % 2)], scale=MDK ** -0.5)
                    mm_fm("m_wk", None, hp * 256, 256, ntok, kcons)
                    for hh in range(2):
                        h = hp * 2 + hh
                        qh, kh = qT[hh], kT[hh]
                        qk, kk_ = ("qT", hh), ("kT", hh)

                        def vcons(t, c0, n, b):
                            CP("act", V1[:, t, 0:MDV], PS.f32(b)[:, 0:MDV], [("ps", b)], ["V1"])
                        mm_tm("m_wv", None, h * MDV, MDV, nt, vcons)

                        def ocons(t, c0, n, b):
                            ACT(ogs[:, t, :], PS.f32(b)[:, 0:MDV], AF.Sigmoid, [("ps", b)], ["ogs"])
                        mm_tm("m_wo", None, h * MDV, MDV, nt, ocons)
                        if kind == "s":
                            P.dma("sp", C0f[:, :, 0:MDV], dr["mC"].ap()[:, h].rearrange("s k v -> k s v"), w=["C0f"])
                            CP("pool", C0f[:, :, MDV:MDV + 1], AP(n0T, h, [[128, 128], [8, 16], [1, 1]]), ["n0T"], ["C0f"])
                            CP("act", C0b[:], C0f[:], ["C0f"], ["C0b"])
                            TT("dve", QZ[:], AP(qh, 0, [[W, 128], [0, 16], [1, 128]]), seqmask[:], ALU.mult, [qk, "seqmask"], ["QZ"])
                        if kind == "p":
                            TS_ = [slice(t * 128, (t + 1) * 128) for t in range(nt)]
                            bks = []
                            for t in range(nt):
                                b = PS.get()
                                bks.append(b)
                                TR(PS.bf16(b)[:, 0:128], kh[:, TS_[t]], idb[:], [kk_, "idb"], [("ps", b)])
                            for t in range(nt):
                                TS("dve", kwL[t][:], PS.bf16(bks[t])[:, 0:128], QT[:, t, 1, h:h + 1], None, ALU.mult, None,
                                   [("ps", bks[t]), "QT"], [("kw", t)])
                            bks = []
                            for t in range(nt):
                                b = PS.get()
                                bks.append(b)
                                MM(PS.f32(b)[:, 0:128], kh[:, TS_[t]], qh[:, TS_[t]], True, True, [qk, kk_], [("ps", b)])
                            for t in range(nt):
                                TT("dve", SbL[t][:], PS.f32(bks[t])[:, 0:128], WT[:, t, h, :], ALU.mult, [("ps", bks[t]), ("WT", t)], [("Sb", t)])
                            bks = []
                            for t in range(nt):
                                b = PS.get()
                                bks.append(b)
                                MM(PS.f32(b)[:, 0:MDV + 1], SbL[t][:], V1[:, t, :], True, True, [("Sb", t), "V1"], [("ps", b)])
                            for t in range(nt):
                                CP("act", inL[t][:], PS.f32(bks[t])[:, 0:MDV + 1], [("ps", bks[t])], [("intra", t)])
                            for t in range(nt):
                                t0_ = t * 128
                                for c in range(2):
                                    r0 = 64 * c
                                    bU = PS.get()
                                    MM(PS.f32(bU)[:, 0:MDV + 1], kwL[t][r0:r0 + 64, :], V1[r0:r0 + 64, t, :], True, True, [("kw", t), "V1"], [("ps", bU)])
                                    bN = PS.get()
                                    MM(PS.f32(bN)[r0:r0 + 64, 0:MDV + 1], qh[:, t0_ + r0:t0_ + r0 + 64], Cb[:, h, :], True, True,
                                       [qk, ("Cb", h)], [("ps", bN)])
                                    STT(Cst[:, h, :], Cst[:, h, :], DEC[:, t, c, h:h + 1], PS.f32(bU)[:, 0:MDV + 1], ALU.mult, ALU.add,
                                        [("ps", bU), "DEC", ("Cst", h)], [("Cst", h)])
                                    CP("act", Cb[:, h, :], Cst[:, h, :], [("Cst", h)], [("Cb", h)])
                                    STT(inL[t][r0:r0 + 64, :], PS.f32(bN)[r0:r0 + 64, 0:MDV + 1], QT[r0:r0 + 64, t, 0, h:h + 1], inL[t][r0:r0 + 64, :],
                                        ALU.mult, ALU.add, [("ps", bN), "QT", ("intra", t)], [("intra", t)])
                            def fin_stages():
                                yield lambda t: TS("dve", scL[t][:, 7:8], inL[t][:, MDV:MDV + 1], -1.0, None, ALU.mult, None, [("intra", t)], [("sc", t)])
                                yield lambda t: TT("dve", scL[t][:, 7:8], scL[t][:, 7:8], inL[t][:, MDV:MDV + 1], ALU.max, [("intra", t), ("sc", t)], [("sc", t)])
                                yield lambda t: TT("dve", scL[t][:, 0:1], scL[t][:, 7:8], QT[:, t, 3, h:h + 1], ALU.max, [("sc", t), "QT"], [("sc", t)])
                                yield lambda t: P.op("dve", lambda e: e.reciprocal(out=scL[t][:, 1:2], in_=scL[t][:, 0:1]), [("sc", t)], [("sc", t)])
                                yield lambda t: ACT(junk[:, 0:MDV], inL[t][:, 0:MDV], AF.Square, [("intra", t)], ["junk", ("sc", t)], accum=scL[t][:, 2:3])
                                yield lambda t: TT("dve", scL[t][:, 3:4], scL[t][:, 1:2], scL[t][:, 1:2], ALU.mult, [("sc", t)], [("sc", t)])
                                yield lambda t: TT("dve", scL[t][:, 3:4], scL[t][:, 3:4], scL[t][:, 2:3], ALU.mult, [("sc", t)], [("sc", t)])
                                yield lambda t: ACT(scL[t][:, 4:5], scL[t][:, 3:4], AF.Sqrt, [("sc", t), "epsb"], [("sc", t)], scale=1.0 / MDV, bias=epsb[:, 0:1])
                                yield lambda t: P.op("dve", lambda e: e.reciprocal(out=scL[t][:, 5:6], in_=scL[t][:, 4:5]), [("sc", t)], [("sc", t)])
                                yield lambda t: TT("dve", scL[t][:, 6:7], scL[t][:, 5:6], scL[t][:, 1:2], ALU.mult, [("sc", t)], [("sc", t)])
                                yield lambda t: STT(HOL[t][:], inL[t][:, 0:MDV], scL[t][:, 6:7], ogs[:, t, :], ALU.mult, ALU.mult,
                                                    [("intra", t), ("sc", t), "ogs"], [("HO", t)])
                            for fn in fin_stages():
                                for t in range(nt):
                                    fn(t)
                            bks = []
                            for t in range(nt):
                                b = PS.get()
                                bks.append(b)
                                for j in range(2):
                                    TR(PS.bf16(b)[:, j * 128:(j + 1) * 128], HOL[t][:, j * 128:(j + 1) * 128], idb[:], [("HO", t), "idb"], [("ps", b)])
                            for t in range(nt):
                                TT("dve", YT[:, 2 * h:2 * h + 2, TS_[t]], PS.bf16(bks[t])[:, 0:256].rearrange("p (a b) -> p a b", a=2),
                                   AP(FMV, V_MNW * 16 + 2 * h, [[NVX * 16, 128], [1, 2], [0, 128]]), ALU.mult, [("ps", bks[t]), "FMV"], ["YT"])
                            continue
                        for t in range(nt):
                            t0_ = t * 128
                            b = PS.get()
                            TR(PS.bf16(b)[:, 0:128], kh[:, t0_:t0_ + 128], idb[:], [kk_, "idb"], [("ps", b)])
                            TS("dve", kw[:], PS.bf16(b)[:, 0:128], QT[:, t, 1, h:h + 1], None, ALU.mult, None, [("ps", b), "QT"], ["kw"])
                            b = PS.get()
                            MM(PS.f32(b)[:, 0:128], kh[:, t0_:t0_ + 128], qh[:, t0_:t0_ + 128], True, True, [qk, kk_], [("ps", b)])
                            TT("dve", Sb[:], PS.f32(b)[:, 0:128], WT[:, t, h, :], ALU.mult, [("ps", b), ("WT", t)], ["Sb"])
                            bI = PS.get()
                            MM(PS.f32(bI)[:, 0:MDV + 1], Sb[:], V1[:, t, :], True, True, ["Sb", "V1"], [("ps", bI)])
                            CP("act", intra[:], PS.f32(bI)[:, 0:MDV + 1], [("ps", bI)], ["intra"])
                            if kind == "p":
                                for c in range(2):
                                    r0 = 64 * c
                                    bN = PS.get()
                                    MM(PS.f32(bN)[r0:r0 + 64, 0:MDV + 1], qh[:, t0_ + r0:t0_ + r0 + 64], Cb[:, h, :], True, True,
                                       [qk, ("Cb", h)], [("ps", bN)])
                                    STT(hn[r0:r0 + 64, :], PS.f32(bN)[r0:r0 + 64, 0:MDV + 1], QT[r0:r0 + 64, t, 0, h:h + 1], intra[r0:r0 + 64, :],
                                        ALU.mult, ALU.add, [("ps", bN), "QT", "intra"], [("hn", c)])
                                    bU = PS.get()
                                    MM(PS.f32(bU)[:, 0:MDV + 1], kw[r0:r0 + 64, :], V1[r0:r0 + 64, t, :], True, True, ["kw", "V1"], [("ps", bU)])
                                    STT(Cst[:, h, :], Cst[:, h, :], DEC[:, t, c, h:h + 1], PS.f32(bU)[:, 0:MDV + 1], ALU.mult, ALU.add,
                                        [("ps", bU), "DEC", ("Cst", h)], [("Cst", h)])
                                    CP("act", Cb[:, h, :], Cst[:, h, :], [("Cst", h)], [("Cb", h)])
                                hnk = [("hn", 0), ("hn", 1)]
                            else:
                                bN = PS.get()
                                for s in range(NSEQ):
                                    MM(PS.f32(bN)[:, 0:MDV + 1], QZ[:, s, :], C0b[:, s, :], s == 0, s == NSEQ - 1, ["QZ", "C0b"], [("ps", bN)])
                                STT(hn[:], PS.f32(bN)[:, 0:MDV + 1], QT[:, t, 0, h:h + 1], intra[:], ALU.mult, ALU.add,
                                    [("ps", bN), "QT", "intra"], [("hn", 0), ("hn", 1)])
                                hnk = [("hn", 0), ("hn", 1)]
                                TT("dve", KZ[:], AP(kw, 0, [[128, 128], [0, 16], [1, 128]]), AP(rowmask, 0, [[16, 128], [1, 16], [0, 128]]),
                                   ALU.mult, ["kw", "rowmask"], ["KZ"])
                                for s in range(NSEQ):
                                    bU = PS.get()
                                    MM(PS.f32(bU)[:, 0:MDV + 1], KZ[:, s, :], V1[:, t, :], True, True, ["KZ", "V1"], [("ps", bU)])
                                    STT(C0f[:, s, :], C0f[:, s, :], DEC[:, t, s, h:h + 1], PS.f32(bU)[:, 0:MDV + 1], ALU.mult, ALU.add,
                                        [("ps", bU), "DEC", "C0f", "C0b"], ["C0f"])
                                P.dma("pool", dr["Cs"].ap()[:, h].rearrange("s k v -> k s v"), C0f[:, :, 0:MDV], r=["C0f"], is_output=True)
                                CP("pool", AP(NS, h, [[128, 128], [8, 16], [1, 1]]), C0f[:, :, MDV:MDV + 1], ["C0f"], ["NS"])
                            TS("dve", sc[:, 7:8], hn[:, MDV:MDV + 1], -1.0, None, ALU.mult, None, hnk, ["sc"])
                            TT("dve", sc[:, 7:8], sc[:, 7:8], hn[:, MDV:MDV + 1], ALU.max, hnk + ["sc"], ["sc"])
                            TT("dve", sc[:, 0:1], sc[:, 7:8], QT[:, t, 3, h:h + 1], ALU.max, ["sc", "QT"], ["sc"])
                            P.op("dve", lambda e: e.reciprocal(out=sc[:, 1:2], in_=sc[:, 0:1]), ["sc"], ["sc"])
                            ACT(intra[:, 0:MDV], hn[:, 0:MDV], AF.Square, hnk, ["intra", "sc"], accum=sc[:, 2:3])
                            TT("dve", sc[:, 3:4], sc[:, 1:2], sc[:, 1:2], ALU.mult, ["sc"], ["sc"])
                            TT("dve", sc[:, 3:4], sc[:, 3:4], sc[:, 2:3], ALU.mult, ["sc"], ["sc"])
                            ACT(sc[:, 4:5], sc[:, 3:4], AF.Sqrt, ["sc", "epsb"], ["sc"], scale=1.0 / MDV, bias=epsb[:, 0:1])
                            P.op("dve", lambda e: e.reciprocal(out=sc[:, 5:6], in_=sc[:, 4:5]), ["sc"], ["sc"])
                            TT("dve", sc[:, 6:7], sc[:, 5:6], sc[:, 1:2], ALU.mult, ["sc"], ["sc"])
                            STT(HO[:], hn[:, 0:MDV], sc[:, 6:7], ogs[:, t, :], ALU.mult, ALU.mult, hnk + ["sc", "ogs"], ["HO"])
                            b = PS.get()
                            for j in range(2):
                                TR(PS.bf16(b)[:, j * 128:(j + 1) * 128], HO[:, j * 128:(j + 1) * 128], idb[:], ["HO", "idb"], [("ps", b)])
                            TT("dve", YT[:, 2 * h:2 * h + 2, t0_:t0_ + 128], PS.bf16(b)[:, 0:256].rearrange("p (a b) -> p a b", a=2),
                               AP(FMV, V_MNW * 16 + 2 * h, [[NVX * 16, 128], [1, 2], [0, 128]]), ALU.mult, [("ps", b), "FMV"], ["YT"])
                def outcons(t, c0, n, b):
                    TT("dve", H[t][:, c0:c0 + n], PS.f32(b)[:, 0:n], H[t][:, c0:c0 + n], ALU.add, [("ps", b), ("H", t)], [("H", t)])
                mm_tm("m_wout", None, 0, D, nt, outcons, lhs_fn=lambda kc, t: YT[:, kc, t * 128:(t + 1) * 128], keys_r=["YT"])
                if kind == "s":
                    b = PS.get()
                    TR(PS.f32(b)[:, 0:128], NS[:], idf[:], ["NS", "idf"], [("ps", b)])
                    CP("dve", n0l[:], PS.f32(b)[:, 0:128], [("ps", b)], ["n0l"])
                    P.dma("pool", dr["ns"].ap(), n0l[:], r=["n0l"], is_output=True)
                elif bi == SEQ // TB - 1:
                    P.dma("pool", dr["Cp"].ap().rearrange("h k v -> k h v"), Cst[:, :, 0:MDV], r=[("Cst", h) for h in range(MH)], is_output=True)
                    b = PS.get()
                    CP("dve", hn[:, 0:MH], Cst[:, :, MDV], [("Cst", h) for h in range(MH)], [("hn", 0), ("hn", 1)])
                    TR(PS.f32(b)[0:MH, 0:128], hn[:, 0:MH], idf[:], [("hn", 0), ("hn", 1), "idf"], [("ps", b)])
                    CP("dve", intra[0:MH, 0:128], PS.f32(b)[0:MH, 0:128], [("ps", b)], ["intra"])
                    P.dma("pool", dr["np_"].ap(), intra[0:MH, 0:128], r=["intra"], is_output=True)
                P.barrier()

        def rwkv(kind, bi, nt):
            ntok = nt * 128
            W = ntok
            W2 = min(ntok, 256)
            nhalf = ntok // W2
            L = 64 if kind == "p" else 8
            sfx = kind
            nsq = 5 if kind == "p" else 2
            norm_to_xt(V_NMIX1, nt)
            last_block = (kind == "s") or (bi == SEQ // TB - 1)
            if last_block:
                with ExitStack() as st0:
                    gt = sb("rk_gt", [128, D], F32, st0)
                    yy = sb("rk_yy", [128, D], F32, st0)
                    P.dma("sp", gt[:], AP(dr["nmix1"], 0, [[0, 128], [1, D]]), w=["rk_gt"])
                    rms_stats(nt - 1)
                    STT(yy[:], H[nt - 1][:], nst[:, 2:3], gt[:], ALU.mult, ALU.mult, [("H", nt - 1), "nst", "rk_gt"], ["rk_yy"])
                    if kind == "p":
                        P.dma("pool", dr["shp"].ap(), yy[127:128, :], r=["rk_yy"], is_output=True)
                    else:
                        P.dma("pool", dr["shs"].ap(), AP(yy, 7 * D, [[8 * D, 16], [1, D]]), r=["rk_yy"], is_output=True)
                    P.barrier()
            with ExitStack() as st:
                YT = sb("rk_YT", [128, KC, W], BF16, st)
                w2b = sb("rk_w2b", [96, D], BF16, st)
                a2b = sb("rk_a2b", [96, D], BF16, st)
                g2b = sb("rk_g2b", [128, 2, D], BF16, st)
                h1w = sb("rk_h1w", [96, W], BF16, st)
                h1a = sb("rk_h1a", [96, W], BF16, st)
                h1g = sb("rk_h1g", [128, 2, W], BF16, st)
                fm = {nm: sb("rk_f_" + nm, [128, W2], F32, st) for nm in
                      ("a", "gg", "lwn", "t1", "t2", "bonus", "e1", "e2", "e3")}
                fm["cum"] = fm["a"]
                fm["e4"] = fm["e3"]
                fmW2 = [{nm: sb("rk_fw_" + nm, [128, W], F32, st) for nm in ("r", "k", "v")} for _ in range(2)]
                rst = sb("rk_rst", [128, W2], F32, st)
                mu2 = sb("rk_mu2", [128, 256], F32, st)
                msl = sb("rk_msl", [128, 128], F32, st)
                CHD = BF16 if cfg.get("ch_bf16", True) else F32
                NTL = W2 // 128
                NU = 2 * NTL
                TM4 = [sb("rk_TM4", [128, 4, 128], CHD, st) for _ in range(NTL)]
                RtZ = [sb("rk_RtZ", [128, 2, 128], CHD, st) for _ in range(NTL)]
                VZ = [sb("rk_VZ", [128, 2, 128], CHD, st) for _ in range(NTL)]
                UZ = [sb("rk_UZ", [128, 2, 128], CHD, st) for _ in range(NTL)]
                M1T = [sb("rk_M1T", [128, 128], CHD, st) for _ in range(NTL)]
                Zs = [sb("rk_Zs", [128, 2, 64], F32, st) for _ in range(NTL)]
                NG4 = [sb("rk_NG4", [128, 2, 256], CHD, st) for _ in range(NU)]
                PQ = [[sb("rk_PQ", [128, 2, 128], CHD, st) for _ in range(2)] for _ in range(NU)]
                Tb = [[sb("rk_T", [128, 128], CHD, st) for _ in range(2)] for _ in range(NU)]
                AVs = [sb("rk_AVs", [128, 64], CHD, st) for _ in range(NU)]
                STDb = sb("rk_STDb", [128, NG, 128], CHD, st)
                fmc = {nm: sb("rk_fc_" + nm, [128, W2], CHD, st) for nm in ("Bt", "Kt", "Bh", "Kh", "vb")}
                ARc = sb("rk_ARc", [128, 2, W2], CHD, st)
                Ytm2 = sb("rk_Ytm2", [128, NTL, 128], F32, st)
                ysq2 = sb("rk_ysq2", [128, NTL, 128], F32, st)
                gst = sb("rk_gst", [128, 32], F32, st)
                yfm = sb("rk_yfm", [128, 128], F32, st)
                P.dma("sp", mu2[:], dr["c_mu2_" + sfx].ap(), w=["mu2"])
                P.dma("sp", msl[:], dr["c_msl_" + sfx].ap(), w=["msl"])
                if kind == "p":
                    P.dma("sp", rst[:], dr["c_rst64"].ap()[:, 0:W2], w=["rst"])
                else:
                    P.dma("sp", rst[:], dr["c_rst8"].ap()[:, 0:W2], w=["rst"])
                for ztl, zk in ((UZ, "UZ"), (VZ, "VZ"), (RtZ, "RtZ")):
                    for tl_, zt in enumerate(ztl):
                        MSET("pool", zt[:], 0.0, [(zk, tl_)])
                if kind == "p":
                    CP("act", STDb[:], STD[:].rearrange("p g j v -> p g (j v)"), [("STD", g_) for g_ in range(NG)], [("STDb", g_) for g_ in range(NG)])
                zkeys = {}
                if kind == "s":
                    XP = sb("rk_XP", [128, KC, 128], BF16, st)
                    seqm = sb("rk_seqm", [128, 16, 128], F32, st)
                    rowm = sb("rk_rowm", [128, 16], F32, st)
                    M1TZ = sb("rk_M1TZ", [128, 16, 128], CHD, st)
                    S0Db = sb("rk_S0Db", [128, 16, 128], CHD, st)
                    RtZs = M1TZ
                    BhZ = sb("rk_BhZ", [128, 16, 128], CHD, st)
                    KhZ = sb("rk_KhZ", [128, 16, 128], CHD, st)
                    S0l = H[1][0:64, :].rearrange("p (s k) -> p s k", s=16)
                    S0D = H[2][:, :].rearrange("p (s j v) -> p s j v", s=16, j=2)
                    Scomp = H[3][:, 0:1024].rearrange("p (s v) -> p s v", s=16)
                    P.dma("sp", seqm[:], dr["c_seqmask_s"].ap(), w=["seqm"])
                    P.dma("sp", rowm[:], dr["c_rowmask_s"].ap(), w=["rowm"])
                    MSET("pool", S0D[:], 0.0, ["S0D"])
                    si = wctr["s"] % NWS
                    wctr["s"] += 1
                    shl = wst[si][0:16, 0:D]
                    P.dma("sp", shl, dr["rsh"].ap(), w=[("wst", si)])
                    b = PS.get()
                    for kc in range(KC):
                        TR(PS.f32(b)[:, kc * 16:(kc + 1) * 16], shl[:, kc * 128:(kc + 1) * 128], idf[0:16, 0:16], [("wst", si), "idf"], [("ps", b)])
                    CP("dve", AP(XP, 0, [[KC * 128, 128], [128, KC], [8, 16]]), PS.f32(b)[:, 0:256].rearrange("p (a b) -> p a b", a=KC),
                       [("ps", b)], ["XP"])
                    for kc in range(KC):
                        CP("pool", AP(XP, kc * 128 + 1, [[KC * 128, 128], [8, 16], [1, 7]]),
                           AP(XT, kc * (TB + 1) + 1, [[KC * (TB + 1), 128], [8, 16], [1, 7]]), ["XT"], ["XP"])
                    cur_fn = lambda kc, c0, n: XT[:, kc, 1 + c0:1 + c0 + n]
                    prev_fn = lambda kc, c0, n: XP[:, kc, c0:c0 + n]
                    xkeys = ["XT", "XP"]
                else:
                    CP("pool", XT[:, :, 0:1], xlast[:].rearrange("p (a b) -> p a b", b=1), ["xlast"], ["XT"])
                    CP("pool", xlast[:].rearrange("p (a b) -> p a b", b=1), XT[:, :, ntok:ntok + 1], ["XT"], ["xlast"])
                    cur_fn = lambda kc, c0, n: XT[:, kc, 1 + c0:1 + c0 + n]
                    prev_fn = lambda kc, c0, n: XT[:, kc, c0:c0 + n]
                    xkeys = ["XT"]

                def rhs2(c0, n):
                    return lambda vi, kc: (cur_fn if vi == 0 else prev_fn)(kc, c0, n)

                def scl(c):
                    return [(V_1MU0 + c) * 16, (V_MU0 + c) * 16]

                def wload_to(dst, name, r0, nrows):
                    src, np_, kcn = wsrc(name, None, r0, nrows, 0, D)
                    wv, key = wpanel(src, np_, kcn, D if kcn == 1 else D)
                    return wv, key
                si = wctr["s"] % NWS
                wctr["s"] += 1
                P.dma("sp", wst[si][0:96, 0:D], dr["r_w2"].ap(), w=[("wst", si)])
                CP("pool", w2b[:], wst[si][0:96, 0:D], [("wst", si)], ["w2b"])
                si = wctr["s"] % NWS
                wctr["s"] += 1
                P.dma("sp", wst[si][0:96, 0:D], dr["r_a2"].ap(), w=[("wst", si)])
                CP("pool", a2b[:], wst[si][0:96, 0:D], [("wst", si)], ["a2b"])
                si = wctr["s"] % NWS
                wctr["s"] += 1
                P.dma("sp", wst[si][:, 0:2 * D].rearrange("p (k n) -> p k n", k=2), dr["r_g2"].ap().rearrange("(k p) n -> p k n", p=128), w=[("wst", si)])
                CP("pool", g2b[:], wst[si][:, 0:2 * D].rearrange("p (k n) -> p k n", k=2), [("wst", si)], ["g2b"])

                def c_h1w(m, mw, b):
                    ACT(h1w[:, 0:ntok], PS.f32(b)[0:96, 0:ntok], AF.Tanh, [("ps", b)], ["h1w"])
                mm_fm("r_w1", None, 0, 96, ntok, c_h1w, rhs_fn=rhs2(0, ntok), scales=scl(1), keys_r=xkeys)

                def c_h1a(m, mw, b):
                    CP("act", h1a[:, 0:ntok], PS.f32(b)[0:96, 0:ntok], [("ps", b)], ["h1a"])
                mm_fm("r_a1", None, 0, 96, ntok, c_h1a, rhs_fn=rhs2(0, ntok), scales=scl(4), keys_r=xkeys)

                def c_h1g(m, mw, b):
                    ACT(h1g[:, m, 0:ntok], PS.f32(b)[:, 0:ntok], AF.Sigmoid, [("ps", b)], ["h1g"])
                mm_fm("r_g1", None, 0, 256, ntok, c_h1g, rhs_fn=rhs2(0, ntok), scales=scl(5), keys_r=xkeys)

                for g in range(NG):
                    gc = slice(g * 128, (g + 1) * 128)
                    def proj_slice(gg, nm):
                        wn, c = {"r": ("r_wr", 0), "k": ("r_wk", 2), "v": ("r_wv", 3)}[nm]
                        par_ = gg % 2
                        src, np_, kcn = wsrc(wn, None, 0, D, gg * 128, 128)
                        wv, wkey = wpanel(src, np_, kcn, 128, scl(c))
                        b_ = PS.get(hold=True)
                        i = 0
                        for vi in range(2):
                            for kc in range(KC):
                                MM(PS.f32(b_)[:, 0:W], wv[vi][:, kc, :], (cur_fn if vi == 0 else prev_fn)(kc, 0, W),
                                   i == 0, i == 2 * KC - 1, [wkey] + xkeys, [("ps", b_)])
                                i += 1

                        def evac():
                            PS.release(b_)
                            CP("act" if nm != "k" else "dve", fmW2[par_][nm][:], PS.f32(b_)[:, 0:W], [("ps", b_)], ["%s%d" % (nm, par_)])
                        return evac
                    if g == 0:
                        for nm in ("r", "k", "v"):
                            proj_slice(0, nm)()
                    par = g % 2
                    kR, kK, kV = "r%d" % par, "k%d" % par, "v%d" % par
                    fmW = fmW2[par]
                    look = (g + 1 < NG)
                    for hb in range(nhalf):
                        c0 = hb * W2
                        fm["r"] = fmW["r"][:, c0:c0 + W2]
                        fm["k"] = fmW["k"][:, c0:c0 + W2]
                        fm["v"] = fmW["v"][:, c0:c0 + W2]
                        b = PS.get()
                        MM(PS.f32(b)[:, 0:W2], w2b[:, gc], h1w[:, c0:c0 + W2], True, True, ["w2b", "h1w"], [("ps", b)])
                        ACT(fm["lwn"][:], PS.f32(b)[:, 0:W2], AF.Sigmoid, [("ps", b), "FMV"], ["lwn"], bias=fmv(V_W0, g))
                        b = PS.get()
                        MM(PS.f32(b)[:, 0:W2], a2b[:, gc], h1a[:, c0:c0 + W2], True, True, ["a2b", "h1a"], [("ps", b)])
                        ACT(fm["a"][:], PS.f32(b)[:, 0:W2], AF.Sigmoid, [("ps", b), "FMV"], ["a"], bias=fmv(V_A0, g))
                        b = PS.get()
                        for j in range(2):
                            MM(PS.f32(b)[:, 0:W2], g2b[:, j, gc], h1g[:, j, c0:c0 + W2], j == 0, j == 1, ["g2b", "h1g"], [("ps", b)])
                        CP("act", fm["gg"][:], PS.f32(b)[:, 0:W2], [("ps", b)], ["gg"])
                        TS("dve", fm["t1"][:], fm["k"], fmv(V_KK, g), None, ALU.mult, None, [kK, "FMV"], ["t1"])
                        ACT(fm["t2"][:], fm["k"], AF.Square, [kK, "FMV"], ["t2"], scale=fmv(V_KK, g))
                        b = PS.get()
                        MM(PS.f32(b)[:, 0:W2], bonesf[:], fm["t2"][:], True, True, ["bonesf", "t2"], [("ps", b)])
                        evacs = []
                        if look and hb == nhalf - 1:
                            evacs.append(proj_slice(g + 1, "r"))
                        ACT(fm["t2"][:], PS.f32(b)[:, 0:W2], AF.Ln, [("ps", b), "epsb"], ["t2"], bias=epsb[:, 3:4])
                        ACT(fm["t2"][:], fm["t2"][:], AF.Exp, ["t2"], ["t2"], scale=-0.5)
                        TT("dve", fm["t1"][:], fm["t1"][:], fm["t2"][:], ALU.mult, ["t1", "t2"], ["t1"])
                        TS("dve", fm["t2"][:], fm["a"][:], fmv(V_KA, g), fmv(V_1KA, g), ALU.mult, ALU.add, ["a", "FMV"], ["t2"])
                        TT("dve", fm["k"], fm["k"], fm["t2"][:], ALU.mult, [kK, "t2"], [kK])
                        TT("dve", fm["t2"][:], fm["t1"][:], fm["a"][:], ALU.mult, ["t1", "a"], ["t2"])
                        STT(fm["e3"][:], fm["r"], fmv(V_RK, g), fm["k"], ALU.mult, ALU.mult, [kR, kK, "FMV"], ["e3"])
                        b = PS.get()
                        MM(PS.f32(b)[:, 0:W2], bonesf[:], fm["e3"][:], True, True, ["bonesf", "e3"], [("ps", b)])
                        if look and hb == nhalf - 1:
                            evacs.append(proj_slice(g + 1, "k"))
                        TT("dve", fm["bonus"][:], PS.f32(b)[:, 0:W2], fm["v"], ALU.mult, [("ps", b), kV], ["bonus"])
                        P.op("dve", lambda e: e.tensor_tensor_scan(out=fm["cum"][:], data0=rst[:], data1=fm["lwn"][:], initial=0.0,
                                                                   op0=ALU.mult, op1=ALU.add), ["rst", "lwn"], ["a"])
                        ACT(fm["e1"][:], fm["cum"][:], AF.Exp, ["a"], ["e1"], scale=-EXPM05)
                        ACT(fm["e2"][:], fm["cum"][:], AF.Exp, ["a"], ["e2"], scale=EXPM05)
                        TT("dve", fm["e3"][:], fm["cum"][:], fm["lwn"][:], ALU.subtract, ["a", "lwn"], ["e3"])
                        ACT(fm["e3"][:], fm["e3"][:], AF.Exp, ["e3"], ["e3"], scale=-EXPM05)
                        STT(ARc[:, 0, :], fm["t1"][:], -1.0, fm["e3"][:], ALU.mult, ALU.mult, ["t1", "e3", "AR"], ["AR"])
                        nch2 = W2 // L
                        TT("pool", fm["lwn"][:].rearrange("p (c l) -> p c l", l=L), AP(fm["cum"], L - 1, [[W2, 128], [L, nch2], [0, L]]),
                           fm["cum"][:].rearrange("p (c l) -> p c l", l=L), ALU.subtract, ["a", "lwn"], ["lwn"])
                        ACT(fm["e4"][:], fm["lwn"][:], AF.Exp, ["lwn"], ["e3"], scale=-EXPM05)
                        TT("pool", ARc[:, 1, :], fm["r"], fm["e1"][:], ALU.mult, [kR, "e1"], ["AR"])
                        TT("pool", fmc["Bt"][:], fm["t2"][:], fm["e2"][:], ALU.mult, ["t2", "e2"], ["Bt"])
                        TT("dve", fmc["Kt"][:], fm["k"], fm["e2"][:], ALU.mult, [kK, "e2"], ["Kt"])
                        TT("pool", fmc["Bh"][:], fm["t2"][:], fm["e4"][:], ALU.mult, ["t2", "e3"], ["Bh"])
                        TT("dve", fmc["Kh"][:], fm["k"], fm["e4"][:], ALU.mult, [kK, "e3"], ["Kh"])
                        CP("act", fmc["vb"][:], fm["v"], [kV], ["vb"])
                        if look and hb == nhalf - 1:
                            evacs.append(proj_slice(g + 1, "v"))
                        for ev_ in evacs:
                            ev_()
                        if kind == "s":
                            for j in range(2):
                                P.dma("sp", S0l[:, :, j * 64:(j + 1) * 64],
                                      dr["rS"].ap()[:, 2 * g + j].rearrange("s v k -> v s k"), w=["S0l"])
                            for s8 in range(2):
                                b = PS.get()
                                for ss_ in range(8):
                                    s = s8 * 8 + ss_
                                    TR(PS.f32(b)[:, ss_ * 64:(ss_ + 1) * 64], S0l[:, s, :], idf[0:64, 0:64], ["S0l", "idf"], [("ps", b)])
                                for j in range(2):
                                    pb = 64 * j
                                    CP("dve", S0D[pb:pb + 64, s8 * 8:s8 * 8 + 8, j, :],
                                       PS.f32(b)[pb:pb + 64, 0:512].rearrange("p (a b) -> p a b", a=8), [("ps", b)], ["S0D"])
                        ntl = W2 // 128
                        units = [(tl, j) for tl in range(ntl) for j in range(2)]
                        bYs = []
                        for tl in range(ntl):
                            tc0 = tl * 128
                            tsl = slice(tc0, tc0 + 128)
                            b = PS.get()
                            pv = PS.bf16(b) if CHD == BF16 else PS.f32(b)
                            idc = idb if CHD == BF16 else idf
                            for qi, src_ in enumerate((fmc["Bh"][:, tsl], fmc["Kh"][:, tsl], ARc[:, 0, tsl], fmc["vb"][:, tsl])):
                                TR(pv[:, qi * 128:(qi + 1) * 128], src_, idc[:], ["Bh", "Kh", "AR", "vb", "idb", "idf"], [("ps", b)])
                            CP("act", TM4[tl][:].rearrange("p a b -> p (a b)"), pv[:, 0:512], [("ps", b)], [("TM4", tl)])
                            if kind == "p":
                                CP("pool", AP(RtZ[tl], 0, [[256, 128], [128 + 64, 2], [1, 64]]),
                                   AP(ARc, W2 + tc0, [[2 * W2, 128], [64, 2], [1, 64]]), ["AR"], [("RtZ", tl)])
                                for c in range(2):
                                    CP("dve", VZ[tl][64 * c:64 * c + 64, c, :], TM4[tl][64 * c:64 * c + 64, 3, :], [("TM4", tl)], [("VZ", tl)])
                            bYs.append(PS.get(hold=True))
                        Pc, Qc, Tc = {}, {}, {}
                        for ui, (tl, j) in enumerate(units):
                            tc0 = tl * 128
                            tsl = slice(tc0, tc0 + 128)
                            pb = 64 * j
                            arv = ARc[pb:pb + 64, :, tsl]
                            b12 = PS.get()
                            MM(PS.f32(b12)[:, 0:256], fmc["Bt"][pb:pb + 64, tsl], arv, True, True, ["Bt", "AR"], [("ps", b12)], skip=True)
                            MM(PS.f32(b12)[:, 256:512], fmc["Kt"][pb:pb + 64, tsl], arv, False, True, ["Kt", "AR"], [("ps", b12)], skip=True)
                            b3 = PS.get()
                            MM(PS.f32(b3)[:, 0:128], ARc[pb:pb + 64, 0, tsl], fmc["Bt"][pb:pb + 64, tsl], True, True, ["AR", "Bt"], [("ps", b3)])
                            TT("dve", NG4[ui][:], PS.f32(b12)[:, 0:512].rearrange("p (a b) -> p a b", a=2),
                               AP(mu2, 0, [[256, 128], [0, 2], [1, 256]]), ALU.mult, [("ps", b12), "mu2"], [("NG4", ui)])
                            TT("dve", PQ[ui][0][:, 0, :], PS.f32(b3)[:, 0:128], msl[:], ALU.mult, [("ps", b3), "msl"], [("PQ", ui, 0)])
                            TT("pool", Tb[ui][0][:], NG4[ui][:, 0, 0:128], idc[:], ALU.add, [("NG4", ui), "idb", "idf"], [("T", ui, 0)])
                            Pc[ui] = (PQ[ui][0][:, 0, :], ("PQ", ui, 0))
                            Qc[ui] = (NG4[ui][:, 0, 0:128], ("NG4", ui))
                            Tc[ui] = 0
                        for k_ in range(nsq + 1):
                            nxt = (k_ + 1) % 2
                            do_sq = k_ < nsq
                            last_sq = (k_ == nsq - 1)
                            do_t = k_ >= 1
                            bpq = {}
                            for ui in range(len(units)):
                                bp = PS.get()
                                bpq[ui] = bp
                                if do_sq:
                                    MM(PS.f32(bp)[:, 0:128], Qc[ui][0], Pc[ui][0], True, True, [Qc[ui][1], Pc[ui][1]], [("ps", bp)], skip=True)
                                    if not last_sq:
                                        MM(PS.f32(bp)[:, 128:256], Pc[ui][0], Qc[ui][0], True, True, [Qc[ui][1], Pc[ui][1]], [("ps", bp)], skip=True)
                                if do_t:
                                    MM(PS.f32(bp)[:, 256:384], Pc[ui][0], Tb[ui][Tc[ui]][:], True, True, [Pc[ui][1], ("T", ui, Tc[ui])], [("ps", bp)], skip=True)
                            for ui in range(len(units)):
                                bp = bpq[ui]
                                eng = "act" if ui % 2 == 0 else "dve"
                                if do_t:
                                    tn = 1 - Tc[ui]
                                    TT("dve", Tb[ui][tn][:], PS.f32(bp)[:, 256:384], Tb[ui][Tc[ui]][:], ALU.add, [("ps", bp), ("T", ui, Tc[ui])], [("T", ui, tn)])
                                    Tc[ui] = tn
                                if do_sq:
                                    if not last_sq:
                                        CP(eng, PQ[ui][nxt][:].rearrange("p a b -> p (a b)"), PS.f32(bp)[:, 0:256], [("ps", bp)], [("PQ", ui, nxt)])
                                    else:
                                        CP(eng, PQ[ui][nxt][:, 0, :], PS.f32(bp)[:, 0:128], [("ps", bp)], [("PQ", ui, nxt)])
                                    Pc[ui] = (PQ[ui][nxt][:, 0, :], ("PQ", ui, nxt))
                                    Qc[ui] = (PQ[ui][nxt][:, 1, :], ("PQ", ui, nxt))
                        bms, bas = {}, {}
                        for ui, (tl, j) in enumerate(units):
                            pb = 64 * j
                            TTj, tk = Tb[ui][Tc[ui]], ("T", ui, Tc[ui])
                            bm = PS.get()
                            bms[ui] = bm
                            MM(PS.f32(bm)[:, 0:128], TM4[tl][:, 2, :], TTj[:], True, True, [("TM4", tl), tk], [("ps", bm)], skip=True)
                            MM(PS.f32(bm)[:, 128:192], NG4[ui][:, 1, 0:128], TM4[tl][:, 3, pb:pb + 64], True, True, [("NG4", ui), ("TM4", tl)], [("ps", bm)], skip=True)
                        for ui, (tl, j) in enumerate(units):
                            pb = 64 * j
                            bm = bms[ui]
                            CP("act", M1T[tl][pb:pb + 64, :], PS.f32(bm)[pb:pb + 64, 0:128], [("ps", bm)], [("M1T", tl)])
                            CP("dve", AVs[ui][:], PS.f32(bm)[:, 128:192], [("ps", bm)], [("AVs", ui)])
                        for ui, (tl, j) in enumerate(units):
                            pb = 64 * j
                            TTj, tk = Tb[ui][Tc[ui]], ("T", ui, Tc[ui])
                            bz = PS.get()
                            bas[ui] = bz
                            MM(PS.f32(bz)[:, 0:64], TTj[:], AVs[ui][:], True, True, [tk, ("AVs", ui)], [("ps", bz)])
                            MM(PS.f32(bYs[tl])[:, pb:pb + 64], NG4[ui][:, 1, 128:256], TM4[tl][:, 3, pb:pb + 64], j == 0, False,
                               [("NG4", ui), ("TM4", tl)], [("ps", bYs[tl])], skip=True)
                        for ui, (tl, j) in enumerate(units):
                            CP("act" if ui % 2 else "dve", Zs[tl][:, j, :], PS.f32(bas[ui])[:, 0:64], [("ps", bas[ui])], [("Zs", tl)])
                        for tl in range(ntl):
                            tc0 = tl * 128
                            tsl = slice(tc0, tc0 + 128)
                            bY = bYs[tl]
                            u0 = 2 * tl
                            if kind == "p":
                                for c in range(2):
                                    r0 = 64 * c
                                    bu = PS.get()
                                    MM(PS.f32(bu)[:, 0:128], M1T[tl][:], STDb[:, g, :], True, True, [("M1T", tl), ("STDb", g)], [("ps", bu)])
                                    TT("dve", UZ[tl][r0:r0 + 64, c, :], PS.f32(bu)[r0:r0 + 64, 0:128], Zs[tl][r0:r0 + 64, :, :].rearrange("p a b -> p (a b)"),
                                       ALU.add, [("ps", bu), ("Zs", tl)], [("UZ", tl)])
                                    MM(PS.f32(bY)[:, 0:128], RtZ[tl][:, c, :], STDb[:, g, :], False, False,
                                       [("RtZ", tl), ("STDb", g)], [("ps", bY)], skip=True)
                                    bs = PS.get()
                                    MM(PS.f32(bs)[:, 0:128], TM4[tl][:, 0, :], UZ[tl][:, c, :], True, False, [("TM4", tl), ("UZ", tl)], [("ps", bs)])
                                    MM(PS.f32(bs)[:, 0:128], TM4[tl][:, 1, :], VZ[tl][:, c, :], False, True, [("TM4", tl), ("VZ", tl)], [("ps", bs)])
                                    for j in range(2):
                                        pb = 64 * j
                                        wl_ap = fm["e1"][pb:pb + 64, tc0 + r0 + 63:tc0 + r0 + 64]
                                        STT(STDb[pb:pb + 64, g, pb:pb + 64], STD[pb:pb + 64, g, j, :], wl_ap, PS.f32(bs)[pb:pb + 64, pb:pb + 64],
                                            ALU.mult, ALU.add, [("ps", bs), "e1", ("STD", g)], [("STDb", g)])
                                    for j in range(2):
                                        pb = 64 * j
                                        wl_ap = fm["e1"][pb:pb + 64, tc0 + r0 + 63:tc0 + r0 + 64]
                                        STT(STD[pb:pb + 64, g, j, :], STD[pb:pb + 64, g, j, :], wl_ap, PS.f32(bs)[pb:pb + 64, pb:pb + 64],
                                            ALU.mult, ALU.add, [("ps", bs), "e1", ("STD", g)], [("STD", g)])
                                for j in range(2):
                                    pb = 64 * j
                                    for c in range(2):
                                        MM(PS.f32(bY)[:, pb:pb + 64], NG4[u0 + j][:, 0, 128:256], UZ[tl][:, c, pb:pb + 64], False, (j == 1 and c == 1),
                                           [("NG4", u0 + j), ("UZ", tl)], [("ps", bY)], skip=True)
                            else:
                                TT("dve", M1TZ[:], AP(M1T[tl], 0, [[128, 128], [0, 16], [1, 128]]), seqm[:], ALU.mult, [("M1T", tl), "seqm"], ["M1TZ"])
                                TT("dve", BhZ[:], AP(TM4[tl], 0, [[512, 128], [0, 16], [1, 128]]), AP(rowm, 0, [[16, 128], [1, 16], [0, 128]]),
                                   ALU.mult, [("TM4", tl), "rowm"], ["BhZ"])
                                TT("pool", KhZ[:], AP(TM4[tl], 128, [[512, 128], [0, 16], [1, 128]]), AP(rowm, 0, [[16, 128], [1, 16], [0, 128]]),
                                   ALU.mult, [("TM4", tl), "rowm"], ["KhZ"])
                                CP("act", S0Db[:], S0D[:].rearrange("p s j v -> p s (j v)"), ["S0D"], ["S0Db"])
                                bu = PS.get()
                                for s in range(NSEQ):
                                    MM(PS.f32(bu)[:, 0:128], M1TZ[:, s, :], S0Db[:, s, :], s == 0, s == NSEQ - 1, ["M1TZ", "S0Db"], [("ps", bu)])
                                TT("dve", UZ[tl][:, 0, :], PS.f32(bu)[:, 0:128], Zs[tl][:].rearrange("p a b -> p (a b)"), ALU.add, [("ps", bu), ("Zs", tl)], [("UZ", tl)])
                                TT("pool", RtZs[:], AP(ARc, W2 + tc0, [[2 * W2, 128], [0, 16], [1, 128]]), seqm[:], ALU.mult, ["AR", "seqm"], ["M1TZ"])
                                for s in range(NSEQ):
                                    MM(PS.f32(bY)[:, 0:128], RtZs[:, s, :], S0Db[:, s, :], False, False, ["M1TZ", "S0Db"], [("ps", bY)], skip=True)
                                for j in range(2):
                                    pb = 64 * j
                                    MM(PS.f32(bY)[:, pb:pb + 64], NG4[u0 + j][:, 0, 128:256], UZ[tl][:, 0, pb:pb + 64], False, j == 1,
                                       [("NG4", u0 + j), ("UZ", tl)], [("ps", bY)], skip=True)
                                for s in range(NSEQ):
                                    bs = PS.get()
                                    MM(PS.f32(bs)[:, 0:128], BhZ[:, s, :], UZ[tl][:, 0, :], True, False, ["BhZ", ("UZ", tl)], [("ps", bs)])
                                    MM(PS.f32(bs)[:, 0:128], KhZ[:, s, :], TM4[tl][:, 3, :], False, True, ["KhZ", ("TM4", tl)], [("ps", bs)])
                                    for j in range(2):
                                        pb = 64 * j
                                        wl_ap = fm["e1"][pb:pb + 64, s * 8 + 7:s * 8 + 8]
                                        STT(S0D[pb:pb + 64, s, j, :], S0D[pb:pb + 64, s, j, :], wl_ap, PS.f32(bs)[pb:pb + 64, pb:pb + 64],
                                            ALU.mult, ALU.add, [("ps", bs), "e1", "S0D", "S0Db"], ["S0D"])
                                TT("pool", Scomp[:], S0D[:, :, 0, :], S0D[:, :, 1, :], ALU.add, ["S0D"], ["Scomp"])
                                for s4 in range(4):
                                    b = PS.get()
                                    for ss_ in range(4):
                                        s = s4 * 4 + ss_
                                        TR(PS.f32(b)[0:64, ss_ * 128:(ss_ + 1) * 128], Scomp[:, s, :], idf[:], ["Scomp", "idf"], [("ps", b)])
                                    CP("act", S0l[:, s4 * 4:s4 * 4 + 4, :].rearrange("p a b -> p (a b)"), PS.f32(b)[0:64, 0:512], [("ps", b)], ["S0l"])
                                for j in range(2):
                                    P.dma("pool", dr["Ss"].ap()[:, 2 * g + j].rearrange("s v k -> v s k"),
                                          S0l[:, :, j * 64:(j + 1) * 64], r=["S0l"], is_output=True)
                            PS.release(bY)
                            CP("act", Ytm2[:, tl, :], PS.f32(bY)[:, 0:128], [("ps", bY)], ["Ytm"])
                        nq = 2 * ntl
                        GW = 32
                        yv = Ytm2[:, 0:ntl, :].rearrange("p t (a b) -> p (t a) b", a=2)
                        P.op("dve", lambda e: e.tensor_reduce(out=gst[:, 0:nq], in_=yv, axis=AX.X, op=ALU.add), ["Ytm"], ["gst"])
                        ACT(ysq2[:, 0:ntl, :], Ytm2[:, 0:ntl, :], AF.Square, ["Ytm"], ["ysq2"])
                        P.op("dve", lambda e: e.tensor_reduce(out=gst[:, nq:2 * nq], in_=ysq2[:, 0:ntl, :].rearrange("p t (a b) -> p (t a) b", a=2),
                                                              axis=AX.X, op=ALU.add), ["ysq2"], ["gst"])
                        TS("dve", gst[:, 2 * nq:3 * nq], gst[:, 0:nq], 1.0 / RN, None, ALU.mult, None, ["gst"], ["gst"])
                        TT("dve", gst[:, 3 * nq:4 * nq], gst[:, 2 * nq:3 * nq], gst[:, 2 * nq:3 * nq], ALU.mult, ["gst"], ["gst"])
                        STT(gst[:, 4 * nq:5 * nq], gst[:, nq:2 * nq], 1.0 / RN, gst[:, 3 * nq:4 * nq], ALU.mult, ALU.subtract, ["gst"], ["gst"])
                        TS("dve", gst[:, 4 * nq:5 * nq], gst[:, 4 * nq:5 * nq], 0.0, None, ALU.max, None, ["gst"], ["gst"])
                        ACT(gst[:, 4 * nq:5 * nq], gst[:, 4 * nq:5 * nq], AF.Sqrt, ["gst", "epsb"], ["gst"], bias=epsb[:, 2:3])
                        P.op("dve", lambda e: e.reciprocal(out=gst[:, 5 * nq:6 * nq], in_=gst[:, 4 * nq:5 * nq]), ["gst"], ["gst"])
                        TT("dve", yv, yv, AP(gst, 2 * nq, [[GW, 128], [1, nq], [0, 64]]), ALU.subtract, ["Ytm", "gst"], ["Ytm"])
                        TT("dve", yv, yv, AP(gst, 5 * nq, [[GW, 128], [1, nq], [0, 64]]), ALU.mult, ["Ytm", "gst"], ["Ytm"])
                        for tl in range(ntl):
                            tc0 = tl * 128
                            tsl = slice(tc0, tc0 + 128)
                            b = PS.get()
                            TR(PS.f32(b)[:, 0:128], Ytm2[:, tl, :], idf[:], ["Ytm", "idf"], [("ps", b)])
                            TS("dve", yfm[:], PS.f32(b)[:, 0:128], fmv(V_LNW, g), fmv(V_LNB, g), ALU.mult, ALU.add, [("ps", b), "FMV"], ["yfm"])
                            TT("pool", yfm[:], yfm[:], fm["bonus"][:, tsl], ALU.add, ["yfm", "bonus"], ["yfm"])
                            TT("pool", YT[:, g, c0 + tc0:c0 + tc0 + 128], yfm[:], fm["gg"][:, tsl], ALU.mult, ["yfm", "gg"], ["YT"])
                def outcons(t, c0_, n, b):
                    TT("dve", H[t][:, c0_:c0_ + n], PS.f32(b)[:, 0:n], H[t][:, c0_:c0_ + n], ALU.add, [("ps", b), ("H", t)], [("H", t)])
                mm_tm("r_wo", None, 0, D, nt, outcons, lhs_fn=lambda kc, t: YT[:, kc, t * 128:(t + 1) * 128], keys_r=["YT"])
                if kind == "p" and bi == SEQ // TB - 1:
                    Sc = fm["t1"]
                    So = fm["t2"]
                    for g in range(NG):
                        TT("pool", Sc[:, 0:64], STD[:, g, 0, :], STD[:, g, 1, :], ALU.add, [("STD", g)], ["t1"])
                        b = PS.get()
                        TR(PS.f32(b)[0:64, 0:128], Sc[:, 0:64], idf[:], ["t1", "idf"], [("ps", b)])
                        CP("act", So[0:64, 0:128], PS.f32(b)[0:64, 0:128], [("ps", b)], ["t2"])
                        P.dma("pool", dr["Sp"].ap()[2 * g:2 * g + 2].rearrange("j v k -> v j k"),
                              So[0:64, 0:128].rearrange("v (j k) -> v j k", j=2), r=["t2"], is_output=True)
                P.barrier()

        for bidx_, (kind, bi) in enumerate(blocks):
            nt = TB // 128 if kind == "p" else 1
            wctr["p"] = 0
            wctr["mode"] = "store" if bidx_ == 0 else "load"
            if cfg.get("no_wcache"):
                wctr["mode"] = "none"
            load_block(kind, bi, nt)
            for layer in range(2):
                if not cfg.get("skip_mixers"):
                    if layer == 0:
                        mlstm(kind, bi, nt)
                    else:
                        rwkv(kind, bi, nt)
                for t in range(nt):
                    tap("h_mix%d_%s%d_%d" % (layer, kind, bi, t), H[t][:], [128, D], [("H", t)])
                if stop == "mix%d" % layer:
                    break
                ffn(layer, nt)
                for t in range(nt):
                    tap("h_ffn%d_%s%d_%d" % (layer, kind, bi, t), H[t][:], [128, D], [("H", t)])
                if stop == "ffn%d" % layer:
                    break
                ple(layer, kind, bi, nt)
                for t in range(nt):
                    tap("h_ple%d_%s%d_%d" % (layer, kind, bi, t), H[t][:], [128, D], [("H", t)])
                if stop == "ple%d" % layer:
                    break
            else:
                final_out(kind, bi, nt)
            P.barrier()
        P.finish()
        cfg["_stats"] = (P.n_inst, P.n_wait, dict(P.cnt))
    return nc, tap_d


def make_in_maps(inp, ncores=8):
    consts = make_consts()
    f = lambda a: np.ascontiguousarray(a, dtype=np.float32)
    vec_names = [("norm_mix", 0), ("norm_ffn", 0), ("norm_ple", 0), ("norm_mix", 1), ("norm_ffn", 1), ("norm_ple", 1)]
    vecs = [inp[n][i] for n, i in vec_names]
    vecs.append(inp["mlstm_norm_w"][0])
    for c in range(6):
        vecs.append(inp["rwkv_mu"][0, c])
    for n in ("rwkv_w0", "rwkv_a0", "rwkv_k_k", "rwkv_k_a", "rwkv_ln_w", "rwkv_ln_b"):
        vecs.append(inp[n][0])
    vecs.append(inp["rwkv_r_k"][0].reshape(-1))
    vecs = f(np.stack(vecs, 0).reshape(NV * 16, 128))
    shared = {
        "vecs": vecs, "nfin": f(inp["norm_final"].reshape(1, D)), "nmix1": f(inp["norm_mix"][1].reshape(1, D)),
        "big": f(inp["mlstm_b_igate"][0].reshape(MH, 1)), "bfg": f(inp["mlstm_b_fgate"][0].reshape(MH, 1)),
        "ffn_wg": f(inp["ffn_w_gate"]), "ffn_wu": f(inp["ffn_w_up"]), "ffn_wd": f(inp["ffn_w_down"]),
        "ple_wp": f(inp["ple_w_proj"]), "ple_wg": f(inp["ple_w_gate"]),
        "m_wq": f(inp["mlstm_w_q"][0]), "m_wk": f(inp["mlstm_w_k"][0]), "m_wv": f(inp["mlstm_w_v"][0]),
        "m_wi": f(inp["mlstm_w_igate"][0]), "m_wf": f(inp["mlstm_w_fgate"][0]),
        "m_wo": f(inp["mlstm_w_ogate"][0]), "m_wout": f(inp["mlstm_w_out"][0]),
        "r_wr": f(inp["rwkv_w_r"][0]), "r_wk": f(inp["rwkv_w_k"][0]), "r_wv": f(inp["rwkv_w_v"][0]), "r_wo": f(inp["rwkv_w_o"][0]),
        "r_w1": f(inp["rwkv_w1"][0]), "r_w2": f(inp["rwkv_w2"][0]), "r_a1": f(inp["rwkv_a1"][0]), "r_a2": f(inp["rwkv_a2"][0]),
        "r_g1": f(inp["rwkv_g1"][0]), "r_g2": f(inp["rwkv_g2"][0]),
    }
    for k, v in consts.items():
        shared["c_" + k] = f(v)
    maps = []
    for c in range(ncores):
        b = c % 4
        s0 = NSEQ * c
        m = dict(shared)
        m["xp"] = f(inp["x_prompt"][b])
        m["xs"] = f(inp["x_sample"][s0:s0 + NSEQ].reshape(128, D))
        m["pp"] = f(inp["p_prompt"][:, b])
        m["psm"] = f(inp["p_sample"][:, s0:s0 + NSEQ].reshape(2, 128, PLE))
        m["mC"] = f(inp["state_mlstm_C"][0, s0:s0 + NSEQ])
        m["mn"] = f(inp["state_mlstm_n"][0, s0:s0 + NSEQ].reshape(128, 128))
        m["mmT"] = f(inp["state_mlstm_m"][0, s0:s0 + NSEQ].T)
        m["rS"] = f(inp["state_rwkv_S"][0, s0:s0 + NSEQ])
        m["rsh"] = f(inp["state_rwkv_shift"][0, s0:s0 + NSEQ])
        maps.append(m)
    return maps


_NC_CACHE = {}


def kernel(**inputs):
    inp = {k: np.asarray(v) for k, v in inputs.items()}
    if "nc" not in _NC_CACHE:
        _NC_CACHE["nc"] = build({})[0]
    nc = _NC_CACHE["nc"]
    maps = make_in_maps(inp, 8)
    res = run_bass_kernel_spmd(nc, maps, core_ids=list(range(8)))
    R = res.results
    f = lambda a: np.ascontiguousarray(a, dtype=np.float32)
    y_prompt = f(np.stack([R[c]["yp"] for c in range(4)], 0))
    y_sample = f(np.concatenate([R[c]["ys"].reshape(NSEQ, ST_, D) for c in range(8)], 0))
    C_p = f(np.stack([R[c]["Cp"] for c in range(4)], 0)[None])
    n_p = f(np.stack([R[c]["np_"] for c in range(4)], 0)[None])
    m_p = f(np.stack([R[c]["mp"][:, 0] for c in range(4)], 0)[None])
    S_p = f(np.stack([R[c]["Sp"] for c in range(4)], 0)[None])
    sh_p = f(np.stack([R[c]["shp"][0] for c in range(4)], 0)[None])
    C_s = f(np.concatenate([R[c]["Cs"] for c in range(8)], 0)[None])
    n_s = f(np.concatenate([R[c]["ns"].reshape(NSEQ, MH, MDK) for c in range(8)], 0)[None])
    m_s = f(np.concatenate([R[c]["msT"].T for c in range(8)], 0)[None])
    S_s = f(np.concatenate([R[c]["Ss"] for c in range(8)], 0)[None])
    sh_s = f(np.concatenate([R[c]["shs"] for c in range(8)], 0)[None])
    return (y_prompt, y_sample, C_p, n_p, m_p, S_p, sh_p, C_s, n_s, m_s, S_s, sh_s)
```

```python
import math
from contextlib import ExitStack

import numpy as np
import concourse.bass as bass
import concourse.mybir as mybir
from concourse.bass_utils import run_bass_kernel_spmd

F32 = mybir.dt.float32
BF16 = mybir.dt.bfloat16
AF = mybir.ActivationFunctionType
ALU = mybir.AluOpType
AX = mybir.AxisListType

D = 2048
KC = 16
DFF = 5632
NFF = DFF // 128
PLE = 256
MH, MDK, MDV = 8, 128, 256
RH, RN = 32, 64
NG = 16
SEQ = 2048
TB = 512
NSEQ = 16
ST_ = 8
NV = 20
(V_NMIX0, V_NFFN0, V_NPLE0, V_NMIX1, V_NFFN1, V_NPLE1, V_MNW, V_MU0) = range(8)
V_W0, V_A0, V_KK, V_KA, V_LNW, V_LNB, V_RK = 13, 14, 15, 16, 17, 18, 19
V_1MU0 = 20
V_1KA = 26
NVX = 27
CH = F32
WC_TOT = 400000
WC_N = 3
EXPM05 = math.exp(-0.5)


class Prog:
    ENG = ("pe", "act", "dve", "pool", "sp")

    def __init__(self, nc, es, n_dma_sems=(12, 8, 8)):
        self.nc = nc
        self.eng = {"pe": nc.tensor, "act": nc.scalar, "dve": nc.vector,
                    "pool": nc.gpsimd, "sp": nc.sync}
        self.sem = {}
        self.cnt = {}
        for e in self.ENG:
            self.sem[e] = es.enter_context(nc.semaphore("s_" + e))
            self.cnt[e] = 0
        self.dq = {}
        for q, n in zip(("sp", "pool", "act"), n_dma_sems):
            sems = []
            for i in range(n):
                nm = "d_%s%d" % (q, i)
                self.sem[nm] = es.enter_context(nc.semaphore(nm))
                sems.append(nm)
            self.dq[q] = {"sems": sems, "i": 0}
        self.waited = {}
        self.res = {}
        self.n_inst = 0
        self.n_wait = 0
        self.out_events = []
        self.last_dma = {}
        self.relax_same = False

    def _need(self, eng, ev):
        if ev is None:
            return
        s, v = ev
        if s == eng and (eng == "pe" or self.relax_same):
            return
        k = (eng, s)
        if self.waited.get(k, 0) >= v:
            return
        self.waited[k] = v
        self.eng[eng].wait_ge(self.sem[s], v)
        self.n_wait += 1

    def _deps(self, eng, reads, writes):
        for r in reads:
            st = self.res.get(r)
            if st is not None:
                self._need(eng, st[0])
                if isinstance(r, tuple) and r[0] == "ps":
                    for s, v in list(st[1].items()):
                        if s != eng:
                            self._need(eng, (s, v))
        for w in writes:
            st = self.res.get(w)
            if st is not None:
                self._need(eng, st[0])
                for s, v in list(st[1].items()):
                    self._need(eng, (s, v))

    def _record(self, ev, reads, writes):
        s, v = ev
        for r in reads:
            st = self.res.setdefault(r, [None, {}])
            if st[1].get(s, 0) < v:
                st[1][s] = v
        for w in writes:
            self.res[w] = [ev, {}]

    def op(self, eng, fn, r=(), w=()):
        self._deps(eng, r, w)
        ins = fn(self.eng[eng])
        self.cnt[eng] += 1
        ev = (eng, self.cnt[eng])
        ins.then_inc(self.sem[eng], 1)
        self._record(ev, r, w)
        self.n_inst += 1
        return ev

    def dma(self, q, out, in_, r=(), w=(), is_output=False, **kw):
        d = self.dq[q]
        i = d["i"]
        d["i"] += 1
        n = len(d["sems"])
        sname = d["sems"][i % n]
        tgt = 16 * (i // n + 1)
        if i >= n:
            self._need(q, (sname, tgt - 16))
        self._deps(q, r, w)
        ins = self.eng[q].dma_start(out=out, in_=in_, **kw)
        ins.then_inc(self.sem[sname], 16)
        ev = (sname, tgt)
        self._record(ev, r, w)
        self.n_inst += 1
        self.last_dma[sname] = tgt
        if is_output:
            self.out_events.append(ev)
        return ev

    def barrier(self):
        evs = [(e, self.cnt[e]) for e in ("pe", "act", "dve", "pool") if self.cnt[e]]
        evs += list(self.last_dma.items())
        for e in self.ENG:
            for ev in evs:
                if ev[0] != e:
                    self._need(e, ev)
        self.res = {}

    def finish(self):
        for ev in self.out_events:
            self._need("sp", ev)
        for e in ("pe", "act", "dve", "pool"):
            if self.cnt[e]:
                self._need("sp", (e, self.cnt[e]))
        for s, v in self.last_dma.items():
            self._need("sp", (s, v))


class PsumPool:
    def __init__(self, nc, es):
        self.banks = [es.enter_context(nc.psum_tensor("psb%d" % i, [128, 512], F32)) for i in range(8)]
        self.held = set()
        self.i = 0

    def get(self, hold=False):
        for _ in range(16):
            b = self.i % 8
            self.i += 1
            if b not in self.held:
                if hold:
                    self.held.add(b)
                return b
        raise RuntimeError("no free PSUM bank")

    def release(self, b):
        self.held.discard(b)

    def f32(self, b):
        return self.banks[b]

    def bf16(self, b):
        return self.banks[b].bitcast(BF16)


def AP(t, offset, dims):
    return bass.AP(t, offset, [list(d) for d in dims])


def make_consts():
    c = {}
    c["ident"] = np.eye(128, dtype=np.float32)
    c["ones"] = np.ones((128, 128), dtype=np.float32)
    p = np.arange(128)
    c["bones"] = (p[:, None] // 64 == p[None, :] // 64).astype(np.float32)
    for L, nm in ((64, "p"), (8, "s")):
        same = (p[:, None] // L == p[None, :] // L)
        le = p[:, None] <= p[None, :]
        lt = p[:, None] < p[None, :]
        c["negmask_" + nm] = np.where(same & le, 0.0, 30000.0).astype(np.float32)
        msu = (same & lt).astype(np.float32)
        miu = (same & le).astype(np.float32)
        c["mu2_" + nm] = np.concatenate([msu, miu], axis=1)
        c["msl_" + nm] = np.ascontiguousarray(msu.T)
    sel = np.zeros((128, 2, 128), np.float32)
    sel[0, 0, :] = 1.0
    sel[64, 1, :] = 1.0
    c["sel_p"] = sel
    oh = np.zeros((128, 16), np.float32)
    oh[np.arange(16) * 8, np.arange(16)] = 1.0
    c["onehot_s"] = oh
    rm = (p[:, None] // 8 == np.arange(16)[None, :]).astype(np.float32)
    c["rowmask_s"] = rm
    sm = (np.arange(16)[:, None] == (p[None, :] // 8)).astype(np.float32)
    c["seqmask_s"] = np.ascontiguousarray(np.broadcast_to(sm[None], (128, 16, 128))).astype(np.float32)
    t = np.arange(TB)
    c["rst64"] = np.ascontiguousarray(np.broadcast_to((t % 64 != 0).astype(np.float32)[None], (128, TB)))
    c["rst8"] = np.ascontiguousarray(np.broadcast_to((p % 8 != 0).astype(np.float32)[None], (128, 128)))
    c["addrst8"] = np.ascontiguousarray(np.broadcast_to(np.where(p % 8 == 0, -1e30, 0.0).astype(np.float32)[None], (128, 128)))
    c["onesrow"] = np.ones((128, TB), np.float32)
    return c


CONST_SHAPES = {k: v.shape for k, v in make_consts().items()}

IN_SPECS = {
    "xp": [SEQ, D], "xs": [128, D], "pp": [2, SEQ, PLE], "psm": [2, 128, PLE],
    "mC": [NSEQ, MH, MDK, MDV], "mn": [128, 128], "mmT": [MH, NSEQ],
    "rS": [NSEQ, RH, RN, RN], "rsh": [NSEQ, D],
    "vecs": [NV * 16, 128], "nfin": [1, D], "nmix1": [1, D], "big": [MH, 1], "bfg": [MH, 1],
    "ffn_wg": [2, D, DFF], "ffn_wu": [2, D, DFF], "ffn_wd": [2, DFF, D],
    "ple_wp": [2, PLE, D], "ple_wg": [2, D, D],
    "m_wq": [D, MH * MDK], "m_wk": [D, MH * MDK], "m_wv": [D, D], "m_wi": [D, MH], "m_wf": [D, MH],
    "m_wo": [D, D], "m_wout": [D, D],
    "r_wr": [D, D], "r_wk": [D, D], "r_wv": [D, D], "r_wo": [D, D],
    "r_w1": [D, 96], "r_w2": [96, D], "r_a1": [D, 96], "r_a2": [96, D], "r_g1": [D, 256], "r_g2": [256, D],
}
OUT_SPECS = {
    "yp": [SEQ, D], "ys": [128, D],
    "Cp": [MH, MDK, MDV], "np_": [MH, MDK], "mp": [MH, 1], "Sp": [RH, RN, RN], "shp": [1, D],
    "Cs": [NSEQ, MH, MDK, MDV], "ns": [128, 128], "msT": [MH, NSEQ], "Ss": [NSEQ, RH, RN, RN], "shs": [NSEQ, D],
}


def build(cfg=None):
    cfg = cfg or {}
    blocks = cfg.get("blocks")
    if blocks is None:
        blocks = [("p", i) for i in range(SEQ // TB)] + [("s", 0)]
    stop = cfg.get("stop", "end")
    taps = cfg.get("taps", ())
    nc = bass.Bass("TRN2", target_bir_lowering=False)
    dr = {}
    for k, shp in IN_SPECS.items():
        dr[k] = nc.dram_tensor(k, list(shp), F32, kind="ExternalInput")
    for k, shp in CONST_SHAPES.items():
        dr["c_" + k] = nc.dram_tensor("c_" + k, list(shp), F32, kind="ExternalInput")
    for k, shp in OUT_SPECS.items():
        dr[k] = nc.dram_tensor(k, list(shp), F32, kind="ExternalOutput")
    for i_ in range(WC_N):
        dr["wcache%d" % i_] = nc.dram_tensor("wcache%d" % i_, [128, WC_TOT], BF16, kind="Internal")
    tap_d = {}

    with ExitStack() as es:
        P = Prog(nc, es)
        P.relax_same = bool(cfg.get("relax_same", False))
        PS = PsumPool(nc, es)

        uid = [0]

        def sb(name, shape, dt=F32, stack=es):
            uid[0] += 1
            return stack.enter_context(nc.sbuf_tensor("%s_%d" % (name, uid[0]), list(shape), dt))

        def tap(name, src_ap, shape, r):
            if name not in taps:
                return
            if name not in tap_d:
                tap_d[name] = nc.dram_tensor("tap_" + name, list(shape), F32, kind="ExternalOutput")
            P.dma("pool", tap_d[name].ap(), src_ap, r=r, is_output=True)

        def TT(eng, out, a, b, op, r, w):
            return P.op(eng, lambda e: e.tensor_tensor(out=out, in0=a, in1=b, op=op), r, w)

        def TS(eng, out, a, s1, s2, op0, op1, r, w):
            if s2 is None:
                return P.op(eng, lambda e: e.tensor_scalar(out=out, in0=a, scalar1=s1, scalar2=None, op0=op0), r, w)
            return P.op(eng, lambda e: e.tensor_scalar(out=out, in0=a, scalar1=s1, scalar2=s2, op0=op0, op1=op1), r, w)

        def STT(out, a, s, b, op0, op1, r, w):
            return P.op("dve", lambda e: e.scalar_tensor_tensor(out=out, in0=a, scalar=s, in1=b, op0=op0, op1=op1), r, w)

        def ACT(out, in_, func, r, w, scale=None, bias=None, accum=None):
            kw = {}
            if scale is not None:
                kw["scale"] = scale
            if bias is not None:
                kw["bias"] = bias
            if accum is not None:
                kw["accum_out"] = accum
            return P.op("act", lambda e: e.activation(out=out, in_=in_, func=func, **kw), r, w)

        def CP(eng, out, in_, r, w):
            if eng == "act":
                return P.op("act", lambda e: e.copy(out=out, in_=in_), r, w)
            return P.op(eng, lambda e: e.tensor_copy(out=out, in_=in_), r, w)

        def MM(out, lhsT, rhs, start, stop_, r, w, skip=False):
            return P.op("pe", lambda e: e.matmul(out, lhsT=lhsT, rhs=rhs, start=start, stop=stop_,
                                                 skip_group_check=skip), r, w)

        def TR(out, in_, ident, r, w):
            return P.op("pe", lambda e: e.transpose(out=out, in_=in_, identity=ident), r, w)

        def MSET(eng, ap, val, w):
            return P.op(eng, lambda e: e.memset(ap, val), (), w)

        NTMAX = TB // 128
        H = [sb("H%d" % t, [128, D]) for t in range(NTMAX)]
        XT = sb("XT", [128, KC, TB + 1], BF16)
        xb = sb("xb", [128, D], BF16)
        nst = sb("nst", [128, 4])
        NWS, NWB = 2, 2
        wst = [sb("wst%d" % i, [128, 4096]) for i in range(NWS)]
        wbf = [sb("wbf%d" % i, [128, 4096], BF16) for i in range(NWB)]
        wctr = {"s": 0, "b": 0, "c": 0, "p": 0, "off": 0, "ci": 0, "mode": "store"}
        wcidx = {}
        idf = sb("idf", [128, 128])
        idb = sb("idb", [128, 128], BF16)
        onesf = sb("onesf", [128, 128])
        bonesf = sb("bonesf", [128, 128])
        FMV = sb("FMV", [128, NVX * 16])
        xlast = sb("xlast", [128, KC], BF16)
        Cst = sb("Cst", [128, MH, MDV + 1])
        STD = sb("STD", [128, NG, 2, RN])
        carry = sb("carry", [8, 4])

        def load_const(name, dst, q="sp"):
            P.dma(q, dst, dr["c_" + name].ap(), w=[name])

        P.dma("sp", idf[:], dr["c_ident"].ap(), w=["idf"])
        P.dma("sp", onesf[:], dr["c_ones"].ap(), w=["onesf"])
        P.dma("sp", bonesf[:], dr["c_bones"].ap(), w=["bonesf"])
        CP("dve", idb[:], idf[:], ["idf"], ["idb"])
        with ExitStack() as st:
            vl = sb("vl", [128, 3, 128], F32, st)
            nrows = NV * 16
            for i in range(3):
                n = min(128, nrows - i * 128)
                P.dma("sp", vl[0:n, i, :], dr["vecs"].ap()[i * 128:i * 128 + n, :], w=[("vl", i)])
            b = PS.get()
            for i in range(3):
                n = min(128, nrows - i * 128)
                TR(PS.f32(b)[:, i * 128:i * 128 + n], vl[0:n, i, :], idf[0:n, 0:n], [("vl", i), "idf"], [("ps", b)])
            CP("dve", FMV[:, 0:nrows], PS.f32(b)[:, 0:nrows], [("ps", b)], ["FMV"])
            TS("dve", FMV[:, V_1MU0 * 16:(V_1MU0 + 6) * 16], FMV[:, V_MU0 * 16:(V_MU0 + 6) * 16], -1.0, 1.0,
               ALU.mult, ALU.add, ["FMV"], ["FMV"])
            TS("dve", FMV[:, V_1KA * 16:(V_1KA + 1) * 16], FMV[:, V_KA * 16:(V_KA + 1) * 16], -1.0, 1.0,
               ALU.mult, ALU.add, ["FMV"], ["FMV"])
            P.barrier()
        MSET("dve", Cst[:], 0.0, ["Cst"])
        MSET("dve", STD[:], 0.0, ["STD"])
        MSET("dve", xlast[:], 0.0, ["xlast"])
        MSET("dve", carry[:], 0.0, ["carry"])

        def fmv(v, kc0=0, n=1):
            return FMV[:, v * 16 + kc0:v * 16 + kc0 + n]

        CAST_ENGS = cfg.get("cast_engs", ("act",))

        def wpanel(src3, np_, kcn, ncols, scales=None):
            sz = kcn * ncols
            nver = 1 if scales is None else len(scales)
            assert sz * nver <= 4096
            bi = wctr["b"] % NWB
            wctr["b"] += 1
            idx = wctr["p"]
            wctr["p"] += 1
            sig = (np_, kcn, ncols, nver)
            if wctr["mode"] == "load":
                ci, off, sig0 = wcidx[idx]
                assert sig0 == sig, (idx, sig0, sig)
                P.dma("sp", wbf[bi][0:np_, 0:nver * sz], dr["wcache%d" % ci].ap()[0:np_, off:off + nver * sz], r=[("wc", idx)], w=[("wbf", bi)])
                return [wbf[bi][0:np_, vi * sz:(vi + 1) * sz].rearrange("p (k n) -> p k n", k=kcn) for vi in range(nver)], ("wbf", bi)
            si = wctr["s"] % NWS
            wctr["s"] += 1
            stg = wst[si][0:np_, 0:sz].rearrange("p (k n) -> p k n", k=kcn)
            P.dma("sp", stg, src3, w=[("wst", si)])
            outs = []
            for vi in range(nver):
                dst = wbf[bi][0:np_, vi * sz:(vi + 1) * sz].rearrange("p (k n) -> p k n", k=kcn)
                eng = CAST_ENGS[wctr["c"] % len(CAST_ENGS)]
                wctr["c"] += 1
                if scales is None or scales[vi] is None:
                    CP(eng, dst, stg, [("wst", si)], [("wbf", bi)] if vi == 0 else [("wbf", bi)])
                else:
                    eng = "pool"
                    sc = AP(FMV, scales[vi], [[NVX * 16, np_], [1, kcn], [0, ncols]])
                    TT(eng, dst, stg, sc, ALU.mult, [("wst", si), "FMV"], [("wbf", bi)])
                outs.append(dst)
            if wctr["mode"] == "store":
                if wctr["off"] + nver * sz > WC_TOT:
                    wctr["ci"] += 1
                    wctr["off"] = 0
                    assert wctr["ci"] < WC_N
                ci, off = wctr["ci"], wctr["off"]
                wctr["off"] += nver * sz
                wcidx[idx] = (ci, off, sig)
                P.dma("act" if (scales is None) else "pool", dr["wcache%d" % ci].ap()[0:np_, off:off + nver * sz], wbf[bi][0:np_, 0:nver * sz],
                      r=[("wbf", bi)], w=[("wc", idx)])
            return outs, ("wbf", bi)

        def wsrc(name, layer, r0, nrows, c0, ncols):
            t = dr[name].ap()
            if layer is not None:
                t = t[layer]
            if nrows <= 128:
                return t[r0:r0 + nrows, c0:c0 + ncols].rearrange("(k p) n -> p k n", k=1), nrows, 1
            return t[r0:r0 + nrows, c0:c0 + ncols].rearrange("(k p) n -> p k n", p=128), 128, nrows // 128

        def load_block(kind, bi, nt):
            for t in range(nt):
                if kind == "p":
                    src = dr["xp"].ap()[bi * TB + t * 128:bi * TB + (t + 1) * 128, :]
                else:
                    src = dr["xs"].ap()
                P.dma("sp", H[t][:], src, w=[("H", t)])

        def rms_stats(t):
            ACT(xb[:], H[t][:], AF.Square, [("H", t)], ["xb", "nst"], accum=nst[:, 0:1])
            ACT(nst[:, 1:2], nst[:, 0:1], AF.Sqrt, ["nst"], ["nst"], scale=1.0 / D, bias=epsb[:, 0:1])
            P.op("dve", lambda e: e.reciprocal(out=nst[:, 2:3], in_=nst[:, 1:2]), ["nst"], ["nst"])

        def norm_to_xt(vidx, nt):
            for t in range(nt):
                rms_stats(t)
                ACT(xb[:], H[t][:], AF.Copy, [("H", t), "nst"], ["xb"], scale=nst[:, 2:3])
                for half in range(2):
                    b = PS.get()
                    for j in range(8):
                        kc = half * 8 + j
                        TR(PS.bf16(b)[:, j * 128:(j + 1) * 128], xb[:, kc * 128:(kc + 1) * 128], idb[:], ["xb", "idb"], [("ps", b)])
                    src = PS.bf16(b)[:, 0:1024].rearrange("p (a b) -> p a b", a=8)
                    gb = AP(FMV, vidx * 16 + half * 8, [[NVX * 16, 128], [1, 8], [0, 128]])
                    TT("dve", XT[:, half * 8:half * 8 + 8, 1 + t * 128:1 + (t + 1) * 128], src, gb, ALU.mult,
                       [("ps", b), "FMV"], ["XT"])

        def xt_cols(kc, c0, n):
            return XT[:, kc, 1 + c0:1 + c0 + n]

        def mm_fm(name, layer, c0, ncols, ntok, consumer, rhs_fn=None, scales=None, keys_r=("XT",)):
            pc = 256 if scales is None else 128
            for p0 in range(0, ncols, pc):
                n = min(pc, ncols - p0)
                src, np_, kcn = wsrc(name, layer, 0, D, c0 + p0, n)
                wv, wkey = wpanel(src, np_, kcn, n, scales)
                for m0 in range(0, n, 128):
                    mw = min(128, n - m0)
                    b = PS.get()
                    steps = []
                    for vi, wt in enumerate(wv):
                        for kc in range(KC):
                            steps.append((wt[:, kc, m0:m0 + mw], vi, kc))
                    for i, (lt, vi, kc) in enumerate(steps):
                        rhs = rhs_fn(vi, kc) if rhs_fn else xt_cols(kc, 0, ntok)
                        MM(PS.f32(b)[0:mw, 0:ntok], lt, rhs, i == 0, i == len(steps) - 1, [wkey] + list(keys_r), [("ps", b)])
                    consumer((c0 + p0 + m0) // 128, mw, b)

        def mm_tm(name, layer, c0, ncols, nt, consumer, lhs_fn=None, keys_r=("XT",), kin=D, r0=0):
            for p0 in range(0, ncols, 256):
                n = min(256, ncols - p0)
                src, np_, kcn = wsrc(name, layer, r0, kin, c0 + p0, n)
                wv, wkey = wpanel(src, np_, kcn, n)
                wt = wv[0]
                for t in range(nt):
                    b = PS.get()
                    for kc in range(kcn):
                        lt = lhs_fn(kc, t) if lhs_fn else xt_cols(kc, t * 128, 128)
                        MM(PS.f32(b)[:, 0:n], lt, wt[:, kc, :], kc == 0, kc == kcn - 1, [wkey] + list(keys_r), [("ps", b)])
                    consumer(t, c0 + p0, n, b)

        def ffn(layer, nt):
            ntok = nt * 128
            norm_to_xt(V_NFFN0 if layer == 0 else V_NFFN1, nt)
            with ExitStack() as st:
                act = sb("ffn_act", [128, 16, TB], BF16, st)
                sg = [sb("ffn_sg%d" % i, [128, TB], F32, st) for i in range(2)]
                groups = [(0, 16), (16, 16), (32, 12)]
                for g0, gn in groups:
                    for m in range(g0, g0 + gn, 2):
                        srcg, np_, kcn = wsrc("ffn_wg", layer, 0, D, m * 128, 256)
                        wg, kg = wpanel(srcg, np_, kcn, 256)
                        srcu, np_, kcn = wsrc("ffn_wu", layer, 0, D, m * 128, 256)
                        wu, ku = wpanel(srcu, np_, kcn, 256)
                        for mm_ in range(2):
                            bg_ = PS.get()
                            for kc in range(KC):
                                MM(PS.f32(bg_)[:, 0:ntok], wg[0][:, kc, mm_ * 128:(mm_ + 1) * 128], xt_cols(kc, 0, ntok),
                                   kc == 0, kc == KC - 1, [kg, "XT"], [("ps", bg_)])
                            bu_ = PS.get()
                            for kc in range(KC):
                                MM(PS.f32(bu_)[:, 0:ntok], wu[0][:, kc, mm_ * 128:(mm_ + 1) * 128], xt_cols(kc, 0, ntok),
                                   kc == 0, kc == KC - 1, [ku, "XT"], [("ps", bu_)])
                            j = m + mm_ - g0
                            s_ = sg[(m + mm_) % 2]
                            sk = ("sg", (m + mm_) % 2)
                            ACT(s_[:, 0:ntok], PS.f32(bg_)[:, 0:ntok], AF.Silu, [("ps", bg_)], [sk])
                            TT("dve", act[:, j, 0:ntok], PS.f32(bu_)[:, 0:ntok], s_[:, 0:ntok], ALU.mult,
                               [("ps", bu_), sk], [("act", j)])
                    def down_cons(t, c0, n, b):
                        TT("dve", H[t][:, c0:c0 + n], PS.f32(b)[:, 0:n], H[t][:, c0:c0 + n], ALU.add,
                           [("ps", b), ("H", t)], [("H", t)])
                    mm_tm("ffn_wd", layer, 0, D, nt, down_cons,
                          lhs_fn=lambda kc, t: act[:, kc, t * 128:(t + 1) * 128],
                          keys_r=[("act", j) for j in range(gn)], kin=gn * 128, r0=g0 * 128)
                P.barrier()

        def ple(layer, kind, bi, nt):
            with ExitStack() as st:
                plL = [sb("ple_p", [128, PLE], F32, st) for _ in range(nt)]
                plbL = [sb("ple_pb", [128, PLE], BF16, st) for _ in range(nt)]
                pT = sb("ple_pT", [128, 2, TB], BF16, st)
                sgm = [sb("ple_sg%d" % i, [128, 256], F32, st) for i in range(2)]
                for t in range(nt):
                    if kind == "p":
                        src = dr["pp"].ap()[layer, bi * TB + t * 128:bi * TB + (t + 1) * 128, :]
                    else:
                        src = dr["psm"].ap()[layer]
                    P.dma("sp", plL[t][:], src, w=[("pl", t)])
                norm_to_xt(V_NPLE0 if layer == 0 else V_NPLE1, nt)
                for t in range(nt):
                    pl, plb = plL[t], plbL[t]
                    CP("dve", plb[:], pl[:], [("pl", t)], [("plb", t)])
                    b = PS.get()
                    for j in range(2):
                        TR(PS.bf16(b)[:, j * 128:(j + 1) * 128], plb[:, j * 128:(j + 1) * 128], idb[:], [("plb", t), "idb"], [("ps", b)])
                    CP("dve", pT[:, :, t * 128:(t + 1) * 128], PS.bf16(b)[:, 0:256].rearrange("p (a b) -> p a b", a=2),
                       [("ps", b)], ["pT"])
                ctr = [0]
                for p0 in range(0, D, 256):
                    srcg, np_, kcn = wsrc("ple_wg", layer, 0, D, p0, 256)
                    wg, kg = wpanel(srcg, np_, kcn, 256)
                    srcp, np2, kcn2 = wsrc("ple_wp", layer, 0, PLE, p0, 256)
                    wp, kp = wpanel(srcp, np2, kcn2, 256)
                    for t in range(nt):
                        b1 = PS.get()
                        for kc in range(KC):
                            MM(PS.f32(b1)[:, 0:256], xt_cols(kc, t * 128, 128), wg[0][:, kc, :], kc == 0, kc == KC - 1,
                               [kg, "XT"], [("ps", b1)])
                        b2 = PS.get()
                        for j in range(2):
                            MM(PS.f32(b2)[:, 0:256], pT[:, j, t * 128:(t + 1) * 128], wp[0][:, j, :], j == 0, j == 1,
                               [kp, "pT"], [("ps", b2)])
                        s_ = sgm[ctr[0] % 2]
                        sk = ("plesg", ctr[0] % 2)
                        ctr[0] += 1
                        ACT(s_[:], PS.f32(b1)[:, 0:256], AF.Sigmoid, [("ps", b1)], [sk])
                        TT("dve", s_[:], PS.f32(b2)[:, 0:256], s_[:], ALU.mult, [("ps", b2), sk], [sk])
                        TT("pool", H[t][:, p0:p0 + 256], H[t][:, p0:p0 + 256], s_[:], ALU.add, [sk, ("H", t)], [("H", t)])
                P.barrier()

        def final_out(kind, bi, nt):
            with ExitStack() as st:
                gt = sb("fin_g", [128, D], F32, st)
                yo = [sb("fin_y%d" % i, [128, D], F32, st) for i in range(2)]
                P.dma("sp", gt[:], AP(dr["nfin"], 0, [[0, 128], [1, D]]), w=["fin_g"])
                for t in range(nt):
                    rms_stats(t)
                    y = yo[t % 2]
                    STT(y[:], H[t][:], nst[:, 2:3], gt[:], ALU.mult, ALU.mult, [("H", t), "nst", "fin_g"], [("yo", t % 2)])
                    if kind == "p":
                        dst = dr["yp"].ap()[bi * TB + t * 128:bi * TB + (t + 1) * 128, :]
                    else:
                        dst = dr["ys"].ap()
                    P.dma("pool", dst, y[:], r=[("yo", t % 2)], is_output=True)
                P.barrier()

        epsb = sb("epsb", [128, 4])
        MSET("dve", epsb[:, 0:1], 1e-6, ["epsb"])
        MSET("dve", epsb[:, 1:2], 1.0, ["epsb"])
        MSET("dve", epsb[:, 2:3], 64e-5, ["epsb"])
        MSET("dve", epsb[:, 3:4], 1e-24, ["epsb"])

        gb15 = sb("gb15", [8, 2])
        P.dma("sp", gb15[:, 0:1], dr["big"].ap(), w=["gb15"])
        P.dma("sp", gb15[:, 1:2], dr["bfg"].ap(), w=["gb15"])
        TS("dve", gb15[:], gb15[:], 1.0 / 15.0, None, ALU.mult, None, ["gb15"], ["gb15"])

        def mlstm(kind, bi, nt):
            ntok = nt * 128
            W = ntok
            L = 64 if kind == "p" else 8
            nch = ntok // L
            sfx = kind
            norm_to_xt(V_NMIX0, nt)
            with ExitStack() as st:
                rw = {nm: sb("ml_r_" + nm, [8, W], F32, st) for nm in
                      ("li", "lf", "bg", "z", "cm", "cmP", "cmE", "t0", "qA", "qB", "qC", "qD")}
                QT = sb("ml_QT", [128, nt, 6, 8], F32, st)
                DEC = sb("ml_DEC", [128, nt, 16, 8], F32, st) if kind == "s" else sb("ml_DEC", [128, nt, 2, 8], F32, st)
                WT = sb("ml_WT", [128, nt, 8, 128], BF16, st)
                Dcm = sb("ml_Dcm", [128, 8, 128], F32, st)
                tmpw = sb("ml_tmpw", [128, 8, 128], F32, st)
                negm = sb("ml_negm", [128, 128], F32, st)
                onesrow = sb("ml_onesrow", [8, W], F32, st)
                qT = [sb("ml_qT%d" % i, [128, W], BF16, st) for i in range(2)]
                kT = [sb("ml_kT%d" % i, [128, W], BF16, st) for i in range(2)]
                V1 = sb("ml_V1", [128, nt, MDV + 1], BF16, st)
                ogs = sb("ml_ogs", [128, nt, MDV], F32, st)
                kw = sb("ml_kw", [128, 128], BF16, st)
                Sb = sb("ml_Sb", [128, 128], BF16, st)
                intra = sb("ml_intra", [128, MDV + 1], F32, st)
                hn = sb("ml_hn", [128, MDV + 1], F32, st)
                sc = sb("ml_sc", [128, 8], F32, st)
                HO = sb("ml_HO", [128, MDV], BF16, st)
                Cb = sb("ml_Cb", [128, MH, MDV + 1], BF16, st)
                YT = sb("ml_YT", [128, KC, W], BF16, st)
                if kind == "p":
                    kwL = [kw] + [sb("ml_kwL", [128, 128], BF16, st) for _ in range(nt - 1)]
                    SbL = [Sb] + [sb("ml_SbL", [128, 128], BF16, st) for _ in range(nt - 1)]
                    inL = [intra] + [sb("ml_inL", [128, MDV + 1], F32, st) for _ in range(nt - 1)]
                    scL = [sc] + [sb("ml_scL", [128, 8], F32, st) for _ in range(nt - 1)]
                    HOL = [HO] + [sb("ml_HOL", [128, MDV], BF16, st) for _ in range(nt - 1)]
                    junk = hn
                P.dma("sp", negm[:], dr["c_negmask_" + sfx].ap(), w=["negm"])
                MSET("pool", V1[:, :, MDV:MDV + 1], 1.0, ["V1"])
                if kind == "p":
                    P.dma("sp", onesrow[:], dr["c_onesrow"].ap()[0:8, 0:W], w=["onesrow"])
                    selp = sb("ml_selp", [128, 2, 128], F32, st)
                    P.dma("sp", selp[:], dr["c_sel_p"].ap(), w=["selp"])
                    CP("act", Cb[:], Cst[:], ["Cst"], ["Cb"])
                else:
                    rst8 = sb("ml_rst8", [8, 128], F32, st)
                    addrst8 = sb("ml_addrst8", [8, 128], F32, st)
                    m0T = sb("ml_m0T", [8, 16], F32, st)
                    onehot = sb("ml_onehot", [128, 16], F32, st)
                    rowmask = sb("ml_rowmask", [128, 16], F32, st)
                    seqmask = sb("ml_seqmask", [128, 16, 128], BF16, st)
                    rhs2 = sb("ml_rhs2", [128, 16, 8], F32, st)
                    QZ = sb("ml_QZ", [128, 16, 128], BF16, st)
                    KZ = sb("ml_KZ", [128, 16, 128], BF16, st)
                    C0f = sb("ml_C0f", [128, 16, MDV + 1], F32, st)
                    C0b = sb("ml_C0b", [128, 16, MDV + 1], BF16, st)
                    n0l = sb("ml_n0l", [128, 128], F32, st)
                    n0T = sb("ml_n0T", [128, 128], F32, st)
                    NS = sb("ml_NS", [128, 128], F32, st)
                    msr = sb("ml_msr", [8, 16], F32, st)
                    P.dma("sp", rst8[:], dr["c_rst8"].ap()[0:8, :], w=["rst8"])
                    P.dma("sp", addrst8[:], dr["c_addrst8"].ap()[0:8, :], w=["addrst8"])
                    P.dma("sp", m0T[:], dr["mmT"].ap(), w=["m0T"])
                    P.dma("sp", onehot[:], dr["c_onehot_s"].ap(), w=["onehot"])
                    P.dma("sp", rowmask[:], dr["c_rowmask_s"].ap(), w=["rowmask"])
                    P.dma("sp", wst[0][:, 0:2048], dr["c_seqmask_s"].ap().rearrange("p a b -> p (a b)"), w=[("wst", 0)])
                    CP("pool", seqmask[:].rearrange("p a b -> p (a b)"), wst[0][:, 0:2048], [("wst", 0)], ["seqmask"])
                    P.dma("sp", n0l[:], dr["mn"].ap(), w=["n0l"])
                    b = PS.get()
                    TR(PS.f32(b)[:, 0:128], n0l[:], idf[:], ["n0l", "idf"], [("ps", b)])
                    CP("dve", n0T[:], PS.f32(b)[:, 0:128], [("ps", b)], ["n0T"])

                wi, ki = wpanel(wsrc("m_wi", None, 0, D, 0, MH)[0], 128, KC, MH)
                bgi = PS.get()
                for kc in range(KC):
                    MM(PS.f32(bgi)[0:8, 0:ntok], wi[0][:, kc, :], xt_cols(kc, 0, ntok), kc == 0, kc == KC - 1, [ki, "XT"], [("ps", bgi)])
                ACT(rw["li"][:, 0:ntok], PS.f32(bgi)[0:8, 0:ntok], AF.Tanh, [("ps", bgi), "gb15"], ["li"], scale=1.0 / 15.0, bias=gb15[:, 0:1])
                TS("dve", rw["li"][:, 0:ntok], rw["li"][:, 0:ntok], 15.0, None, ALU.mult, None, ["li"], ["li"])
                wf, kf = wpanel(wsrc("m_wf", None, 0, D, 0, MH)[0], 128, KC, MH)
                bgf = PS.get()
                for kc in range(KC):
                    MM(PS.f32(bgf)[0:8, 0:ntok], wf[0][:, kc, :], xt_cols(kc, 0, ntok), kc == 0, kc == KC - 1, [kf, "XT"], [("ps", bgf)])
                ACT(rw["t0"][:, 0:ntok], PS.f32(bgf)[0:8, 0:ntok], AF.Tanh, [("ps", bgf), "gb15"], ["t0"], scale=1.0 / 15.0, bias=gb15[:, 1:2])
                ACT(rw["t0"][:, 0:ntok], rw["t0"][:, 0:ntok], AF.Exp, ["t0"], ["t0"], scale=-15.0)
                ACT(rw["lf"][:, 0:ntok], rw["t0"][:, 0:ntok], AF.Ln, ["t0", "epsb"], ["lf"], bias=epsb[0:8, 1:2])
                TS("dve", rw["lf"][:, 0:ntok], rw["lf"][:, 0:ntok], -1.0, None, ALU.mult, None, ["lf"], ["lf"])
                if kind == "p":
                    P.op("dve", lambda e: e.tensor_tensor_scan(out=rw["bg"][:, 0:ntok], data0=onesrow[:, 0:ntok], data1=rw["lf"][:, 0:ntok],
                                                               initial=carry[:, 0:1], op0=ALU.mult, op1=ALU.add),
                         ["onesrow", "lf", "carry"], ["bg"])
                else:
                    P.op("dve", lambda e: e.tensor_tensor_scan(out=rw["bg"][:, 0:ntok], data0=rst8[:, 0:ntok], data1=rw["lf"][:, 0:ntok],
                                                               initial=0.0, op0=ALU.mult, op1=ALU.add),
                         ["rst8", "lf"], ["bg"])
                TT("dve", rw["z"][:, 0:ntok], rw["li"][:, 0:ntok], rw["bg"][:, 0:ntok], ALU.subtract, ["li", "bg"], ["z"])
                if kind == "p":
                    P.op("dve", lambda e: e.tensor_tensor_scan(out=rw["cm"][:, 0:ntok], data0=rw["z"][:, 0:ntok], data1=rw["z"][:, 0:ntok],
                                                               initial=carry[:, 1:2], op0=ALU.max, op1=ALU.max),
                         ["z", "carry"], ["cm"])
                    CP("dve", rw["cmP"][:, L:ntok].rearrange("p (c l) -> p c l", l=L),
                       AP(rw["cm"], L - 1, [[W, 8], [L, nch - 1], [0, L]]), ["cm"], ["cmP"])
                    CP("dve", rw["cmP"][:, 0:L], AP(carry, 1, [[4, 8], [0, L]]), ["carry"], ["cmP"])
                else:
                    CP("dve", rw["t0"][:, 0:ntok], rw["z"][:, 0:ntok], ["z"], ["t0"])
                    zv = AP(rw["t0"], 0, [[W, 8], [8, 16], [1, 1]])
                    TT("dve", zv, zv, AP(m0T, 0, [[16, 8], [1, 16], [1, 1]]), ALU.max, ["t0", "m0T"], ["t0"])
                    P.op("dve", lambda e: e.tensor_tensor_scan(out=rw["cm"][:, 0:ntok], data0=addrst8[:, 0:ntok], data1=rw["t0"][:, 0:ntok],
                                                               initial=0.0, op0=ALU.add, op1=ALU.max),
                         ["t0", "addrst8"], ["cm"])
                    CP("dve", rw["cmP"][:, 0:ntok].rearrange("p (c l) -> p c l", l=L), AP(m0T, 0, [[16, 8], [1, 16], [0, L]]), ["m0T"], ["cmP"])
                CP("dve", rw["cmE"][:, 0:ntok].rearrange("p (c l) -> p c l", l=L),
                   AP(rw["cm"], L - 1, [[W, 8], [L, nch], [0, L]]), ["cm"], ["cmE"])
                if kind == "p":
                    if bi == SEQ // TB - 1:
                        TT("dve", rw["t0"][:, 0:1], rw["bg"][:, ntok - 1:ntok], rw["cm"][:, ntok - 1:ntok], ALU.add, ["bg", "cm"], ["t0"])
                        P.dma("pool", dr["mp"].ap(), rw["t0"][:, 0:1], r=["t0"], is_output=True)
                    CP("dve", carry[:, 0:1], rw["bg"][:, ntok - 1:ntok], ["bg", "cmP"], ["carry"])
                    CP("dve", carry[:, 1:2], rw["cm"][:, ntok - 1:ntok], ["cm", "cmP"], ["carry"])
                else:
                    TT("dve", msr[:], AP(rw["bg"], 7, [[W, 8], [8, 16]]), AP(rw["cm"], 7, [[W, 8], [8, 16]]), ALU.add, ["bg", "cm"], ["msr"])
                    P.dma("pool", dr["msT"].ap(), msr[:], r=["msr"], is_output=True)
                for nm in ("li", "lf", "bg", "z", "cm", "cmP", "cmE"):
                    tap("ml_%s_%s%d" % (nm, kind, bi), rw[nm][:, 0:ntok], [8, ntok], [nm])
                N_ = slice(0, ntok)
                TT("dve", rw["qA"][:, N_], rw["cmP"][:, N_], rw["cm"][:, N_], ALU.subtract, ["cmP", "cm"], ["qA"])
                ACT(rw["qA"][:, N_], rw["qA"][:, N_], AF.Exp, ["qA"], ["qA"])
                TT("dve", rw["qB"][:, N_], rw["z"][:, N_], rw["cmE"][:, N_], ALU.subtract, ["z", "cmE"], ["qB"])
                ACT(rw["qB"][:, N_], rw["qB"][:, N_], AF.Exp, ["qB"], ["qB"])
                TT("dve", rw["qC"][:, N_], rw["cmP"][:, N_], rw["cmE"][:, N_], ALU.subtract, ["cmP", "cmE"], ["qC"])
                ACT(rw["qC"][:, N_], rw["qC"][:, N_], AF.Exp, ["qC"], ["qC"])
                TT("dve", rw["qD"][:, N_], rw["bg"][:, N_], rw["cm"][:, N_], ALU.add, ["bg", "cm"], ["qD"])
                ACT(rw["qD"][:, N_], rw["qD"][:, N_], AF.Exp, ["qD"], ["qD"], scale=-1.0)
                qnames = ["qA", "qB", "qC", "qD", "z", "cm"]
                for t in range(nt):
                    b = PS.get()
                    for qi, qn in enumerate(qnames):
                        TR(PS.f32(b)[:, qi * 8:(qi + 1) * 8], rw[qn][:, t * 128:(t + 1) * 128], idf[0:8, 0:8], [qn, "idf"], [("ps", b)])
                    CP("dve", QT[:, t, :, :].rearrange("p a b -> p (a b)"), PS.f32(b)[:, 0:48], [("ps", b)], ["QT"])
                for t in range(nt):
                    if kind == "p":
                        b = PS.get()
                        for c in range(2):
                            MM(PS.f32(b)[:, c * 8:(c + 1) * 8], selp[:, c, :], QT[:, t, 2, :], True, True, ["selp", "QT"], [("ps", b)], skip=True)
                        CP("dve", DEC[:, t, :, :].rearrange("p a b -> p (a b)"), PS.f32(b)[:, 0:16], [("ps", b)], ["DEC"])
                    else:
                        TT("dve", rhs2[:], AP(QT, 2 * 8, [[nt * 48, 128], [0, 16], [1, 8]]), AP(onehot, 0, [[16, 128], [1, 16], [0, 8]]),
                           ALU.mult, ["QT", "onehot"], ["rhs2"])
                        b = PS.get()
                        MM(PS.f32(b)[:, 0:128], onesf[:], rhs2[:].rearrange("p a b -> p (a b)"), True, True, ["onesf", "rhs2"], [("ps", b)])
                        CP("dve", DEC[:, t, :, :].rearrange("p a b -> p (a b)"), PS.f32(b)[:, 0:128], [("ps", b)], ["DEC"])
                    TT("dve", Dcm[:], AP(idf, 0, [[128, 128], [0, 8], [1, 128]]),
                       AP(QT, t * 48 + 5 * 8, [[nt * 48, 128], [1, 8], [0, 128]]), ALU.mult, ["idf", "QT"], ["Dcm"])
                    for hh in range(2):
                        b = PS.get()
                        MM(PS.f32(b)[:, 0:512], onesf[:], Dcm[:, 4 * hh:4 * hh + 4, :].rearrange("p a b -> p (a b)"), True, True,
                           ["onesf", "Dcm"], [("ps", b)])
                        TT("dve", tmpw[:, 4 * hh:4 * hh + 4, :], PS.f32(b)[:, 0:512].rearrange("p (a b) -> p a b", a=4),
                           AP(QT, t * 48 + 4 * 8 + 4 * hh, [[nt * 48, 128], [1, 4], [0, 128]]), ALU.subtract, [("ps", b), "QT"], ["tmpw"])
                    TT("dve", tmpw[:], tmpw[:], AP(negm, 0, [[128, 128], [0, 8], [1, 128]]), ALU.max, ["tmpw", "negm"], ["tmpw"])
                    ACT(WT[:, t, :, :], tmpw[:], AF.Exp, ["tmpw"], [("WT", t)], scale=-1.0)

                for hp in range(MH // 2):
                    def qcons(m, mw, b):
                        CP("act", qT[m % 2][:, 0:ntok], PS.f32(b)[:, 0:ntok], [("ps", b)], [("qT", m % 2)])
                    mm_fm("m_wq", None, hp * 256, 256, ntok, qcons)

                    def kcons(m, mw, b):
                        ACT(kT[m % 2][:, 0:ntok], PS.f32(b)[:, 0:ntok], AF.Copy, [("ps", b)], [("kT", m % 2)], scale=MDK ** -0.5)
                    mm_fm("m_wk", None, hp * 256, 256, ntok, kcons)
                    for hh in range(2):
                        h = hp * 2 + hh
                        qh, kh = qT[hh], kT[hh]
                        qk, kk_ = ("qT", hh), ("kT", hh)

                        def vcons(t, c0, n, b):
                            CP("act", V1[:, t, 0:MDV], PS.f32(b)[:, 0:MDV], [("ps", b)], ["V1"])
                        mm_tm("m_wv", None, h * MDV, MDV, nt, vcons)

                        def ocons(t, c0, n, b):
                            ACT(ogs[:, t, :], PS.f32(b)[:, 0:MDV], AF.Sigmoid, [("ps", b)], ["ogs"])
                        mm_tm("m_wo", None, h * MDV, MDV, nt, ocons)
                        if kind == "s":
                            P.dma("sp", C0f[:, :, 0:MDV], dr["mC"].ap()[:, h].rearrange("s k v -> k s v"), w=["C0f"])
                            CP("pool", C0f[:, :, MDV:MDV + 1], AP(n0T, h, [[128, 128], [8, 16], [1, 1]]), ["n0T"], ["C0f"])
                            CP("act", C0b[:], C0f[:], ["C0f"], ["C0b"])
                            TT("dve", QZ[:], AP(qh, 0, [[W, 128], [0, 16], [1, 128]]), seqmask[:], ALU.mult, [qk, "seqmask"], ["QZ"])
                        if kind == "p":
                            TS_ = [slice(t * 128, (t + 1) * 128) for t in range(nt)]
                            bks = []
                            for t in range(nt):
                                b = PS.get()
                                bks.append(b)
                                TR(PS.bf16(b)[:, 0:128], kh[:, TS_[t]], idb[:], [kk_, "idb"], [("ps", b)])
                            for t in range(nt):
                                TS("dve", kwL[t][:], PS.bf16(bks[t])[:, 0:128], QT[:, t, 1, h:h + 1], None, ALU.mult, None,
                                   [("ps", bks[t]), "QT"], [("kw", t)])
                            bks = []
                            for t in range(nt):
                                b = PS.get()
                                bks.append(b)
                                MM(PS.f32(b)[:, 0:128], kh[:, TS_[t]], qh[:, TS_[t]], True, True, [qk, kk_], [("ps", b)])
                            for t in range(nt):
                                TT("dve", SbL[t][:], PS.f32(bks[t])[:, 0:128], WT[:, t, h, :], ALU.mult, [("ps", bks[t]), ("WT", t)], [("Sb", t)])
                            bks = []
                            for t in range(nt):
                                b = PS.get()
                                bks.append(b)
                                MM(PS.f32(b)[:, 0:MDV + 1], SbL[t][:], V1[:, t, :], True, True, [("Sb", t), "V1"], [("ps", b)])
                            for t in range(nt):
                                CP("act", inL[t][:], PS.f32(bks[t])[:, 0:MDV + 1], [("ps", bks[t])], [("intra", t)])
                            for t in range(nt):
                                t0_ = t * 128
                                for c in range(2):
                                    r0 = 64 * c
                                    bU = PS.get()
                                    MM(PS.f32(bU)[:, 0:MDV + 1], kwL[t][r0:r0 + 64, :], V1[r0:r0 + 64, t, :], True, True, [("kw", t), "V1"], [("ps", bU)])
                                    bN = PS.get()
                                    MM(PS.f32(bN)[r0:r0 + 64, 0:MDV + 1], qh[:, t0_ + r0:t0_ + r0 + 64], Cb[:, h, :], True, True,
                                       [qk, ("Cb", h)], [("ps", bN)])
                                    STT(Cst[:, h, :], Cst[:, h, :], DEC[:, t, c, h:h + 1], PS.f32(bU)[:, 0:MDV + 1], ALU.mult, ALU.add,
                                        [("ps", bU), "DEC", ("Cst", h)], [("Cst", h)])
                                    CP("act", Cb[:, h, :], Cst[:, h, :], [("Cst", h)], [("Cb", h)])
                                    STT(inL[t][r0:r0 + 64, :], PS.f32(bN)[r0:r0 + 64, 0:MDV + 1], QT[r0:r0 + 64, t, 0, h:h + 1], inL[t][r0:r0 + 64, :],
                                        ALU.mult, ALU.add, [("ps", bN), "QT", ("intra", t)], [("intra", t)])
                            def fin_stages():
                                yield lambda t: TS("dve", scL[t][:, 7:8], inL[t][:, MDV:MDV + 1], -1.0, None, ALU.mult, None, [("intra", t)], [("sc", t)])
                                yield lambda t: TT("dve", scL[t][:, 7:8], scL[t][:, 7:8], inL[t][:, MDV:MDV + 1], ALU.max, [("intra", t), ("sc", t)], [("sc", t)])
                                yield lambda t: TT("dve", scL[t][:, 0:1], scL[t][:, 7:8], QT[:, t, 3, h:h + 1], ALU.max, [("sc", t), "QT"], [("sc", t)])
                                yield lambda t: P.op("dve", lambda e: e.reciprocal(out=scL[t][:, 1:2], in_=scL[t][:, 0:1]), [("sc", t)], [("sc", t)])
                                yield lambda t: ACT(junk[:, 0:MDV], inL[t][:, 0:MDV], AF.Square, [("intra", t)], ["junk", ("sc", t)], accum=scL[t][:, 2:3])
                                yield lambda t: TT("dve", scL[t][:, 3:4], scL[t][:, 1:2], scL[t][:, 1:2], ALU.mult, [("sc", t)], [("sc", t)])
                                yield lambda t: TT("dve", scL[t][:, 3:4], scL[t][:, 3:4], scL[t][:, 2:3], ALU.mult, [("sc", t)], [("sc", t)])
                                yield lambda t: ACT(scL[t][:, 4:5], scL[t][:, 3:4], AF.Sqrt, [("sc", t), "epsb"], [("sc", t)], scale=1.0 / MDV, bias=epsb[:, 0:1])
                                yield lambda t: P.op("dve", lambda e: e.reciprocal(out=scL[t][:, 5:6], in_=scL[t][:, 4:5]), [("sc", t)], [("sc", t)])
                                yield lambda t: TT("dve", scL[t][:, 6:7], scL[t][:, 5:6], scL[t][:, 1:2], ALU.mult, [("sc", t)], [("sc", t)])
                                yield lambda t: STT(HOL[t][:], inL[t][:, 0:MDV], scL[t][:, 6:7], ogs[:, t, :], ALU.mult, ALU.mult,
                                                    [("intra", t), ("sc", t), "ogs"], [("HO", t)])
                            for fn in fin_stages():
                                for t in range(nt):
                                    fn(t)
                            bks = []
                            for t in range(nt):
                                b = PS.get()
                                bks.append(b)
                                for j in range(2):
                                    TR(PS.bf16(b)[:, j * 128:(j + 1) * 128], HOL[t][:, j * 128:(j + 1) * 128], idb[:], [("HO", t), "idb"], [("ps", b)])
                            for t in range(nt):
                                TT("dve", YT[:, 2 * h:2 * h + 2, TS_[t]], PS.bf16(bks[t])[:, 0:256].rearrange("p (a b) -> p a b", a=2),
                                   AP(FMV, V_MNW * 16 + 2 * h, [[NVX * 16, 128], [1, 2], [0, 128]]), ALU.mult, [("ps", bks[t]), "FMV"], ["YT"])
                            continue
                        for t in range(nt):
                            t0_ = t * 128
                            b = PS.get()
                            TR(PS.bf16(b)[:, 0:128], kh[:, t0_:t0_ + 128], idb[:], [kk_, "idb"], [("ps", b)])
                            TS("dve", kw[:], PS.bf16(b)[:, 0:128], QT[:, t, 1, h:h + 1], None, ALU.mult, None, [("ps", b), "QT"], ["kw"])
                            b = PS.get()
                            MM(PS.f32(b)[:, 0:128], kh[:, t0_:t0_ + 128], qh[:, t0_:t0_ + 128], True, True, [qk, kk_], [("ps", b)])
                            TT("dve", Sb[:], PS.f32(b)[:, 0:128], WT[:, t, h, :], ALU.mult, [("ps", b), ("WT", t)], ["Sb"])
                            bI = PS.get()
                            MM(PS.f32(bI)[:, 0:MDV + 1], Sb[:], V1[:, t, :], True, True, ["Sb", "V1"], [("ps", bI)])
                            CP("act", intra[:], PS.f32(bI)[:, 0:MDV + 1], [("ps", bI)], ["intra"])
                            if kind == "p":
                                for c in range(2):
                                    r0 = 64 * c
                                    bN = PS.get()
                                    MM(PS.f32(bN)[r0:r0 + 64, 0:MDV + 1], qh[:, t0_ + r0:t0_ + r0 + 64], Cb[:, h, :], True, True,
                                       [qk, ("Cb", h)], [("ps", bN)])
                                    STT(hn[r0:r0 + 64, :], PS.f32(bN)[r0:r0 + 64, 0:MDV + 1], QT[r0:r0 + 64, t, 0, h:h + 1], intra[r0:r0 + 64, :],
                                        ALU.mult, ALU.add, [("ps", bN), "QT", "intra"], [("hn", c)])
                                    bU = PS.get()
                                    MM(PS.f32(bU)[:, 0:MDV + 1], kw[r0:r0 + 64, :], V1[r0:r0 + 64, t, :], True, True, ["kw", "V1"], [("ps", bU)])
                                    STT(Cst[:, h, :], Cst[:, h, :], DEC[:, t, c, h:h + 1], PS.f32(bU)[:, 0:MDV + 1], ALU.mult, ALU.add,
                                        [("ps", bU), "DEC", ("Cst", h)], [("Cst", h)])
                                    CP("act", Cb[:, h, :], Cst[:, h, :], [("Cst", h)], [("Cb", h)])
                                hnk = [("hn", 0), ("hn", 1)]
                            else:
                                bN = PS.get()
                                for s in range(NSEQ):
                                    MM(PS.f32(bN)[:, 0:MDV + 1], QZ[:, s, :], C0b[:, s, :], s == 0, s == NSEQ - 1, ["QZ", "C0b"], [("ps", bN)])
                                STT(hn[:], PS.f32(bN)[:, 0:MDV + 1], QT[:, t, 0, h:h + 1], intra[:], ALU.mult, ALU.add,
                                    [("ps", bN), "QT", "intra"], [("hn", 0), ("hn", 1)])
                                hnk = [("hn", 0), ("hn", 1)]
                                TT("dve", KZ[:], AP(kw, 0, [[128, 128], [0, 16], [1, 128]]), AP(rowmask, 0, [[16, 128], [1, 16], [0, 128]]),
                                   ALU.mult, ["kw", "rowmask"], ["KZ"])
                                for s in range(NSEQ):
                                    bU = PS.get()
                                    MM(PS.f32(bU)[:, 0:MDV + 1], KZ[:, s, :], V1[:, t, :], True, True, ["KZ", "V1"], [("ps", bU)])
                                    STT(C0f[:, s, :], C0f[:, s, :], DEC[:, t, s, h:h + 1], PS.f32(bU)[:, 0:MDV + 1], ALU.mult, ALU.add,
                                        [("ps", bU), "DEC", "C0f", "C0b"], ["C0f"])
                                P.dma("pool", dr["Cs"].ap()[:, h].rearrange("s k v -> k s v"), C0f[:, :, 0:MDV], r=["C0f"], is_output=True)
                                CP("pool", AP(NS, h, [[128, 128], [8, 16], [1, 1]]), C0f[:, :, MDV:MDV + 1], ["C0f"], ["NS"])
                            TS("dve", sc[:, 7:8], hn[:, MDV:MDV + 1], -1.0, None, ALU.mult, None, hnk, ["sc"])
                            TT("dve", sc[:, 7:8], sc[:, 7:8], hn[:, MDV:MDV + 1], ALU.max, hnk + ["sc"], ["sc"])
                            TT("dve", sc[:, 0:1], sc[:, 7:8], QT[:, t, 3, h:h + 1], ALU.max, ["sc", "QT"], ["sc"])
                            P.op("dve", lambda e: e.reciprocal(out=sc[:, 1:2], in_=sc[:, 0:1]), ["sc"], ["sc"])
                            ACT(intra[:, 0:MDV], hn[:, 0:MDV], AF.Square, hnk, ["intra", "sc"], accum=sc[:, 2:3])
                            TT("dve", sc[:, 3:4], sc[:, 1:2], sc[:, 1:2], ALU.mult, ["sc"], ["sc"])
                            TT("dve", sc[:, 3:4], sc[:, 3:4], sc[:, 2:3], ALU.mult, ["sc"], ["sc"])
                            ACT(sc[:, 4:5], sc[:, 3:4], AF.Sqrt, ["sc", "epsb"], ["sc"], scale=1.0 / MDV, bias=epsb[:, 0:1])
                            P.op("dve", lambda e: e.reciprocal(out=sc[:, 5:6], in_=sc[:, 4:5]), ["sc"], ["sc"])
                            TT("dve", sc[:, 6:7], sc[:, 5:6], sc[:, 1:2], ALU.mult, ["sc"], ["sc"])
                            STT(HO[:], hn[:, 0:MDV], sc[:, 6:7], ogs[:, t, :], ALU.mult, ALU.mult, hnk + ["sc", "ogs"], ["HO"])
                            b = PS.get()
                            for j in range(2):
                                TR(PS.bf16(b)[:, j * 128:(j + 1) * 128], HO[:, j * 128:(j + 1) * 128], idb[:], ["HO", "idb"], [("ps", b)])
                            TT("dve", YT[:, 2 * h:2 * h + 2, t0_:t0_ + 128], PS.bf16(b)[:, 0:256].rearrange("p (a b) -> p a b", a=2),
                               AP(FMV, V_MNW * 16 + 2 * h, [[NVX * 16, 128], [1, 2], [0, 128]]), ALU.mult, [("ps", b), "FMV"], ["YT"])
                def outcons(t, c0, n, b):
                    TT("dve", H[t][:, c0:c0 + n], PS.f32(b)[:, 0:n], H[t][:, c0:c0 + n], ALU.add, [("ps", b), ("H", t)], [("H", t)])
                mm_tm("m_wout", None, 0, D, nt, outcons, lhs_fn=lambda kc, t: YT[:, kc, t * 128:(t + 1) * 128], keys_r=["YT"])
                if kind == "s":
                    b = PS.get()
                    TR(PS.f32(b)[:, 0:128], NS[:], idf[:], ["NS", "idf"], [("ps", b)])
                    CP("dve", n0l[:], PS.f32(b)[:, 0:128], [("ps", b)], ["n0l"])
                    P.dma("pool", dr["ns"].ap(), n0l[:], r=["n0l"], is_output=True)
                elif bi == SEQ // TB - 1:
                    P.dma("pool", dr["Cp"].ap().rearrange("h k v -> k h v"), Cst[:, :, 0:MDV], r=[("Cst", h) for h in range(MH)], is_output=True)
                    b = PS.get()
                    CP("dve", hn[:, 0:MH], Cst[:, :, MDV], [("Cst", h) for h in range(MH)], [("hn", 0), ("hn", 1)])
                    TR(PS.f32(b)[0:MH, 0:128], hn[:, 0:MH], idf[:], [("hn", 0), ("hn", 1), "idf"], [("ps", b)])
                    CP("dve", intra[0:MH, 0:128], PS.f32(b)[0:MH, 0:128], [("ps", b)], ["intra"])
                    P.dma("pool", dr["np_"].ap(), intra[0:MH, 0:128], r=["intra"], is_output=True)
                P.barrier()

        def rwkv(kind, bi, nt):
            ntok = nt * 128
            W = ntok
            W2 = min(ntok, 256)
            nhalf = ntok // W2
            L = 64 if kind == "p" else 8
            sfx = kind
            nsq = 5 if kind == "p" else 2
            norm_to_xt(V_NMIX1, nt)
            last_block = (kind == "s") or (bi == SEQ // TB - 1)
            if last_block:
                with ExitStack() as st0:
                    gt = sb("rk_gt", [128, D], F32, st0)
                    yy = sb("rk_yy", [128, D], F32, st0)
                    P.dma("sp", gt[:], AP(dr["nmix1"], 0, [[0, 128], [1, D]]), w=["rk_gt"])
                    rms_stats(nt - 1)
                    STT(yy[:], H[nt - 1][:], nst[:, 2:3], gt[:], ALU.mult, ALU.mult, [("H", nt - 1), "nst", "rk_gt"], ["rk_yy"])
                    if kind == "p":
                        P.dma("pool", dr["shp"].ap(), yy[127:128, :], r=["rk_yy"], is_output=True)
                    else:
                        P.dma("pool", dr["shs"].ap(), AP(yy, 7 * D, [[8 * D, 16], [1, D]]), r=["rk_yy"], is_output=True)
                    P.barrier()
            with ExitStack() as st:
                YT = sb("rk_YT", [128, KC, W], BF16, st)
                w2b = sb("rk_w2b", [96, D], BF16, st)
                a2b = sb("rk_a2b", [96, D], BF16, st)
                g2b = sb("rk_g2b", [128, 2, D], BF16, st)
                h1w = sb("rk_h1w", [96, W], BF16, st)
                h1a = sb("rk_h1a", [96, W], BF16, st)
                h1g = sb("rk_h1g", [128, 2, W], BF16, st)
                fm = {nm: sb("rk_f_" + nm, [128, W2], F32, st) for nm in
                      ("a", "gg", "lwn", "t1", "t2", "bonus", "e1", "e2", "e3")}
                fm["cum"] = fm["a"]
                fm["e4"] = fm["e3"]
                fmW2 = [{nm: sb("rk_fw_" + nm, [128, W], F32, st) for nm in ("r", "k", "v")} for _ in range(2)]
                rst = sb("rk_rst", [128, W2], F32, st)
                mu2 = sb("rk_mu2", [128, 256], F32, st)
                msl = sb("rk_msl", [128, 128], F32, st)
                CHD = BF16 if cfg.get("ch_bf16", True) else F32
                NTL = W2 // 128
                NU = 2 * NTL
                TM4 = [sb("rk_TM4", [128, 4, 128], CHD, st) for _ in range(NTL)]
                RtZ = [sb("rk_RtZ", [128, 2, 128], CHD, st) for _ in range(NTL)]
                VZ = [sb("rk_VZ", [128, 2, 128], CHD, st) for _ in range(NTL)]
                UZ = [sb("rk_UZ", [128, 2, 128], CHD, st) for _ in range(NTL)]
                M1T = [sb("rk_M1T", [128, 128], CHD, st) for _ in range(NTL)]
                Zs = [sb("rk_Zs", [128, 2, 64], F32, st) for _ in range(NTL)]
                NG4 = [sb("rk_NG4", [128, 2, 256], CHD, st) for _ in range(NU)]
                PQ = [[sb("rk_PQ", [128, 2, 128], CHD, st) for _ in range(2)] for _ in range(NU)]
                Tb = [[sb("rk_T", [128, 128], CHD, st) for _ in range(2)] for _ in range(NU)]
                AVs = [sb("rk_AVs", [128, 64], CHD, st) for _ in range(NU)]
                STDb = sb("rk_STDb", [128, NG, 128], CHD, st)
                fmc = {nm: sb("rk_fc_" + nm, [128, W2], CHD, st) for nm in ("Bt", "Kt", "Bh", "Kh", "vb")}
                ARc = sb("rk_ARc", [128, 2, W2], CHD, st)
                Ytm2 = sb("rk_Ytm2", [128, NTL, 128], F32, st)
                ysq2 = sb("rk_ysq2", [128, NTL, 128], F32, st)
                gst = sb("rk_gst", [128, 32], F32, st)
                yfm = sb("rk_yfm", [128, 128], F32, st)
                P.dma("sp", mu2[:], dr["c_mu2_" + sfx].ap(), w=["mu2"])
                P.dma("sp", msl[:], dr["c_msl_" + sfx].ap(), w=["msl"])
                if kind == "p":
                    P.dma("sp", rst[:], dr["c_rst64"].ap()[:, 0:W2], w=["rst"])
                else:
                    P.dma("sp", rst[:], dr["c_rst8"].ap()[:, 0:W2], w=["rst"])
                for ztl, zk in ((UZ, "UZ"), (VZ, "VZ"), (RtZ, "RtZ")):
                    for tl_, zt in enumerate(ztl):
                        MSET("pool", zt[:], 0.0, [(zk, tl_)])
                if kind == "p":
                    CP("act", STDb[:], STD[:].rearrange("p g j v -> p g (j v)"), [("STD", g_) for g_ in range(NG)], [("STDb", g_) for g_ in range(NG)])
                zkeys = {}
                if kind == "s":
                    XP = sb("rk_XP", [128, KC, 128], BF16, st)
                    seqm = sb("rk_seqm", [128, 16, 128], F32, st)
                    rowm = sb("rk_rowm", [128, 16], F32, st)
                    M1TZ = sb("rk_M1TZ", [128, 16, 128], CHD, st)
                    S0Db = sb("rk_S0Db", [128, 16, 128], CHD, st)
                    RtZs = M1TZ
                    BhZ = sb("rk_BhZ", [128, 16, 128], CHD, st)
                    KhZ = sb("rk_KhZ", [128, 16, 128], CHD, st)
                    S0l = H[1][0:64, :].rearrange("p (s k) -> p s k", s=16)
                    S0D = H[2][:, :].rearrange("p (s j v) -> p s j v", s=16, j=2)
                    Scomp = H[3][:, 0:1024].rearrange("p (s v) -> p s v", s=16)
                    P.dma("sp", seqm[:], dr["c_seqmask_s"].ap(), w=["seqm"])
                    P.dma("sp", rowm[:], dr["c_rowmask_s"].ap(), w=["rowm"])
                    MSET("pool", S0D[:], 0.0, ["S0D"])
                    si = wctr["s"] % NWS
                    wctr["s"] += 1
                    shl = wst[si][0:16, 0:D]
                    P.dma("sp", shl, dr["rsh"].ap(), w=[("wst", si)])
                    b = PS.get()
                    for kc in range(KC):
                        TR(PS.f32(b)[:, kc * 16:(kc + 1) * 16], shl[:, kc * 128:(kc + 1) * 128], idf[0:16, 0:16], [("wst", si), "idf"], [("ps", b)])
                    CP("dve", AP(XP, 0, [[KC * 128, 128], [128, KC], [8, 16]]), PS.f32(b)[:, 0:256].rearrange("p (a b) -> p a b", a=KC),
                       [("ps", b)], ["XP"])
                    for kc in range(KC):
                        CP("pool", AP(XP, kc * 128 + 1, [[KC * 128, 128], [8, 16], [1, 7]]),
                           AP(XT, kc * (TB + 1) + 1, [[KC * (TB + 1), 128], [8, 16], [1, 7]]), ["XT"], ["XP"])
                    cur_fn = lambda kc, c0, n: XT[:, kc, 1 + c0:1 + c0 + n]
                    prev_fn = lambda kc, c0, n: XP[:, kc, c0:c0 + n]
                    xkeys = ["XT", "XP"]
                else:
                    CP("pool", XT[:, :, 0:1], xlast[:].rearrange("p (a b) -> p a b", b=1), ["xlast"], ["XT"])
                    CP("pool", xlast[:].rearrange("p (a b) -> p a b", b=1), XT[:, :, ntok:ntok + 1], ["XT"], ["xlast"])
                    cur_fn = lambda kc, c0, n: XT[:, kc, 1 + c0:1 + c0 + n]
                    prev_fn = lambda kc, c0, n: XT[:, kc, c0:c0 + n]
                    xkeys = ["XT"]

                def rhs2(c0, n):
                    return lambda vi, kc: (cur_fn if vi == 0 else prev_fn)(kc, c0, n)

                def scl(c):
                    return [(V_1MU0 + c) * 16, (V_MU0 + c) * 16]

                def wload_to(dst, name, r0, nrows):
                    src, np_, kcn = wsrc(name, None, r0, nrows, 0, D)
                    wv, key = wpanel(src, np_, kcn, D if kcn == 1 else D)
                    return wv, key
                si = wctr["s"] % NWS
                wctr["s"] += 1
                P.dma("sp", wst[si][0:96, 0:D], dr["r_w2"].ap(), w=[("wst", si)])
                CP("pool", w2b[:], wst[si][0:96, 0:D], [("wst", si)], ["w2b"])
                si = wctr["s"] % NWS
                wctr["s"] += 1
                P.dma("sp", wst[si][0:96, 0:D], dr["r_a2"].ap(), w=[("wst", si)])
                CP("pool", a2b[:], wst[si][0:96, 0:D], [("wst", si)], ["a2b"])
                si = wctr["s"] % NWS
                wctr["s"] += 1
                P.dma("sp", wst[si][:, 0:2 * D].rearrange("p (k n) -> p k n", k=2), dr["r_g2"].ap().rearrange("(k p) n -> p k n", p=128), w=[("wst", si)])
                CP("pool", g2b[:], wst[si][:, 0:2 * D].rearrange("p (k n) -> p k n", k=2), [("wst", si)], ["g2b"])

                def c_h1w(m, mw, b):
                    ACT(h1w[:, 0:ntok], PS.f32(b)[0:96, 0:ntok], AF.Tanh, [("ps", b)], ["h1w"])
                mm_fm("r_w1", None, 0, 96, ntok, c_h1w, rhs_fn=rhs2(0, ntok), scales=scl(1), keys_r=xkeys)

                def c_h1a(m, mw, b):
                    CP("act", h1a[:, 0:ntok], PS.f32(b)[0:96, 0:ntok], [("ps", b)], ["h1a"])
                mm_fm("r_a1", None, 0, 96, ntok, c_h1a, rhs_fn=rhs2(0, ntok), scales=scl(4), keys_r=xkeys)

                def c_h1g(m, mw, b):
                    ACT(h1g[:, m, 0:ntok], PS.f32(b)[:, 0:ntok], AF.Sigmoid, [("ps", b)], ["h1g"])
                mm_fm("r_g1", None, 0, 256, ntok, c_h1g, rhs_fn=rhs2(0, ntok), scales=scl(5), keys_r=xkeys)

                for g in range(NG):
                    gc = slice(g * 128, (g + 1) * 128)
                    def proj_slice(gg, nm):
                        wn, c = {"r": ("r_wr", 0), "k": ("r_wk", 2), "v": ("r_wv", 3)}[nm]
                        par_ = gg % 2
                        src, np_, kcn = wsrc(wn, None, 0, D, gg * 128, 128)
                        wv, wkey = wpanel(src, np_, kcn, 128, scl(c))
                        b_ = PS.get(hold=True)
                        i = 0
                        for vi in range(2):
                            for kc in range(KC):
                                MM(PS.f32(b_)[:, 0:W], wv[vi][:, kc, :], (cur_fn if vi == 0 else prev_fn)(kc, 0, W),
                                   i == 0, i == 2 * KC - 1, [wkey] + xkeys, [("ps", b_)])
                                i += 1

                        def evac():
                            PS.release(b_)
                            CP("act" if nm != "k" else "dve", fmW2[par_][nm][:], PS.f32(b_)[:, 0:W], [("ps", b_)], ["%s%d" % (nm, par_)])
                        return evac
                    if g == 0:
                        for nm in ("r", "k", "v"):
                            proj_slice(0, nm)()
                    par = g % 2
                    kR, kK, kV = "r%d" % par, "k%d" % par, "v%d" % par
                    fmW = fmW2[par]
                    look = (g + 1 < NG)
                    for hb in range(nhalf):
                        c0 = hb * W2
                        fm["r"] = fmW["r"][:, c0:c0 + W2]
                        fm["k"] = fmW["k"][:, c0:c0 + W2]
                        fm["v"] = fmW["v"][:, c0:c0 + W2]
                        b = PS.get()
                        MM(PS.f32(b)[:, 0:W2], w2b[:, gc], h1w[:, c0:c0 + W2], True, True, ["w2b", "h1w"], [("ps", b)])
                        ACT(fm["lwn"][:], PS.f32(b)[:, 0:W2], AF.Sigmoid, [("ps", b), "FMV"], ["lwn"], bias=fmv(V_W0, g))
                        b = PS.get()
                        MM(PS.f32(b)[:, 0:W2], a2b[:, gc], h1a[:, c0:c0 + W2], True, True, ["a2b", "h1a"], [("ps", b)])
                        ACT(fm["a"][:], PS.f32(b)[:, 0:W2], AF.Sigmoid, [("ps", b), "FMV"], ["a"], bias=fmv(V_A0, g))
                        b = PS.get()
                        for j in range(2):
                            MM(PS.f32(b)[:, 0:W2], g2b[:, j, gc], h1g[:, j, c0:c0 + W2], j == 0, j == 1, ["g2b", "h1g"], [("ps", b)])
                        CP("act", fm["gg"][:], PS.f32(b)[:, 0:W2], [("ps", b)], ["gg"])
                        TS("dve", fm["t1"][:], fm["k"], fmv(V_KK, g), None, ALU.mult, None, [kK, "FMV"], ["t1"])
                        ACT(fm["t2"][:], fm["k"], AF.Square, [kK, "FMV"], ["t2"], scale=fmv(V_KK, g))
                        b = PS.get()
                        MM(PS.f32(b)[:, 0:W2], bonesf[:], fm["t2"][:], True, True, ["bonesf", "t2"], [("ps", b)])
                        evacs = []
                        if look and hb == nhalf - 1:
                            evacs.append(proj_slice(g + 1, "r"))
                        ACT(fm["t2"][:], PS.f32(b)[:, 0:W2], AF.Ln, [("ps", b), "epsb"], ["t2"], bias=epsb[:, 3:4])
                        ACT(fm["t2"][:], fm["t2"][:], AF.Exp, ["t2"], ["t2"], scale=-0.5)
                        TT("pool", fm["t1"][:], fm["t1"][:], fm["t2"][:], ALU.mult, ["t1", "t2"], ["t1"])
                        TS("dve", fm["t2"][:], fm["a"][:], fmv(V_KA, g), fmv(V_1KA, g), ALU.mult, ALU.add, ["a", "FMV"], ["t2"])
                        TT("pool", fm["k"], fm["k"], fm["t2"][:], ALU.mult, [kK, "t2"], [kK])
                        TT("pool", fm["t2"][:], fm["t1"][:], fm["a"][:], ALU.mult, ["t1", "a"], ["t2"])
                        STT(fm["e3"][:], fm["r"], fmv(V_RK, g), fm["k"], ALU.mult, ALU.mult, [kR, kK, "FMV"], ["e3"])
                        b = PS.get()
                        MM(PS.f32(b)[:, 0:W2], bonesf[:], fm["e3"][:], True, True, ["bonesf", "e3"], [("ps", b)])
                        if look and hb == nhalf - 1:
                            evacs.append(proj_slice(g + 1, "k"))
                        TT("dve", fm["bonus"][:], PS.f32(b)[:, 0:W2], fm["v"], ALU.mult, [("ps", b), kV], ["bonus"])
                        P.op("dve", lambda e: e.tensor_tensor_scan(out=fm["cum"][:], data0=rst[:], data1=fm["lwn"][:], initial=0.0,
                                                                   op0=ALU.mult, op1=ALU.add), ["rst", "lwn"], ["a"])
                        ACT(fm["e1"][:], fm["cum"][:], AF.Exp, ["a"], ["e1"], scale=-EXPM05)
                        ACT(fm["e2"][:], fm["cum"][:], AF.Exp, ["a"], ["e2"], scale=EXPM05)
                        TT("pool", fm["e3"][:], fm["cum"][:], fm["lwn"][:], ALU.subtract, ["a", "lwn"], ["e3"])
                        ACT(fm["e3"][:], fm["e3"][:], AF.Exp, ["e3"], ["e3"], scale=-EXPM05)
                        STT(ARc[:, 0, :], fm["t1"][:], -1.0, fm["e3"][:], ALU.mult, ALU.mult, ["t1", "e3", "AR"], ["AR"])
                        nch2 = W2 // L
                        TT("pool", fm["lwn"][:].rearrange("p (c l) -> p c l", l=L), AP(fm["cum"], L - 1, [[W2, 128], [L, nch2], [0, L]]),
                           fm["cum"][:].rearrange("p (c l) -> p c l", l=L), ALU.subtract, ["a", "lwn"], ["lwn"])
                        ACT(fm["e4"][:], fm["lwn"][:], AF.Exp, ["lwn"], ["e3"], scale=-EXPM05)
                        TT("pool", ARc[:, 1, :], fm["r"], fm["e1"][:], ALU.mult, [kR, "e1"], ["AR"])
                        TT("pool", fmc["Bt"][:], fm["t2"][:], fm["e2"][:], ALU.mult, ["t2", "e2"], ["Bt"])
                        TT("dve", fmc["Kt"][:], fm["k"], fm["e2"][:], ALU.mult, [kK, "e2"], ["Kt"])
                        TT("pool", fmc["Bh"][:], fm["t2"][:], fm["e4"][:], ALU.mult, ["t2", "e3"], ["Bh"])
                        TT("dve", fmc["Kh"][:], fm["k"], fm["e4"][:], ALU.mult, [kK, "e3"], ["Kh"])
                        CP("act", fmc["vb"][:], fm["v"], [kV], ["vb"])
                        if look and hb == nhalf - 1:
                            evacs.append(proj_slice(g + 1, "v"))
                        for ev_ in evacs:
                            ev_()
                        if kind == "s":
                            for j in range(2):
                                P.dma("sp", S0l[:, :, j * 64:(j + 1) * 64],
                                      dr["rS"].ap()[:, 2 * g + j].rearrange("s v k -> v s k"), w=["S0l"])
                            for s8 in range(2):
                                b = PS.get()
                                for ss_ in range(8):
                                    s = s8 * 8 + ss_
                                    TR(PS.f32(b)[:, ss_ * 64:(ss_ + 1) * 64], S0l[:, s, :], idf[0:64, 0:64], ["S0l", "idf"], [("ps", b)])
                                for j in range(2):
                                    pb = 64 * j
                                    CP("dve", S0D[pb:pb + 64, s8 * 8:s8 * 8 + 8, j, :],
                                       PS.f32(b)[pb:pb + 64, 0:512].rearrange("p (a b) -> p a b", a=8), [("ps", b)], ["S0D"])
                        ntl = W2 // 128
                        units = [(tl, j) for tl in range(ntl) for j in range(2)]
                        bYs = []
                        for tl in range(ntl):
                            tc0 = tl * 128
                            tsl = slice(tc0, tc0 + 128)
                            b = PS.get()
                            pv = PS.bf16(b) if CHD == BF16 else PS.f32(b)
                            idc = idb if CHD == BF16 else idf
                            for qi, src_ in enumerate((fmc["Bh"][:, tsl], fmc["Kh"][:, tsl], ARc[:, 0, tsl], fmc["vb"][:, tsl])):
                                TR(pv[:, qi * 128:(qi + 1) * 128], src_, idc[:], ["Bh", "Kh", "AR", "vb", "idb", "idf"], [("ps", b)])
                            CP("act", TM4[tl][:].rearrange("p a b -> p (a b)"), pv[:, 0:512], [("ps", b)], [("TM4", tl)])
                            if kind == "p":
                                CP("pool", AP(RtZ[tl], 0, [[256, 128], [128 + 64, 2], [1, 64]]),
                                   AP(ARc, W2 + tc0, [[2 * W2, 128], [64, 2], [1, 64]]), ["AR"], [("RtZ", tl)])
                                for c in range(2):
                                    CP("dve", VZ[tl][64 * c:64 * c + 64, c, :], TM4[tl][64 * c:64 * c + 64, 3, :], [("TM4", tl)], [("VZ", tl)])
                            bYs.append(PS.get(hold=True))
                        Pc, Qc, Tc = {}, {}, {}
                        for ui, (tl, j) in enumerate(units):
                            tc0 = tl * 128
                            tsl = slice(tc0, tc0 + 128)
                            pb = 64 * j
                            arv = ARc[pb:pb + 64, :, tsl]
                            b12 = PS.get()
                            MM(PS.f32(b12)[:, 0:256], fmc["Bt"][pb:pb + 64, tsl], arv, True, True, ["Bt", "AR"], [("ps", b12)], skip=True)
                            MM(PS.f32(b12)[:, 256:512], fmc["Kt"][pb:pb + 64, tsl], arv, False, True, ["Kt", "AR"], [("ps", b12)], skip=True)
                            b3 = PS.get()
                            MM(PS.f32(b3)[:, 0:128], ARc[pb:pb + 64, 0, tsl], fmc["Bt"][pb:pb + 64, tsl], True, True, ["AR", "Bt"], [("ps", b3)])
                            TT("dve", NG4[ui][:], PS.f32(b12)[:, 0:512].rearrange("p (a b) -> p a b", a=2),
                               AP(mu2, 0, [[256, 128], [0, 2], [1, 256]]), ALU.mult, [("ps", b12), "mu2"], [("NG4", ui)])
                            TT("dve", PQ[ui][0][:, 0, :], PS.f32(b3)[:, 0:128], msl[:], ALU.mult, [("ps", b3), "msl"], [("PQ", ui, 0)])
                            TT("pool", Tb[ui][0][:], NG4[ui][:, 0, 0:128], idc[:], ALU.add, [("NG4", ui), "idb", "idf"], [("T", ui, 0)])
                            Pc[ui] = (PQ[ui][0][:, 0, :], ("PQ", ui, 0))
                            Qc[ui] = (NG4[ui][:, 0, 0:128], ("NG4", ui))
                            Tc[ui] = 0
                        for k_ in range(nsq + 1):
                            nxt = (k_ + 1) % 2
                            do_sq = k_ < nsq
                            last_sq = (k_ == nsq - 1)
                            do_t = k_ >= 1
                            bpq = {}
                            for ui in range(len(units)):
                                bp = PS.get()
                                bpq[ui] = bp
                                if do_sq:
                                    MM(PS.f32(bp)[:, 0:128], Qc[ui][0], Pc[ui][0], True, True, [Qc[ui][1], Pc[ui][1]], [("ps", bp)], skip=True)
                                    if not last_sq:
                                        MM(PS.f32(bp)[:, 128:256], Pc[ui][0], Qc[ui][0], True, True, [Qc[ui][1], Pc[ui][1]], [("ps", bp)], skip=True)
                                if do_t:
                                    MM(PS.f32(bp)[:, 256:384], Pc[ui][0], Tb[ui][Tc[ui]][:], True, True, [Pc[ui][1], ("T", ui, Tc[ui])], [("ps", bp)], skip=True)
                            for ui in range(len(units)):
                                bp = bpq[ui]
                                eng = "act" if ui % 2 == 0 else "dve"
                                if do_t:
                                    tn = 1 - Tc[ui]
                                    TT("dve", Tb[ui][tn][:], PS.f32(bp)[:, 256:384], Tb[ui][Tc[ui]][:], ALU.add, [("ps", bp), ("T", ui, Tc[ui])], [("T", ui, tn)])
                                    Tc[ui] = tn
                                if do_sq:
                                    if not last_sq:
                                        CP(eng, PQ[ui][nxt][:].rearrange("p a b -> p (a b)"), PS.f32(bp)[:, 0:256], [("ps", bp)], [("PQ", ui, nxt)])
                                    else:
                                        CP(eng, PQ[ui][nxt][:, 0, :], PS.f32(bp)[:, 0:128], [("ps", bp)], [("PQ", ui, nxt)])
                                    Pc[ui] = (PQ[ui][nxt][:, 0, :], ("PQ", ui, nxt))
                                    Qc[ui] = (PQ[ui][nxt][:, 1, :], ("PQ", ui, nxt))
                        bms, bas = {}, {}
                        for ui, (tl, j) in enumerate(units):
                            pb = 64 * j
                            TTj, tk = Tb[ui][Tc[ui]], ("T", ui, Tc[ui])
                            bm = PS.get()
                            bms[ui] = bm
                            MM(PS.f32(bm)[:, 0:128], TM4[tl][:, 2, :], TTj[:], True, True, [("TM4", tl), tk], [("ps", bm)], skip=True)
                            MM(PS.f32(bm)[:, 128:192], NG4[ui][:, 1, 0:128], TM4[tl][:, 3, pb:pb + 64], True, True, [("NG4", ui), ("TM4", tl)], [("ps", bm)], skip=True)
                        for ui, (tl, j) in enumerate(units):
                            pb = 64 * j
                            bm = bms[ui]
                            CP("act", M1T[tl][pb:pb + 64, :], PS.f32(bm)[pb:pb + 64, 0:128], [("ps", bm)], [("M1T", tl)])
                            CP("dve", AVs[ui][:], PS.f32(bm)[:, 128:192], [("ps", bm)], [("AVs", ui)])
                        for ui, (tl, j) in enumerate(units):
                            pb = 64 * j
                            TTj, tk = Tb[ui][Tc[ui]], ("T", ui, Tc[ui])
                            bz = PS.get()
                            bas[ui] = bz
                            MM(PS.f32(bz)[:, 0:64], TTj[:], AVs[ui][:], True, True, [tk, ("AVs", ui)], [("ps", bz)])
                            MM(PS.f32(bYs[tl])[:, pb:pb + 64], NG4[ui][:, 1, 128:256], TM4[tl][:, 3, pb:pb + 64], j == 0, False,
                               [("NG4", ui), ("TM4", tl)], [("ps", bYs[tl])], skip=True)
                        for ui, (tl, j) in enumerate(units):
                            CP("act" if ui % 2 else "dve", Zs[tl][:, j, :], PS.f32(bas[ui])[:, 0:64], [("ps", bas[ui])], [("Zs", tl)])
                        for tl in range(ntl):
                            tc0 = tl * 128
                            tsl = slice(tc0, tc0 + 128)
                            bY = bYs[tl]
                            u0 = 2 * tl
                            if kind == "p":
                                for c in range(2):
                                    r0 = 64 * c
                                    bu = PS.get()
                                    MM(PS.f32(bu)[:, 0:128], M1T[tl][:], STDb[:, g, :], True, True, [("M1T", tl), ("STDb", g)], [("ps", bu)])
                                    TT("dve", UZ[tl][r0:r0 + 64, c, :], PS.f32(bu)[r0:r0 + 64, 0:128], Zs[tl][r0:r0 + 64, :, :].rearrange("p a b -> p (a b)"),
                                       ALU.add, [("ps", bu), ("Zs", tl)], [("UZ", tl)])
                                    MM(PS.f32(bY)[:, 0:128], RtZ[tl][:, c, :], STDb[:, g, :], False, False,
                                       [("RtZ", tl), ("STDb", g)], [("ps", bY)], skip=True)
                                    bs = PS.get()
                                    MM(PS.f32(bs)[:, 0:128], TM4[tl][:, 0, :], UZ[tl][:, c, :], True, False, [("TM4", tl), ("UZ", tl)], [("ps", bs)])
                                    MM(PS.f32(bs)[:, 0:128], TM4[tl][:, 1, :], VZ[tl][:, c, :], False, True, [("TM4", tl), ("VZ", tl)], [("ps", bs)])
                                    for j in range(2):
                                        pb = 64 * j
                                        wl_ap = fm["e1"][pb:pb + 64, tc0 + r0 + 63:tc0 + r0 + 64]
                                        STT(STDb[pb:pb + 64, g, pb:pb + 64], STD[pb:pb + 64, g, j, :], wl_ap, PS.f32(bs)[pb:pb + 64, pb:pb + 64],
                                            ALU.mult, ALU.add, [("ps", bs), "e1", ("STD", g)], [("STDb", g)])
                                    for j in range(2):
                                        pb = 64 * j
                                        wl_ap = fm["e1"][pb:pb + 64, tc0 + r0 + 63:tc0 + r0 + 64]
                                        STT(STD[pb:pb + 64, g, j, :], STD[pb:pb + 64, g, j, :], wl_ap, PS.f32(bs)[pb:pb + 64, pb:pb + 64],
                                            ALU.mult, ALU.add, [("ps", bs), "e1", ("STD", g)], [("STD", g)])
                                for j in range(2):
                                    pb = 64 * j
                                    for c in range(2):
                                        MM(PS.f32(bY)[:, pb:pb + 64], NG4[u0 + j][:, 0, 128:256], UZ[tl][:, c, pb:pb + 64], False, (j == 1 and c == 1),
                                           [("NG4", u0 + j), ("UZ", tl)], [("ps", bY)], skip=True)
                            else:
                                TT("dve", M1TZ[:], AP(M1T[tl], 0, [[128, 128], [0, 16], [1, 128]]), seqm[:], ALU.mult, [("M1T", tl), "seqm"], ["M1TZ"])
                                TT("dve", BhZ[:], AP(TM4[tl], 0, [[512, 128], [0, 16], [1, 128]]), AP(rowm, 0, [[16, 128], [1, 16], [0, 128]]),
                                   ALU.mult, [("TM4", tl), "rowm"], ["BhZ"])
                                TT("pool", KhZ[:], AP(TM4[tl], 128, [[512, 128], [0, 16], [1, 128]]), AP(rowm, 0, [[16, 128], [1, 16], [0, 128]]),
                                   ALU.mult, [("TM4", tl), "rowm"], ["KhZ"])
                                CP("act", S0Db[:], S0D[:].rearrange("p s j v -> p s (j v)"), ["S0D"], ["S0Db"])
                                bu = PS.get()
                                for s in range(NSEQ):
                                    MM(PS.f32(bu)[:, 0:128], M1TZ[:, s, :], S0Db[:, s, :], s == 0, s == NSEQ - 1, ["M1TZ", "S0Db"], [("ps", bu)])
                                TT("dve", UZ[tl][:, 0, :], PS.f32(bu)[:, 0:128], Zs[tl][:].rearrange("p a b -> p (a b)"), ALU.add, [("ps", bu), ("Zs", tl)], [("UZ", tl)])
                                TT("pool", RtZs[:], AP(ARc, W2 + tc0, [[2 * W2, 128], [0, 16], [1, 128]]), seqm[:], ALU.mult, ["AR", "seqm"], ["M1TZ"])
                                for s in range(NSEQ):
                                    MM(PS.f32(bY)[:, 0:128], RtZs[:, s, :], S0Db[:, s, :], False, False, ["M1TZ", "S0Db"], [("ps", bY)], skip=True)
                                for j in range(2):
                                    pb = 64 * j
                                    MM(PS.f32(bY)[:, pb:pb + 64], NG4[u0 + j][:, 0, 128:256], UZ[tl][:, 0, pb:pb + 64], False, j == 1,
                                       [("NG4", u0 + j), ("UZ", tl)], [("ps", bY)], skip=True)
                                for s in range(NSEQ):
                                    bs = PS.get()
                                    MM(PS.f32(bs)[:, 0:128], BhZ[:, s, :], UZ[tl][:, 0, :], True, False, ["BhZ", ("UZ", tl)], [("ps", bs)])
                                    MM(PS.f32(bs)[:, 0:128], KhZ[:, s, :], TM4[tl][:, 3, :], False, True, ["KhZ", ("TM4", tl)], [("ps", bs)])
                                    for j in range(2):
                                        pb = 64 * j
                                        wl_ap = fm["e1"][pb:pb + 64, s * 8 + 7:s * 8 + 8]
                                        STT(S0D[pb:pb + 64, s, j, :], S0D[pb:pb + 64, s, j, :], wl_ap, PS.f32(bs)[pb:pb + 64, pb:pb + 64],
                                            ALU.mult, ALU.add, [("ps", bs), "e1", "S0D", "S0Db"], ["S0D"])
                                TT("pool", Scomp[:], S0D[:, :, 0, :], S0D[:, :, 1, :], ALU.add, ["S0D"], ["Scomp"])
                                for s4 in range(4):
                                    b = PS.get()
                                    for ss_ in range(4):
                                        s = s4 * 4 + ss_
                                        TR(PS.f32(b)[0:64, ss_ * 128:(ss_ + 1) * 128], Scomp[:, s, :], idf[:], ["Scomp", "idf"], [("ps", b)])
                                    CP("act", S0l[:, s4 * 4:s4 * 4 + 4, :].rearrange("p a b -> p (a b)"), PS.f32(b)[0:64, 0:512], [("ps", b)], ["S0l"])
                                for j in range(2):
                                    P.dma("pool", dr["Ss"].ap()[:, 2 * g + j].rearrange("s v k -> v s k"),
                                          S0l[:, :, j * 64:(j + 1) * 64], r=["S0l"], is_output=True)
                            PS.release(bY)
                            CP("act", Ytm2[:, tl, :], PS.f32(bY)[:, 0:128], [("ps", bY)], ["Ytm"])
                        nq = 2 * ntl
                        GW = 32
                        yv = Ytm2[:, 0:ntl, :].rearrange("p t (a b) -> p (t a) b", a=2)
                        P.op("dve", lambda e: e.tensor_reduce(out=gst[:, 0:nq], in_=yv, axis=AX.X, op=ALU.add), ["Ytm"], ["gst"])
                        ACT(ysq2[:, 0:ntl, :], Ytm2[:, 0:ntl, :], AF.Square, ["Ytm"], ["ysq2"])
                        P.op("dve", lambda e: e.tensor_reduce(out=gst[:, nq:2 * nq], in_=ysq2[:, 0:ntl, :].rearrange("p t (a b) -> p (t a) b", a=2),
                                                              axis=AX.X, op=ALU.add), ["ysq2"], ["gst"])
                        TS("dve", gst[:, 2 * nq:3 * nq], gst[:, 0:nq], 1.0 / RN, None, ALU.mult, None, ["gst"], ["gst"])
                        TT("dve", gst[:, 3 * nq:4 * nq], gst[:, 2 * nq:3 * nq], gst[:, 2 * nq:3 * nq], ALU.mult, ["gst"], ["gst"])
                        STT(gst[:, 4 * nq:5 * nq], gst[:, nq:2 * nq], 1.0 / RN, gst[:, 3 * nq:4 * nq], ALU.mult, ALU.subtract, ["gst"], ["gst"])
                        TS("dve", gst[:, 4 * nq:5 * nq], gst[:, 4 * nq:5 * nq], 0.0, None, ALU.max, None, ["gst"], ["gst"])
                        ACT(gst[:, 4 * nq:5 * nq], gst[:, 4 * nq:5 * nq], AF.Sqrt, ["gst", "epsb"], ["gst"], bias=epsb[:, 2:3])
                        P.op("dve", lambda e: e.reciprocal(out=gst[:, 5 * nq:6 * nq], in_=gst[:, 4 * nq:5 * nq]), ["gst"], ["gst"])
                        TT("dve", yv, yv, AP(gst, 2 * nq, [[GW, 128], [1, nq], [0, 64]]), ALU.subtract, ["Ytm", "gst"], ["Ytm"])
                        TT("dve", yv, yv, AP(gst, 5 * nq, [[GW, 128], [1, nq], [0, 64]]), ALU.mult, ["Ytm", "gst"], ["Ytm"])
                        for tl in range(ntl):
                            tc0 = tl * 128
                            tsl = slice(tc0, tc0 + 128)
                            b = PS.get()
                            TR(PS.f32(b)[:, 0:128], Ytm2[:, tl, :], idf[:], ["Ytm", "idf"], [("ps", b)])
                            TS("dve", yfm[:], PS.f32(b)[:, 0:128], fmv(V_LNW, g), fmv(V_LNB, g), ALU.mult, ALU.add, [("ps", b), "FMV"], ["yfm"])
                            TT("pool", yfm[:], yfm[:], fm["bonus"][:, tsl], ALU.add, ["yfm", "bonus"], ["yfm"])
                            TT("pool", YT[:, g, c0 + tc0:c0 + tc0 + 128], yfm[:], fm["gg"][:, tsl], ALU.mult, ["yfm", "gg"], ["YT"])
                def outcons(t, c0_, n, b):
                    TT("dve", H[t][:, c0_:c0_ + n], PS.f32(b)[:, 0:n], H[t][:, c0_:c0_ + n], ALU.add, [("ps", b), ("H", t)], [("H", t)])
                mm_tm("r_wo", None, 0, D, nt, outcons, lhs_fn=lambda kc, t: YT[:, kc, t * 128:(t + 1) * 128], keys_r=["YT"])
                if kind == "p" and bi == SEQ // TB - 1:
                    Sc = fm["t1"]
                    So = fm["t2"]
                    for g in range(NG):
                        TT("pool", Sc[:, 0:64], STD[:, g, 0, :], STD[:, g, 1, :], ALU.add, [("STD", g)], ["t1"])
                        b = PS.get()
                        TR(PS.f32(b)[0:64, 0:128], Sc[:, 0:64], idf[:], ["t1", "idf"], [("ps", b)])
                        CP("act", So[0:64, 0:128], PS.f32(b)[0:64, 0:128], [("ps", b)], ["t2"])
                        P.dma("pool", dr["Sp"].ap()[2 * g:2 * g + 2].rearrange("j v k -> v j k"),
                              So[0:64, 0:128].rearrange("v (j k) -> v j k", j=2), r=["t2"], is_output=True)
                P.barrier()

        for bidx_, (kind, bi) in enumerate(blocks):
            nt = TB // 128 if kind == "p" else 1
            wctr["p"] = 0
            wctr["mode"] = "store" if bidx_ == 0 else "load"
            if cfg.get("no_wcache"):
                wctr["mode"] = "none"
            load_block(kind, bi, nt)
            for layer in range(2):
                if not cfg.get("skip_mixers"):
                    if layer == 0:
                        mlstm(kind, bi, nt)
                    else:
                        rwkv(kind, bi, nt)
                for t in range(nt):
                    tap("h_mix%d_%s%d_%d" % (layer, kind, bi, t), H[t][:], [128, D], [("H", t)])
                if stop == "mix%d" % layer:
                    break
                ffn(layer, nt)
                for t in range(nt):
                    tap("h_ffn%d_%s%d_%d" % (layer, kind, bi, t), H[t][:], [128, D], [("H", t)])
                if stop == "ffn%d" % layer:
                    break
                ple(layer, kind, bi, nt)
                for t in range(nt):
                    tap("h_ple%d_%s%d_%d" % (layer, kind, bi, t), H[t][:], [128, D], [("H", t)])
                if stop == "ple%d" % layer:
                    break
            else:
                final_out(kind, bi, nt)
            P.barrier()
        P.finish()
        cfg["_stats"] = (P.n_inst, P.n_wait, dict(P.cnt))
    return nc, tap_d


def make_in_maps(inp, ncores=8):
    consts = make_consts()
    f = lambda a: np.ascontiguousarray(a, dtype=np.float32)
    vec_names = [("norm_mix", 0), ("norm_ffn", 0), ("norm_ple", 0), ("norm_mix", 1), ("norm_ffn", 1), ("norm_ple", 1)]
    vecs = [inp[n][i] for n, i in vec_names]
    vecs.append(inp["mlstm_norm_w"][0])
    for c in range(6):
        vecs.append(inp["rwkv_mu"][0, c])
    for n in ("rwkv_w0", "rwkv_a0", "rwkv_k_k", "rwkv_k_a", "rwkv_ln_w", "rwkv_ln_b"):
        vecs.append(inp[n][0])
    vecs.append(inp["rwkv_r_k"][0].reshape(-1))
    vecs = f(np.stack(vecs, 0).reshape(NV * 16, 128))
    shared = {
        "vecs": vecs, "nfin": f(inp["norm_final"].reshape(1, D)), "nmix1": f(inp["norm_mix"][1].reshape(1, D)),
        "big": f(inp["mlstm_b_igate"][0].reshape(MH, 1)), "bfg": f(inp["mlstm_b_fgate"][0].reshape(MH, 1)),
        "ffn_wg": f(inp["ffn_w_gate"]), "ffn_wu": f(inp["ffn_w_up"]), "ffn_wd": f(inp["ffn_w_down"]),
        "ple_wp": f(inp["ple_w_proj"]), "ple_wg": f(inp["ple_w_gate"]),
        "m_wq": f(inp["mlstm_w_q"][0]), "m_wk": f(inp["mlstm_w_k"][0]), "m_wv": f(inp["mlstm_w_v"][0]),
        "m_wi": f(inp["mlstm_w_igate"][0]), "m_wf": f(inp["mlstm_w_fgate"][0]),
        "m_wo": f(inp["mlstm_w_ogate"][0]), "m_wout": f(inp["mlstm_w_out"][0]),
        "r_wr": f(inp["rwkv_w_r"][0]), "r_wk": f(inp["rwkv_w_k"][0]), "r_wv": f(inp["rwkv_w_v"][0]), "r_wo": f(inp["rwkv_w_o"][0]),
        "r_w1": f(inp["rwkv_w1"][0]), "r_w2": f(inp["rwkv_w2"][0]), "r_a1": f(inp["rwkv_a1"][0]), "r_a2": f(inp["rwkv_a2"][0]),
        "r_g1": f(inp["rwkv_g1"][0]), "r_g2": f(inp["rwkv_g2"][0]),
    }
    for k, v in consts.items():
        shared["c_" + k] = f(v)
    maps = []
    for c in range(ncores):
        b = c % 4
        s0 = NSEQ * c
        m = dict(shared)
        m["xp"] = f(inp["x_prompt"][b])
        m["xs"] = f(inp["x_sample"][s0:s0 + NSEQ].reshape(128, D))
        m["pp"] = f(inp["p_prompt"][:, b])
        m["psm"] = f(inp["p_sample"][:, s0:s0 + NSEQ].reshape(2, 128, PLE))
        m["mC"] = f(inp["state_mlstm_C"][0, s0:s0 + NSEQ])
        m["mn"] = f(inp["state_mlstm_n"][0, s0:s0 + NSEQ].reshape(128, 128))
        m["mmT"] = f(inp["state_mlstm_m"][0, s0:s0 + NSEQ].T)
        m["rS"] = f(inp["state_rwkv_S"][0, s0:s0 + NSEQ])
        m["rsh"] = f(inp["state_rwkv_shift"][0, s0:s0 + NSEQ])
        maps.append(m)
    return maps


_NC_CACHE = {}


def kernel(**inputs):
    inp = {k: np.asarray(v) for k, v in inputs.items()}
    if "nc" not in _NC_CACHE:
        _NC_CACHE["nc"] = build({})[0]
    nc = _NC_CACHE["nc"]
    maps = make_in_maps(inp, 8)
    res = run_bass_kernel_spmd(nc, maps, core_ids=list(range(8)))
    R = res.results
    f = lambda a: np.ascontiguousarray(a, dtype=np.float32)
    y_prompt = f(np.stack([R[c]["yp"] for c in range(4)], 0))
    y_sample = f(np.concatenate([R[c]["ys"].reshape(NSEQ, ST_, D) for c in range(8)], 0))
    C_p = f(np.stack([R[c]["Cp"] for c in range(4)], 0)[None])
    n_p = f(np.stack([R[c]["np_"] for c in range(4)], 0)[None])
    m_p = f(np.stack([R[c]["mp"][:, 0] for c in range(4)], 0)[None])
    S_p = f(np.stack([R[c]["Sp"] for c in range(4)], 0)[None])
    sh_p = f(np.stack([R[c]["shp"][0] for c in range(4)], 0)[None])
    C_s = f(np.concatenate([R[c]["Cs"] for c in range(8)], 0)[None])
    n_s = f(np.concatenate([R[c]["ns"].reshape(NSEQ, MH, MDK) for c in range(8)], 0)[None])
    m_s = f(np.concatenate([R[c]["msT"].T for c in range(8)], 0)[None])
    S_s = f(np.concatenate([R[c]["Ss"] for c in range(8)], 0)[None])
    sh_s = f(np.concatenate([R[c]["shs"] for c in range(8)], 0)[None])
    return (y_prompt, y_sample, C_p, n_p, m_p, S_p, sh_p, C_s, n_s, m_s, S_s, sh_s)
```
